# Optimizing a Trainium2 kernel written in Bass

```python
import math
import jax, jax.numpy as jnp
from jax import lax
import numpy as np

D_MODEL = 1024
BATCH = 8
SEQ = 4096
DEPTH = 4

N_META = 16
N_A_LAYERS = DEPTH // 2
N_B_LAYERS = DEPTH - N_A_LAYERS
GDN_HEADS = 8
GDN_DK = D_MODEL // GDN_HEADS
GDN_DV = D_MODEL // GDN_HEADS
GDN_QK_W = GDN_HEADS * GDN_DK
GDN_V_W = GDN_HEADS * GDN_DV
GDN_CONV = 4
CHUNK = 64
DIFF_HEADS = 8
DIFF_DH = D_MODEL // (2 * DIFF_HEADS)
DIFF_W = DIFF_HEADS * 2 * DIFF_DH
Q_BLOCK = 128
D_FF = 11 * D_MODEL // 4
FFN_CONV = 3
EPS = 1e-6

kernel_name = 'yoco_gdn_diffattn_hybrid'


def rms_norm(x, g):
    xf = x.astype(jnp.float32)
    y = xf * lax.rsqrt(jnp.mean(xf * xf, axis=-1, keepdims=True) + EPS)
    return (y * g.astype(jnp.float32)).astype(x.dtype)


def l2norm(t):
    return t * lax.rsqrt(jnp.sum(t * t, axis=-1, keepdims=True) + EPS)


def causal_depthwise_conv(x, w):
    width, c = w.shape
    return lax.conv_general_dilated(x, w[:, None, :].astype(x.dtype), window_strides=(1,), padding=[(width - 1, 0)], dimension_numbers=('NWC', 'WIO', 'NWC'), feature_group_count=c)


def chunked_gated_delta_rule(q, k, v, beta, g):
    bn, lp, h, dk = q.shape
    dv = v.shape[-1]
    n = lp // CHUNK

    def chunks(t):
        return t.reshape(bn, n, CHUNK, h, -1).transpose(1, 0, 3, 2, 4)

    q, k, v = chunks(q), chunks(k), chunks(v)
    beta, g = chunks(beta)[..., 0], chunks(g)[..., 0]
    g_cum = jnp.cumsum(g, axis=-1)
    causal = jnp.tril(jnp.ones((CHUNK, CHUNK), dtype=bool))
    strict = jnp.tril(jnp.ones((CHUNK, CHUNK), dtype=bool), k=-1)
    decay = jnp.exp(jnp.where(causal, g_cum[..., :, None] - g_cum[..., None, :], -jnp.inf))
    k_beta = k * beta[..., None]
    lower = jnp.where(strict, jnp.einsum('nbhcd,nbhsd->nbhcs', k_beta, k) * decay, 0.0)
    rhs = jnp.concatenate([v * beta[..., None], k_beta * jnp.exp(g_cum)[..., None]], axis=-1)
    sol = lax.linalg.triangular_solve(lower + jnp.eye(CHUNK, dtype=lower.dtype), rhs, left_side=True, lower=True, unit_diagonal=True)
    u, w = sol[..., :dv], sol[..., dv:]
    intra = jnp.einsum('nbhcd,nbhsd->nbhcs', q, k) * decay

    def step(state, xs):
        q_c, k_c, u_c, w_c, g_c, a_c = xs
        v_new = u_c - jnp.einsum('bhcd,bhde->bhce', w_c, state)
        o_c = jnp.einsum('bhcd,bhde->bhce', q_c * jnp.exp(g_c)[..., None], state) + jnp.einsum('bhcs,bhse->bhce', a_c, v_new)
        g_last = g_c[..., -1:]
        k_dec = k_c * jnp.exp(g_last - g_c)[..., None]
        state = state * jnp.exp(g_last)[..., None] + jnp.einsum('bhcd,bhce->bhde', k_dec, v_new)
        return state, o_c

    state0 = jnp.zeros((bn, h, dk, dv), dtype=q.dtype)
    _, o = lax.scan(step, state0, (q, k, u, w, g_cum, intra))
    return o.transpose(1, 0, 3, 2, 4).reshape(bn, lp, h, dv)


def gdn_mixer(h, w_in, conv_w, a_log, dt_bias, o_gain, w_o):
    bn, L, _ = h.shape
    H, dk, dv = GDN_HEADS, GDN_DK, GDN_DV
    f32 = jnp.float32
    proj = h @ w_in
    c0 = 2 * GDN_QK_W + GDN_V_W
    qkv = jax.nn.silu(causal_depthwise_conv(proj[..., :c0], conv_w))
    z = proj[..., c0:c0 + GDN_V_W]
    b = proj[..., c0 + GDN_V_W:c0 + GDN_V_W + H]
    a = proj[..., c0 + GDN_V_W + H:]
    q = l2norm(qkv[..., :GDN_QK_W].reshape(bn, L, H, dk).astype(f32)) * (dk ** -0.5)
    k = l2norm(qkv[..., GDN_QK_W:2 * GDN_QK_W].reshape(bn, L, H, dk).astype(f32))
    v = qkv[..., 2 * GDN_QK_W:].reshape(bn, L, H, dv).astype(f32)
    beta = jax.nn.sigmoid(b.astype(f32))
    g = -jnp.exp(a_log.astype(f32)) * jax.nn.softplus(a.astype(f32) + dt_bias.astype(f32))
    pad = CHUNK - N_META

    def padt(t):
        return jnp.pad(t, [(0, 0), (pad, 0)] + [(0, 0)] * (t.ndim - 2))

    o = chunked_gated_delta_rule(padt(q), padt(k), padt(v), padt(beta), padt(g))[:, pad:]
    o = rms_norm(o, o_gain) * jax.nn.silu(z.reshape(bn, L, H, dv).astype(f32))
    return o.reshape(bn, L, GDN_V_W).astype(h.dtype) @ w_o


def alibi_slopes(n_heads):
    return jnp.exp2(-(8.0 / n_heads) * jnp.arange(1, n_heads + 1, dtype=jnp.float32))


def diff_attn_mixer(h, k_shared, v_shared, w_q, lam, subln, lam_init, w_o):
    bn, L, _ = h.shape
    H, dh = DIFF_HEADS, DIFF_DH
    f32 = jnp.float32
    q = (h @ w_q).reshape(bn, L, H, 2, dh)
    nb = -(-L // Q_BLOCK)
    lq = nb * Q_BLOCK
    q = jnp.pad(q, [(0, 0), (0, lq - L), (0, 0), (0, 0), (0, 0)])
    q_blocks = q.reshape(bn, nb, Q_BLOCK, H, 2, dh).transpose(1, 0, 2, 3, 4, 5)
    q_pos = jnp.arange(lq, dtype=jnp.int32).reshape(nb, Q_BLOCK)
    k_pos = jnp.arange(L, dtype=jnp.int32)
    slopes = alibi_slopes(H)[:, None, None, None]
    scale = dh ** -0.5
    vf = v_shared.astype(f32)

    def block(args):
        qb, qp = args
        s = jnp.einsum('bqhid,bkhid->bhiqk', qb, k_shared).astype(f32) * scale
        dist = (qp[:, None] - k_pos[None, :]).astype(f32)
        s = jnp.where(dist >= 0, s - slopes * dist, -jnp.inf)
        p = jax.nn.softmax(s, axis=-1)
        p = p[:, :, 0] - lam * p[:, :, 1]
        return jnp.einsum('bhqk,bkhe->bqhe', p, vf)

    o = lax.map(block, (q_blocks, q_pos))
    o = o.transpose(1, 0, 2, 3, 4).reshape(bn, lq, H, 2 * dh)[:, :L]
    o = rms_norm(o, subln) * (1.0 - lam_init)
    return o.reshape(bn, L, DIFF_W).astype(h.dtype) @ w_o


def conv_ffn(h, w_up, conv_w, w_down):
    u = causal_depthwise_conv(h @ w_up, conv_w)
    gate, up = u[..., :D_FF], u[..., D_FF:]
    return (jax.nn.silu(gate) * up) @ w_down


def setup_inputs(seed: int = 0) -> dict:
    key = jax.random.key(seed)
    ks = jax.random.split(key, 32)
    f32 = jnp.float32
    D = D_MODEL
    nA, nB = N_A_LAYERS, N_B_LAYERS

    def nrm(k, shape, scale):
        return jax.random.normal(k, shape, f32) * scale

    def gain(k, shape):
        return 1.0 + 0.02 * jax.random.normal(k, shape, f32)

    gdn_in_w = 2 * GDN_QK_W + 2 * GDN_V_W + 2 * GDN_HEADS
    dt = jnp.exp(jax.random.uniform(ks[9], (nA, GDN_HEADS), f32, math.log(1e-3), math.log(1e-1)))
    return {
        'x': jax.random.normal(ks[0], (BATCH, SEQ, D), f32),
        'meta_tokens': nrm(ks[1], (N_META, D), 1.0),
        'a_norm': gain(ks[2], (nA, D)),
        'a_w_in': nrm(ks[3], (nA, D, gdn_in_w), D ** -0.5),
        'a_conv': nrm(ks[4], (nA, GDN_CONV, 2 * GDN_QK_W + GDN_V_W), GDN_CONV ** -0.5),
        'a_log': jnp.log(jax.random.uniform(ks[5], (nA, GDN_HEADS), f32, 1.0, 16.0)),
        'a_dt_bias': jnp.log(jnp.expm1(dt)),
        'a_onorm': gain(ks[6], (nA, GDN_DV)),
        'a_w_o': nrm(ks[7], (nA, GDN_V_W, D), GDN_V_W ** -0.5),
        'kv_norm': gain(ks[8], (D,)),
        'w_kv': nrm(ks[10], (D, 2 * DIFF_W), D ** -0.5),
        'lambda_k1': nrm(ks[11], (DIFF_DH,), 0.1),
        'lambda_k2': nrm(ks[12], (DIFF_DH,), 0.1),
        'b_norm': gain(ks[13], (nB, D)),
        'b_w_q': nrm(ks[14], (nB, D, DIFF_W), D ** -0.5),
        'b_lambda_q1': nrm(ks[15], (nB, DIFF_DH), 0.1),
        'b_lambda_q2': nrm(ks[16], (nB, DIFF_DH), 0.1),
        'b_subln': gain(ks[17], (nB, 2 * DIFF_DH)),
        'b_w_o': nrm(ks[18], (nB, DIFF_W, D), DIFF_W ** -0.5),
        'ffn_norm': gain(ks[19], (DEPTH, D)),
        'ffn_w_up': nrm(ks[20], (DEPTH, D, 2 * D_FF), D ** -0.5),
        'ffn_conv': nrm(ks[21], (DEPTH, FFN_CONV, 2 * D_FF), FFN_CONV ** -0.5),
        'ffn_w_down': nrm(ks[22], (DEPTH, D_FF, D), D_FF ** -0.5),
        'final_norm': gain(ks[23], (D,)),
    }


def reference(x, meta_tokens, a_norm, a_w_in, a_conv, a_log, a_dt_bias, a_onorm, a_w_o, kv_norm, w_kv, lambda_k1, lambda_k2, b_norm, b_w_q, b_lambda_q1, b_lambda_q2, b_subln, b_w_o, ffn_norm, ffn_w_up, ffn_conv, ffn_w_down, final_norm):
    bn = x.shape[0]
    meta = jnp.broadcast_to(meta_tokens.astype(x.dtype)[None], (bn, N_META, D_MODEL))
    h = jnp.concatenate([meta, x], axis=1)
    L = h.shape[1]
    k_shared = None
    v_shared = None
    for layer in range(DEPTH):
        if layer < N_A_LAYERS:
            i = layer
            h = h + gdn_mixer(rms_norm(h, a_norm[i]), a_w_in[i], a_conv[i], a_log[i], a_dt_bias[i], a_onorm[i], a_w_o[i])
        else:
            if layer == N_A_LAYERS:
                kv = rms_norm(h, kv_norm) @ w_kv
                k_shared = kv[..., :DIFF_W].reshape(bn, L, DIFF_HEADS, 2, DIFF_DH)
                v_shared = kv[..., DIFF_W:].reshape(bn, L, DIFF_HEADS, 2 * DIFF_DH)
            j = layer - N_A_LAYERS
            lam_init = 0.8 - 0.6 * math.exp(-0.3 * layer)
            lam = (jnp.exp(jnp.sum(b_lambda_q1[j].astype(jnp.float32) * lambda_k1.astype(jnp.float32)))
                   - jnp.exp(jnp.sum(b_lambda_q2[j].astype(jnp.float32) * lambda_k2.astype(jnp.float32))) + lam_init)
            h = h + diff_attn_mixer(rms_norm(h, b_norm[j]), k_shared, v_shared, b_w_q[j], lam, b_subln[j], lam_init, b_w_o[j])
        h = h + conv_ffn(rms_norm(h, ffn_norm[layer]), ffn_w_up[layer], ffn_conv[layer], ffn_w_down[layer])
    return rms_norm(h, final_norm)[:, N_META:]
```

```python
import math
import numpy as np
from contextlib import ExitStack
import concourse.bass as bass
import concourse.mybir as mybir
from concourse.bass_utils import run_bass_kernel_spmd

F32 = mybir.dt.float32
BF16 = mybir.dt.bfloat16
U8 = mybir.dt.uint8
AF = mybir.ActivationFunctionType
ALU = mybir.AluOpType
AX = mybir.AxisListType

EPOCH = 16000
NDMASEM = 8
DT_SIZE = {F32: 4, BF16: 2, U8: 1}


class Reg:
    __slots__ = ("name", "w", "r")

    def __init__(self, name=""):
        self.name = name
        self.w = None
        self.r = {}


class Prog:
    CE = ("pe", "act", "dve", "pool")
    ALLE = ("pe", "act", "dve", "pool", "sp")

    def __init__(self, nc, es, same_engine_sync=True):
        self.nc = nc
        self.es = es
        self.streams = {k: [] for k in self.ALLE}
        self.cnt = {k: 0 for k in self.CE}
        self.sems = {k: [] for k in self.CE}
        self.waited = {k: {} for k in self.ALLE}
        self.dma_cnt = {k: 0 for k in self.ALLE}
        self.dma_sems = {}
        self.same_engine_sync = same_engine_sync
        self.nsem = 0
        self.ninst = 0
        self.sb_total = 212800
        self.sbuf = es.enter_context(nc.sbuf_tensor("arena", [128, self.sb_total], U8))
        self.psum = es.enter_context(nc.psum_tensor("psum", [128, 4096], F32))
        self.sb_persist = 0
        self.sb_off = 0
        self.bank_regs = [Reg(f"bank{i}") for i in range(8)]

    def alloc(self, shape, dtype, persist=False):
        n = int(np.prod(shape[1:])) * DT_SIZE[dtype]
        n = (n + 31) // 32 * 32
        off = self.sb_off
        assert off + n <= self.sb_total, f"SBUF arena overflow {off}+{n}"
        self.sb_off = off + n
        if persist:
            assert self.sb_persist == off, "persistent allocs must come first"
            self.sb_persist = self.sb_off
        ap = self.sbuf[0:shape[0], off:off + n].bitcast(dtype)
        fs = int(np.prod(shape[1:]))
        ap = ap[:, 0:fs]
        if len(shape) == 3:
            ap = ap.rearrange("p (a b) -> p a b", a=shape[1])
        elif len(shape) == 4:
            ap = ap.rearrange("p (a b c) -> p a b c", a=shape[1], b=shape[2])
        return ap

    def reset_arena(self):
        self.sb_off = self.sb_persist

    def bank(self, i, n=512, parts=128):
        return self.psum[0:parts, i * 512:i * 512 + n]

    def _new_sem(self, name):
        self.nsem += 1
        return self.es.enter_context(self.nc.semaphore(name))

    def _sem_for(self, eng, n):
        e = (n - 1) // EPOCH
        while len(self.sems[eng]) <= e:
            self.sems[eng].append(self._new_sem(f"s_{eng}_{len(self.sems[eng])}"))
        return self.sems[eng][e], (n - 1) % EPOCH + 1

    def _dsem(self, q, slot):
        if q not in self.dma_sems:
            self.dma_sems[q] = [self._new_sem(f"d_{q}_{i}") for i in range(NDMASEM)]
        return self.dma_sems[q][slot]

    def _wait(self, eng, ev):
        if ev[0] == 'c':
            _, src, n = ev
            if src == eng and (eng == 'pe' or not self.same_engine_sync):
                return
            key = ('c', src)
            if self.waited[eng].get(key, 0) >= n:
                return
            self.waited[eng][key] = n
            sem, val = self._sem_for(src, n)
        else:
            _, q, j = ev
            slot = j % NDMASEM
            need = j // NDMASEM + 1
            key = ('d', q, slot)
            if self.waited[eng].get(key, 0) >= need:
                return
            self.waited[eng][key] = need
            sem = self._dsem(q, slot)
            val = 16 * need
        self.streams[eng].append(lambda e, sem=sem, val=val: e.wait_ge(sem, val))
        self.ninst += 1

    def _deps(self, eng, r, w, is_dma=False):
        for reg in r:
            for ev in (reg.w or ()):
                self._wait(eng, ev)
        for reg in w:
            if is_dma and reg.w and not reg.r and all(ev[0] == 'd' for ev in reg.w):
                continue
            for ev in (reg.w or ()):
                self._wait(eng, ev)
            for ev in reg.r.values():
                self._wait(eng, ev)

    @staticmethod
    def _evkey(ev):
        return (ev[0], ev[1]) if ev[0] == 'c' else (ev[0], ev[1], ev[2] % NDMASEM)

    def _record(self, ev, r, w):
        k = self._evkey(ev)
        for reg in r:
            reg.r[k] = ev
        for reg in w:
            if ev[0] == 'd' and reg.w and not reg.r and all(e2[0] == 'd' for e2 in reg.w):
                reg.w = [e2 for e2 in reg.w if self._evkey(e2) != k] + [ev]
            else:
                reg.w = [ev]
            reg.r = {}

    def op(self, eng, fn, r=(), w=()):
        self._deps(eng, r, w)
        n = self.cnt[eng] + 1
        self.cnt[eng] = n
        sem, _ = self._sem_for(eng, n)
        self.streams[eng].append(lambda e, fn=fn, sem=sem: fn(e).then_inc(sem, 1))
        self.ninst += 1
        self._record(('c', eng, n), r, w)

    def mm_group(self, mms, r_list, w):
        eng = "pe"
        self._deps(eng, (), w)
        n = self.cnt[eng] + 1
        self.cnt[eng] = n
        sem, _ = self._sem_for(eng, n)
        ev = ('c', eng, n)
        last = len(mms) - 1
        for i, fn in enumerate(mms):
            self._deps(eng, r_list[i], ())
            if i == last:
                self.streams[eng].append(lambda e, fn=fn, sem=sem: fn(e).then_inc(sem, 1))
            else:
                self.streams[eng].append(lambda e, fn=fn: fn(e))
            self.ninst += 1
            self._record(ev, r_list[i], ())
        self._record(ev, (), w)

    def dma(self, q, out, in_, r=(), w=()):
        self._deps(q, r, w, is_dma=True)
        j = self.dma_cnt[q]
        self.dma_cnt[q] = j + 1
        sem = self._dsem(q, j % NDMASEM)
        self.streams[q].append(lambda e, out=out, in_=in_, sem=sem: e.dma_start(out=out, in_=in_).then_inc(sem, 16))
        self.ninst += 1
        self._record(('d', q, j), r, w)

    def barrier(self):
        for eng in self.ALLE:
            for src in self.CE:
                if self.cnt[src] > 0 and src != eng:
                    self._wait(eng, ('c', src, self.cnt[src]))
            for q, c in self.dma_cnt.items():
                for j in range(max(0, c - NDMASEM), c):
                    self._wait(eng, ('d', q, j))

    def finish(self):
        for q, c in self.dma_cnt.items():
            for j in range(max(0, c - NDMASEM), c):
                self._wait("sp", ('d', q, j))
        for src in self.CE:
            if self.cnt[src] > 0:
                self._wait("sp", ('c', src, self.cnt[src]))
        nc = self.nc
        block = self.es.enter_context(nc.Block())
        st = self.streams

        @block.sync
        def _(e):
            for f in st["sp"]:
                f(e)

        @block.tensor
        def _(e):
            for f in st["pe"]:
                f(e)

        @block.scalar
        def _(e):
            for f in st["act"]:
                f(e)

        @block.vector
        def _(e):
            for f in st["dve"]:
                f(e)

        @block.gpsimd
        def _(e):
            for f in st["pool"]:
                f(e)


G = 384
D = 1024
DFF = 2816
NFC = 22
EPS = 1e-6


def consts(P):
    C = {}
    C["ones_f"] = P.alloc([128, 128], F32, persist=True)
    C["eps"] = P.alloc([128, 1], F32, persist=True)
    C["r"] = Reg("consts")
    P.op("pool", lambda e: e.memset(C["ones_f"], 1.0), w=[C["r"]])
    P.op("pool", lambda e: e.memset(C["eps"], EPS), w=[C["r"]])
    return C


def rms_group(P, C, hbuf, r_h, N, sqc, ssum, rs, rstd, hn, r_hn, r_tmp, bank, nchunk=8, dim=1024):
    for c in range(nchunk):
        P.op("pool", lambda e, c=c: e.tensor_tensor(out=sqc[c % 2], in0=hbuf[:, c, :], in1=hbuf[:, c, :], op=ALU.mult),
             r=[r_h], w=[r_tmp[c % 2]])
        if c == 0:
            P.op("pool", lambda e: e.tensor_copy(out=ssum, in_=sqc[0]), r=[r_tmp[0]], w=[r_tmp[2]])
        else:
            P.op("pool", lambda e, c=c: e.tensor_tensor(out=ssum, in0=ssum, in1=sqc[c % 2], op=ALU.add),
                 r=[r_tmp[c % 2]], w=[r_tmp[2]])
    br = P.bank_regs[bank]
    P.mm_group([lambda e: e.matmul(P.bank(bank, N), lhsT=C["ones_f"], rhs=ssum, start=True, stop=True)],
               [[C["r"], r_tmp[2]]], [br])
    P.op("act", lambda e: e.activation(out=rs, in_=P.bank(bank, N), func=AF.Ln, bias=C["eps"], scale=1.0 / dim),
         r=[br, C["r"]], w=[r_tmp[3]])
    P.op("act", lambda e: e.activation(out=rstd, in_=rs, func=AF.Exp, scale=-0.5), r=[r_tmp[3]], w=[r_tmp[4]])
    if hn is not None:
        P.op("dve", lambda e: e.tensor_tensor(out=hn, in0=hbuf, in1=rstd.unsqueeze(1).broadcast_to([128, nchunk, N]), op=ALU.mult),
             r=[r_h, r_tmp[4]], w=[r_hn])


def ffn_phase(P, C, layer, w_up, w_down, normT, convT, src, dst, LP):
    P.reset_arena()
    ng = LP // G
    N = G + 2
    wup = P.alloc([128, 8, 2 * DFF], BF16)
    wdn = P.alloc([128, NFC, D], BF16)
    r_wup = [Reg() for _ in range(16)]
    r_wdn = [Reg() for _ in range(11)]
    gain = P.alloc([128, 8], F32)
    cw = P.alloc([128, 44, 3], F32)
    r_small = Reg()
    SW = 1408
    stage = [P.alloc([128, SW], F32) for _ in range(2)]
    r_stage = [Reg(), Reg()]
    hbuf = [P.alloc([128, 8, N], F32) for _ in range(2)]
    r_hbuf = [Reg(), Reg()]
    hn = P.alloc([128, 8, N], BF16)
    r_hn = Reg()
    act = P.alloc([128, NFC, G], BF16)
    r_act = [Reg() for _ in range(NFC)]
    tg = [P.alloc([128, G], F32) for _ in range(2)]
    tu = [P.alloc([128, G], F32) for _ in range(2)]
    r_tg = [Reg(), Reg()]
    r_tu = [Reg(), Reg()]
    sqc = [P.alloc([128, N], F32) for _ in range(2)]
    ssum = P.alloc([128, N], F32)
    rs = P.alloc([128, N], F32)
    rstd = P.alloc([128, N], F32)
    r_tmp = [Reg() for _ in range(5)]

    P.dma("sp", gain, normT, w=[r_small])
    P.dma("sp", cw, convT, w=[r_small])
    stage4 = stage + [hb.rearrange("p c t -> p (c t)")[:, 0:SW] for hb in hbuf]
    r_stage4 = r_stage + r_hbuf
    load_weight_bf16(P, w_up, 8, 2 * DFF, gain, r_small, stage4, r_stage4, wup, lambda kc, c0: r_wup[kc * 2 + c0 // DFF], SW)
    load_weight_bf16(P, w_down, NFC, D, None, None, [s_[:, 0:D] for s_ in stage4], r_stage4, wdn, lambda kc, c0: r_wdn[kc // 2], D)

    tg = tg + [stage[0][:, 0:G]]
    tu = tu + [stage[1][:, 0:G]]
    r_tg = r_tg + [r_stage[0]]
    r_tu = r_tu + [r_stage[1]]

    srcv = src.rearrange("(c p) t -> p c t", p=128)
    dstv = dst.rearrange("(c p) t -> p c t", p=128)

    def load(g):
        b = g % 2
        n0 = g * G
        if g == 0:
            P.op("pool", lambda e: e.memset(hbuf[b][:, :, 0:2], 0.0), w=[r_hbuf[b]])
            P.dma("sp", hbuf[b][:, :, 2:N], srcv[:, :, 0:G], w=[r_hbuf[b]])
        else:
            P.dma("sp", hbuf[b], srcv[:, :, n0 - 2:n0 + G], w=[r_hbuf[b]])

    load(0)
    rms_group(P, C, hbuf[0], r_hbuf[0], N, sqc, ssum, rs, rstd, hn, r_hn, r_tmp, bank=7)
    for g in range(ng):
        b = g % 2
        n0 = g * G
        if g + 1 < ng:
            load(g + 1)
        for j in range(NFC):
            pb = (j % 3) * 2
            for half, (tt, r_tt) in enumerate(((tg, r_tg), (tu, r_tu))):
                bk = pb + half
                col = half * DFF + j * 128
                ch = half * NFC + j
                P.mm_group(
                    [lambda e, kc=kc, bk=bk, col=col: e.matmul(P.bank(bk, N), lhsT=wup[:, kc, col:col + 128], rhs=hn[:, kc, :],
                                                                start=(kc == 0), stop=(kc == 7)) for kc in range(8)],
                    [[r_wup[kc * 2 + (col // (2 * SW))], r_hn] for kc in range(8)], [P.bank_regs[bk]])
                t = tt[j % 3]
                rt = r_tt[j % 3]
                P.op("act", lambda e, bk=bk, t=t, ch=ch: e.activation(out=t, in_=P.bank(bk, N)[:, 0:G], func=AF.Copy, scale=cw[:, ch, 0:1]),
                     r=[P.bank_regs[bk], r_small], w=[rt])
                P.op("dve", lambda e, bk=bk, t=t, ch=ch: e.scalar_tensor_tensor(out=t, in0=P.bank(bk, N)[:, 1:G + 1], scalar=cw[:, ch, 1:2], in1=t,
                                                                                   op0=ALU.mult, op1=ALU.add),
                     r=[P.bank_regs[bk], r_small, rt], w=[rt])
                P.op("dve", lambda e, bk=bk, t=t, ch=ch: e.scalar_tensor_tensor(out=t, in0=P.bank(bk, N)[:, 2:G + 2], scalar=cw[:, ch, 2:3], in1=t,
                                                                                   op0=ALU.mult, op1=ALU.add),
                     r=[P.bank_regs[bk], r_small, rt], w=[rt])
            tgj, tuj = tg[j % 3], tu[j % 3]
            P.op("act", lambda e, tgj=tgj: e.activation(out=tgj, in_=tgj, func=AF.Silu), r=[r_tg[j % 3]], w=[r_tg[j % 3]])
            P.op("dve", lambda e, tgj=tgj, tuj=tuj, j=j: e.tensor_tensor(out=act[:, j, :], in0=tgj, in1=tuj, op=ALU.mult),
                 r=[r_tg[j % 3], r_tu[j % 3]], w=[r_act[j]])
        if g + 1 < ng:
            rms_group(P, C, hbuf[1 - b], r_hbuf[1 - b], N, sqc, ssum, rs, rstd, hn, r_hn, r_tmp, bank=7)
        for oc in range(8):
            bk = 6 + oc % 2
            P.mm_group(
                [lambda e, j=j, bk=bk, oc=oc: e.matmul(P.bank(bk, G), lhsT=wdn[:, j, oc * 128:(oc + 1) * 128], rhs=act[:, j, :],
                                                        start=(j == 0), stop=(j == NFC - 1)) for j in range(NFC)],
                [[r_wdn[j // 2], r_act[j]] for j in range(NFC)], [P.bank_regs[bk]])
            P.op("dve", lambda e, bk=bk, oc=oc, b=b: e.tensor_tensor(out=hbuf[b][:, oc, 2:N], in0=P.bank(bk, G), in1=hbuf[b][:, oc, 2:N], op=ALU.add),
                 r=[P.bank_regs[bk]], w=[r_hbuf[b]])
        P.dma("sp", dstv[:, :, n0:n0 + G], hbuf[b][:, :, 2:N], r=[r_hbuf[b]])
    P.barrier()


NWIN = 4112


def gdn_consts(P, C):
    r = C["r"]
    for name in ("triu", "mus", "negu", "ident"):
        C[name] = P.alloc([128, 128], F32, persist=True)
    P.op("pool", lambda e: e.memset(C["triu"], 1.0), w=[r])
    P.op("pool", lambda e: e.affine_select(out=C["triu"], in_=C["triu"], pattern=[[1, 128]], compare_op=ALU.is_ge, fill=0.0, base=0, channel_multiplier=-1), w=[r])
    P.op("pool", lambda e: e.memset(C["mus"], 1.0), w=[r])
    P.op("pool", lambda e: e.affine_select(out=C["mus"], in_=C["mus"], pattern=[[1, 128]], compare_op=ALU.is_gt, fill=0.0, base=0, channel_multiplier=-1), w=[r])
    P.op("pool", lambda e: e.memset(C["negu"], 0.0), w=[r])
    P.op("pool", lambda e: e.affine_select(out=C["negu"], in_=C["negu"], pattern=[[1, 128]], compare_op=ALU.is_ge, fill=-30000.0, base=0, channel_multiplier=-1), w=[r])
    P.op("pool", lambda e: e.memset(C["ident"], 1.0), w=[r])
    P.op("pool", lambda e: e.affine_select(out=C["ident"], in_=C["ident"], pattern=[[-1, 128]], compare_op=ALU.is_equal, fill=0.0, base=0, channel_multiplier=1), w=[r])


def gdn_proj_phase(P, C, w_in, normT, convT, alogB, dtbB, src, QT, KT, VT, ZT, BT, GT, LP):
    P.reset_arena()
    ng = LP // G
    N = G + 3
    wb = P.alloc([128, 8, NWIN], BF16)
    r_wb = [Reg() for _ in range(8)]
    gain = P.alloc([128, 8], F32)
    cw = P.alloc([128, 24, 4], F32)
    nega = P.alloc([128, 8], F32)
    dtb = P.alloc([128, 8], F32)
    r_small = Reg()
    SW = 2056
    stage = [P.alloc([128, SW], F32) for _ in range(3)]
    r_stage = [Reg() for _ in range(3)]
    hbuf = [P.alloc([128, 8, N], F32) for _ in range(2)]
    r_hbuf = [Reg(), Reg()]
    hn = P.alloc([128, 8, N], BF16)
    r_hn = Reg()
    sqc = [P.alloc([128, N], F32) for _ in range(2)]
    ssum = P.alloc([128, N], F32)
    rs = P.alloc([128, N], F32)
    rstd = P.alloc([128, N], F32)
    r_tmp = [Reg() for _ in range(5)]
    qkb = P.alloc([128, 16, G], F32)
    r_qk = [Reg() for _ in range(16)]
    NT = 3
    tb = [P.alloc([128, G], F32) for _ in range(NT)]
    r_tb = [Reg() for _ in range(NT)]
    sq2 = [P.alloc([128, G], F32) for _ in range(2)]
    r_sq2 = [Reg(), Reg()]
    rr = [P.alloc([128, G], F32) for _ in range(2)]
    r_rr = [Reg(), Reg()]
    sm = [P.alloc([128, 8, 8], F32) for _ in range(2)]
    r_sm = [Reg(), Reg()]
    cr = C["r"]

    P.dma("sp", gain, normT, w=[r_small])
    P.dma("sp", cw, convT, w=[r_small])
    P.dma("sp", nega, alogB, w=[r_small])
    P.dma("sp", dtb, dtbB, w=[r_small])
    P.op("act", lambda e: e.activation(out=nega, in_=nega, func=AF.Exp), r=[r_small], w=[r_small])
    P.op("dve", lambda e: e.tensor_scalar(out=nega, in0=nega, scalar1=-1.0, scalar2=None, op0=ALU.mult), r=[r_small], w=[r_small])
    load_weight_bf16(P, w_in, 8, NWIN, gain, r_small, stage, r_stage, wb, lambda kc, c0: r_wb[kc], SW)

    srcv = src.rearrange("(c p) t -> p c t", p=128)

    def load(g):
        b = g % 2
        n0 = g * G
        if g == 0:
            P.op("pool", lambda e: e.memset(hbuf[b][:, :, 0:3], 0.0), w=[r_hbuf[b]])
            P.dma("sp", hbuf[b][:, :, 3:N], srcv[:, :, 0:G], w=[r_hbuf[b]])
        else:
            P.dma("sp", hbuf[b], srcv[:, :, n0 - 3:n0 + G], w=[r_hbuf[b]])

    def rms(g):
        rms_group(P, C, hbuf[g % 2], r_hbuf[g % 2], N, sqc, ssum, rs, rstd, hn, r_hn, r_tmp, bank=7)

    load(0)
    rms(0)
    ti = 0
    for g in range(ng):
        n0 = g * G
        if g + 1 < ng:
            load(g + 1)
        for ch in range(24):
            bk = ch % 3
            col = ch * 128
            P.mm_group(
                [lambda e, kc=kc, bk=bk, col=col: e.matmul(P.bank(bk, N), lhsT=wb[:, kc, col:col + 128], rhs=hn[:, kc, :],
                                                            start=(kc == 0), stop=(kc == 7)) for kc in range(8)],
                [[r_wb[kc], r_hn] for kc in range(8)], [P.bank_regs[bk]])
            if ch < 16:
                t = qkb[:, ch, :]
                rt = r_qk[ch]
            else:
                t = tb[ti % NT]
                rt = r_tb[ti % NT]
                ti += 1
            P.op("act", lambda e, bk=bk, t=t, ch=ch: e.activation(out=t, in_=P.bank(bk, N)[:, 0:G], func=AF.Copy, scale=cw[:, ch, 0:1]),
                 r=[P.bank_regs[bk], r_small], w=[rt])
            for k in (1, 2, 3):
                P.op("dve", lambda e, bk=bk, t=t, ch=ch, k=k: e.scalar_tensor_tensor(
                    out=t, in0=P.bank(bk, N)[:, k:G + k], scalar=cw[:, ch, k:k + 1], in1=t, op0=ALU.mult, op1=ALU.add),
                    r=[P.bank_regs[bk], r_small, rt], w=[rt])
            P.op("act", lambda e, t=t: e.activation(out=t, in_=t, func=AF.Silu), r=[rt], w=[rt])
            if ch >= 16:
                hh = ch - 16
                P.dma("sp", VT[hh * 128:(hh + 1) * 128, n0:n0 + G], t, r=[rt])
        for hh in range(8):
            bk = hh % 3
            col = 3072 + hh * 128
            P.mm_group(
                [lambda e, kc=kc, bk=bk, col=col: e.matmul(P.bank(bk, G), lhsT=wb[:, kc, col:col + 128], rhs=hn[:, kc, 3:N],
                                                            start=(kc == 0), stop=(kc == 7)) for kc in range(8)],
                [[r_wb[kc], r_hn] for kc in range(8)], [P.bank_regs[bk]])
            t = tb[ti % NT]
            rt = r_tb[ti % NT]
            ti += 1
            P.op("act", lambda e, bk=bk, t=t: e.activation(out=t, in_=P.bank(bk, G), func=AF.Silu), r=[P.bank_regs[bk]], w=[rt])
            P.dma("sp", ZT[hh * 128:(hh + 1) * 128, n0:n0 + G], t, r=[rt])
        for i in range(G // 128):
            bk = 5 + i % 2
            c0 = 3 + i * 128
            P.mm_group(
                [lambda e, kc=kc, bk=bk, c0=c0: e.matmul(P.bank(bk, 16), lhsT=hn[:, kc, c0:c0 + 128], rhs=wb[:, kc, 4096:4112],
                                                          start=(kc == 0), stop=(kc == 7)) for kc in range(8)],
                [[r_wb[kc], r_hn] for kc in range(8)], [P.bank_regs[bk]])
            s_ = sm[i % 2]
            rs_ = r_sm[i % 2]
            ps = P.bank(bk, 16)
            P.op("act", lambda e, ps=ps, s_=s_: e.activation(out=s_[:, 0, :], in_=ps[:, 0:8], func=AF.Exp, scale=-1.0), r=[P.bank_regs[bk]], w=[rs_])
            P.op("dve", lambda e, s_=s_: e.tensor_scalar(out=s_[:, 0, :], in0=s_[:, 0, :], scalar1=1.0, scalar2=None, op0=ALU.add), r=[rs_], w=[rs_])
            P.op("dve", lambda e, s_=s_: e.reciprocal(out=s_[:, 0, :], in_=s_[:, 0, :]), r=[rs_], w=[rs_])
            P.op("dve", lambda e, ps=ps, s_=s_: e.tensor_tensor(out=s_[:, 1, :], in0=ps[:, 8:16], in1=dtb, op=ALU.add), r=[P.bank_regs[bk], r_small], w=[rs_])
            P.op("dve", lambda e, s_=s_: e.tensor_scalar(out=s_[:, 2, :], in0=s_[:, 1, :], scalar1=-1.0, scalar2=None, op0=ALU.mult), r=[rs_], w=[rs_])
            P.op("dve", lambda e, s_=s_: e.tensor_tensor(out=s_[:, 2, :], in0=s_[:, 2, :], in1=s_[:, 1, :], op=ALU.max), r=[rs_], w=[rs_])
            P.op("act", lambda e, s_=s_: e.activation(out=s_[:, 3, :], in_=s_[:, 2, :], func=AF.Exp, scale=-1.0), r=[rs_], w=[rs_])
            P.op("act", lambda e, s_=s_: e.activation(out=s_[:, 3, :], in_=s_[:, 3, :], func=AF.Ln, bias=C["ones_f"][:, 0:1], scale=1.0), r=[rs_, cr], w=[rs_])
            P.op("dve", lambda e, s_=s_: e.tensor_scalar(out=s_[:, 4, :], in0=s_[:, 1, :], scalar1=0.0, scalar2=None, op0=ALU.max), r=[rs_], w=[rs_])
            P.op("dve", lambda e, s_=s_: e.tensor_tensor(out=s_[:, 4, :], in0=s_[:, 4, :], in1=s_[:, 3, :], op=ALU.add), r=[rs_], w=[rs_])
            P.op("dve", lambda e, s_=s_: e.tensor_tensor(out=s_[:, 5, :], in0=s_[:, 4, :], in1=nega, op=ALU.mult), r=[rs_, r_small], w=[rs_])
            r0 = n0 + i * 128
            P.dma("sp", BT[r0:r0 + 128, :], s_[:, 0, :], r=[rs_])
            P.dma("sp", GT[r0:r0 + 128, :], s_[:, 5, :], r=[rs_])
        if g + 1 < ng:
            rms(g + 1)
        for ch in range(16):
            t = qkb[:, ch, :]
            rt = r_qk[ch]
            s2 = sq2[ch % 2]
            P.op("pool", lambda e, t=t, s2=s2: e.tensor_tensor(out=s2, in0=t, in1=t, op=ALU.mult), r=[rt], w=[r_sq2[ch % 2]])
            bk2 = 3 + ch % 2
            P.mm_group([lambda e, bk2=bk2, s2=s2: e.matmul(P.bank(bk2, G), lhsT=C["ones_f"], rhs=s2, start=True, stop=True)],
                       [[cr, r_sq2[ch % 2]]], [P.bank_regs[bk2]])
            r2 = rr[ch % 2]
            P.op("act", lambda e, bk2=bk2, r2=r2: e.activation(out=r2, in_=P.bank(bk2, G), func=AF.Ln, bias=C["eps"], scale=1.0),
                 r=[P.bank_regs[bk2], cr], w=[r_rr[ch % 2]])
            P.op("act", lambda e, r2=r2: e.activation(out=r2, in_=r2, func=AF.Exp, scale=-0.5), r=[r_rr[ch % 2]], w=[r_rr[ch % 2]])
            sc = (128.0 ** -0.5) if ch < 8 else 1.0
            P.op("dve", lambda e, t=t, r2=r2, sc=sc: e.scalar_tensor_tensor(out=t, in0=t, scalar=sc, in1=r2, op0=ALU.mult, op1=ALU.mult),
                 r=[rt, r_rr[ch % 2]], w=[rt])
            dstT = (QT, KT)[ch // 8]
            hh = ch % 8
            P.dma("sp", dstT[hh * 128:(hh + 1) * 128, n0:n0 + G], t, r=[rt])
    P.barrier()


def gdn_scan_phase(P, C, w_o, onormB, QT, KT, VT, ZT, BT, GT, src, dst, LP):
    P.reset_arena()
    nt = LP // 128
    cr = C["r"]
    wo = P.alloc([128, 8, 1024], BF16)
    r_wo = Reg()
    ogain = P.alloc([128, 1], F32)
    identb = P.alloc([128, 128], BF16)
    r_small = Reg()
    stage = [P.alloc([128, 1024], F32) for _ in range(2)]
    r_stage = [Reg(), Reg()]
    P.dma("sp", ogain, onormB, w=[r_small])
    P.op("pool", lambda e: e.tensor_copy(out=identb, in_=C["ident"]), r=[cr], w=[r_small])
    load_weight_bf16(P, w_o, 8, 1024, None, None, stage, r_stage, wo, r_wo, 1024)

    def big(dt=F32):
        return P.alloc([128, 8, 128], dt)

    IN = [dict(k=big(), q=big(), v=big(), z=big(), h=big(), b=P.alloc([128, 8], F32), g=P.alloc([128, 8], F32), r=Reg()) for _ in range(2)]
    S = big(); r_S = Reg()
    f32names = ["Dg", "DT", "DTs", "egc", "Y", "u", "vtmp", "sq", "og", "P0", "Pt0", "P1", "Pt1"]
    bfnames = ["Ybf", "Sbf", "kbf", "qbf", "kg", "kdec", "vtm", "wT", "qg", "IT", "vnew"]
    T = {n: big() for n in f32names}
    T.update({n: big(BF16) for n in bfnames})
    R = {n: Reg(n) for n in f32names + bfnames}
    ogb = P.alloc([128, 8, 128], BF16); r_ogb = Reg()
    sm = {n: P.alloc([128, 8], F32) for n in ("gc", "egc", "gl", "egl", "fd")}
    r_sm = {n: Reg() for n in sm}
    rs = P.alloc([128, 8, 128], F32); r_rs = Reg()

    P.op("pool", lambda e: e.memset(S, 0.0), w=[r_S])
    P.op("pool", lambda e: e.memset(T["Sbf"], 0.0), w=[R["Sbf"]])

    def v3(dram):
        return dram.rearrange("(h d) t -> d h t", d=128)

    def load(t):
        I = IN[t % 2]
        c0 = t * 128
        for key, dr in (("k", KT), ("q", QT), ("v", VT), ("z", ZT), ("h", src)):
            P.dma("sp", I[key], v3(dr)[:, :, c0:c0 + 128], w=[I["r"]])
        P.dma("sp", I["b"], BT[c0:c0 + 128, :], w=[I["r"]])
        P.dma("sp", I["g"], GT[c0:c0 + 128, :], w=[I["r"]])

    slot_i = [0]

    def slot():
        s = slot_i[0] % 3
        slot_i[0] += 1
        ap = P.psum[:, s * 1024:(s + 1) * 1024].rearrange("p (h c) -> p h c", h=8)
        return ap, [P.bank_regs[2 * s], P.bank_regs[2 * s + 1]], s

    def bc_row(m):
        return m.unsqueeze(1).broadcast_to([128, 8, 128])

    def bc_col(x):
        return x.unsqueeze(2).broadcast_to([128, 8, 128])

    def mm8(ps, regs, lhs, rhs, rl, rr_, transpose=None):
        fns = []
        rls = []
        for h in range(8):
            if transpose is not None:
                fn = (lambda e, h=h: e.transpose(ps[:, h, :], lhs[:, h, :], transpose))
            else:
                fn = (lambda e, h=h: e.matmul(ps[:, h, :], lhsT=lhs[:, h, :], rhs=rhs[:, h, :], start=True, stop=True))
            fns.append(fn)
            rls.append(list(rl) + list(rr_))
        P.mm_group(fns, rls, regs)

    load(0)
    for t in range(nt):
        I = IN[t % 2]
        rI = I["r"]
        c0 = t * 128
        if t + 1 < nt:
            load(t + 1)
        P.op("pool", lambda e, I=I: e.tensor_copy(out=T["kbf"], in_=I["k"]), r=[rI], w=[R["kbf"]])
        P.op("pool", lambda e, I=I: e.tensor_copy(out=T["qbf"], in_=I["q"]), r=[rI], w=[R["qbf"]])
        P.op("dve", lambda e, I=I: e.tensor_tensor(out=T["Dg"], in0=bc_row(C["triu"]), in1=bc_col(I["g"]), op=ALU.mult), r=[cr, rI], w=[R["Dg"]])
        gcB, rg_gcB, sg = slot()
        gcBf = P.psum[:, sg * 1024:(sg + 1) * 1024]
        Dgf = T["Dg"].rearrange("p h c -> p (h c)")
        P.mm_group([lambda e: e.matmul(gcBf[:, 0:512], lhsT=C["ones_f"], rhs=Dgf[:, 0:512], start=True, stop=True),
                    lambda e: e.matmul(gcBf[:, 512:1024], lhsT=C["ones_f"], rhs=Dgf[:, 512:1024], start=True, stop=True)],
                   [[cr, R["Dg"]], [cr, R["Dg"]]], rg_gcB)
        P.mm_group([lambda e, I=I: e.matmul(P.bank(6, 8), lhsT=C["triu"], rhs=I["g"], start=True, stop=True)], [[cr, rI]], [P.bank_regs[6]])
        P.op("dve", lambda e: e.tensor_copy(out=sm["gc"], in_=P.bank(6, 8)), r=[P.bank_regs[6]], w=[r_sm["gc"]])
        P.op("dve", lambda e: e.tensor_copy(out=sm["gl"], in_=gcB[:, :, 127]), r=rg_gcB, w=[r_sm["gl"]])
        P.op("act", lambda e: e.activation(out=sm["egc"], in_=sm["gc"], func=AF.Exp), r=[r_sm["gc"]], w=[r_sm["egc"]])
        P.op("act", lambda e: e.activation(out=sm["egl"], in_=sm["gl"], func=AF.Exp), r=[r_sm["gl"]], w=[r_sm["egl"]])
        P.op("dve", lambda e: e.tensor_tensor(out=sm["fd"], in0=sm["gl"], in1=sm["gc"], op=ALU.subtract), r=[r_sm["gl"], r_sm["gc"]], w=[r_sm["fd"]])
        P.op("act", lambda e: e.activation(out=sm["fd"], in_=sm["fd"], func=AF.Exp), r=[r_sm["fd"]], w=[r_sm["fd"]])
        P.op("dve", lambda e: e.tensor_tensor(out=T["DT"], in0=gcB, in1=bc_col(sm["gc"]), op=ALU.subtract), r=rg_gcB + [r_sm["gc"]], w=[R["DT"]])
        P.op("dve", lambda e: e.tensor_tensor(out=T["DT"], in0=T["DT"], in1=bc_row(C["negu"]), op=ALU.add), r=[cr, R["DT"]], w=[R["DT"]])
        P.op("act", lambda e: e.activation(out=T["DT"], in_=T["DT"], func=AF.Exp), r=[R["DT"]], w=[R["DT"]])
        P.op("act", lambda e: e.activation(out=T["egc"], in_=gcB, func=AF.Exp), r=rg_gcB, w=[R["egc"]])
        P.op("dve", lambda e: e.tensor_tensor(out=T["DTs"], in0=T["DT"], in1=bc_row(C["mus"]), op=ALU.mult), r=[cr, R["DT"]], w=[R["DTs"]])
        KK, rg_KK, _ = slot()
        mm8(KK, rg_KK, T["kbf"], T["kbf"], [R["kbf"]], [])
        ITp, rg_IT, _ = slot()
        mm8(ITp, rg_IT, T["kbf"], T["qbf"], [R["kbf"]], [R["qbf"]])
        for h in range(8):
            P.op("dve", lambda e, h=h, I=I: e.scalar_tensor_tensor(out=T["P0"][:, h, :], in0=KK[:, h, :], scalar=I["b"][:, h:h + 1], in1=T["DTs"][:, h, :],
                                                                      op0=ALU.mult, op1=ALU.mult), r=rg_KK + [rI, R["DTs"]], w=[R["P0"]])
        P.op("dve", lambda e: e.tensor_tensor(out=T["IT"], in0=ITp, in1=T["DT"], op=ALU.mult), r=rg_IT + [R["DT"]], w=[R["IT"]])
        P.op("pool", lambda e, I=I: e.tensor_tensor(out=T["qg"], in0=I["q"], in1=T["egc"], op=ALU.mult), r=[rI, R["egc"]], w=[R["qg"]])
        psb, rg, st_ = slot()
        mm8(psb, rg, T["P0"], None, [R["P0"], cr], [], transpose=C["ident"])
        P.op("act", lambda e, psb=psb: e.activation(out=T["Pt0"], in_=psb, func=AF.Copy), r=rg, w=[R["Pt0"]])
        P.op("dve", lambda e: e.tensor_tensor(out=T["Y"], in0=bc_row(C["ident"]), in1=T["P0"], op=ALU.subtract), r=[cr, R["P0"]], w=[R["Y"]])
        cur = 0
        for k in range(1, 7):
            Pc, Ptc = T[f"P{cur}"], T[f"Pt{cur}"]
            rPc, rPtc = R[f"P{cur}"], R[f"Pt{cur}"]
            nx = 1 - cur
            Pn, Ptn = T[f"P{nx}"], T[f"Pt{nx}"]
            rPn, rPtn = R[f"P{nx}"], R[f"Pt{nx}"]
            ps2, rg2, _ = slot()
            mm8(ps2, rg2, Pc, Ptc, [rPc], [rPtc])
            if k < 6:
                ps1, rg1, _ = slot()
                mm8(ps1, rg1, Ptc, Pc, [rPtc], [rPc])
            P.op("dve", lambda e, ps2=ps2, Ptn=Ptn: e.tensor_copy(out=Ptn, in_=ps2), r=rg2, w=[rPtn])
            if k < 6:
                P.op("act", lambda e, ps1=ps1, Pn=Pn: e.activation(out=Pn, in_=ps1, func=AF.Copy), r=rg1, w=[rPn])
            ps3, rg3, _ = slot()
            mm8(ps3, rg3, Ptn, T["Y"], [rPtn], [R["Y"]])
            P.op("dve", lambda e, ps3=ps3: e.tensor_tensor(out=T["Y"], in0=ps3, in1=T["Y"], op=ALU.add), r=rg3, w=[R["Y"]])
            cur = nx
        P.op("act", lambda e: e.activation(out=T["Ybf"], in_=T["Y"], func=AF.Copy), r=[R["Y"]], w=[R["Ybf"]])
        ps, rg, _ = slot()
        mm8(ps, rg, I["k"], None, [rI, cr], [], transpose=C["ident"])
        P.op("dve", lambda e, ps=ps: e.tensor_tensor(out=T["kg"], in0=ps, in1=bc_col(sm["egc"]), op=ALU.mult), r=rg + [r_sm["egc"]], w=[R["kg"]])
        P.op("dve", lambda e, ps=ps: e.tensor_tensor(out=T["kdec"], in0=ps, in1=bc_col(sm["fd"]), op=ALU.mult), r=rg + [r_sm["fd"]], w=[R["kdec"]])
        ps, rg, _ = slot()
        mm8(ps, rg, I["v"], None, [rI, cr], [], transpose=C["ident"])
        P.op("act", lambda e, ps=ps: e.activation(out=T["vtm"], in_=ps, func=AF.Copy), r=rg, w=[R["vtm"]])
        ps, rg, _ = slot()
        mm8(ps, rg, T["Ybf"], T["vtm"], [R["Ybf"]], [R["vtm"]])
        P.op("act", lambda e, ps=ps: e.activation(out=T["u"], in_=ps, func=AF.Copy), r=rg, w=[R["u"]])
        ps, rg, _ = slot()
        mm8(ps, rg, T["kg"], T["Ybf"], [R["kg"]], [R["Ybf"]])
        P.op("act", lambda e, ps=ps: e.activation(out=T["wT"], in_=ps, func=AF.Copy), r=rg, w=[R["wT"]])
        ps, rg, _ = slot()
        mm8(ps, rg, T["wT"], T["Sbf"], [R["wT"]], [R["Sbf"]])
        P.op("dve", lambda e, ps=ps: e.tensor_tensor(out=T["vtmp"], in0=T["u"], in1=ps, op=ALU.subtract), r=rg + [R["u"]], w=[R["vtmp"]])
        P.op("dve", lambda e, I=I: e.tensor_tensor(out=T["vnew"], in0=T["vtmp"], in1=bc_col(I["b"]), op=ALU.mult), r=[rI, R["vtmp"]], w=[R["vnew"]])
        pso, rgo, _ = slot()
        fns = []
        rls = []
        for h in range(8):
            fns.append(lambda e, h=h: e.matmul(pso[:, h, :], lhsT=T["Sbf"][:, h, :], rhs=T["qg"][:, h, :], start=True, stop=False))
            rls.append([R["Sbf"], R["qg"]])
            fns.append(lambda e, h=h: e.matmul(pso[:, h, :], lhsT=T["vnew"][:, h, :], rhs=T["IT"][:, h, :], start=False, stop=True))
            rls.append([R["vnew"], R["IT"]])
        P.mm_group(fns, rls, rgo)
        ps, rg, _ = slot()
        mm8(ps, rg, T["kdec"], T["vnew"], [R["kdec"]], [R["vnew"]])
        P.op("dve", lambda e: e.tensor_tensor(out=S, in0=S, in1=bc_col(sm["egl"]), op=ALU.mult), r=[r_sm["egl"]], w=[r_S])
        P.op("dve", lambda e, ps=ps: e.tensor_tensor(out=S, in0=S, in1=ps, op=ALU.add), r=rg, w=[r_S])
        P.op("act", lambda e: e.activation(out=T["Sbf"], in_=S, func=AF.Copy), r=[r_S], w=[R["Sbf"]])
        P.op("act", lambda e: e.activation(out=T["sq"], in_=pso, func=AF.Square), r=rgo, w=[R["sq"]])
        pst, rgt, st_ = slot()
        pstf = P.psum[:, st_ * 1024:(st_ + 1) * 1024]
        sqf = T["sq"].rearrange("p h c -> p (h c)")
        P.mm_group([lambda e: e.matmul(pstf[:, 0:512], lhsT=C["ones_f"], rhs=sqf[:, 0:512], start=True, stop=True),
                    lambda e: e.matmul(pstf[:, 512:1024], lhsT=C["ones_f"], rhs=sqf[:, 512:1024], start=True, stop=True)],
                   [[cr, R["sq"]], [cr, R["sq"]]], rgt)
        P.op("act", lambda e: e.activation(out=rs, in_=pst, func=AF.Ln, bias=C["eps"], scale=1.0 / 128), r=rgt + [cr], w=[r_rs])
        P.op("act", lambda e: e.activation(out=rs, in_=rs, func=AF.Exp, scale=-0.5), r=[r_rs], w=[r_rs])
        P.op("dve", lambda e: e.tensor_tensor(out=T["og"], in0=pso, in1=rs, op=ALU.mult), r=rgo + [r_rs], w=[R["og"]])
        P.op("dve", lambda e, I=I: e.scalar_tensor_tensor(out=ogb, in0=T["og"], scalar=ogain[:, 0:1], in1=I["z"], op0=ALU.mult, op1=ALU.mult),
             r=[R["og"], r_small, rI], w=[r_ogb])
        psp, rgp, _ = slot()
        fns = []
        rls = []
        for oc in range(8):
            for h in range(8):
                fns.append(lambda e, oc=oc, h=h: e.matmul(psp[:, oc, :], lhsT=wo[:, h, oc * 128:(oc + 1) * 128], rhs=ogb[:, h, :], start=(h == 0), stop=(h == 7)))
                rls.append([r_wo, r_ogb])
        P.mm_group(fns, rls, rgp)
        P.op("dve", lambda e, I=I: e.tensor_tensor(out=I["h"], in0=psp, in1=I["h"], op=ALU.add), r=rgp, w=[rI])
        P.dma("sp", v3(dst)[:, :, c0:c0 + 128], I["h"], r=[rI])
    P.barrier()


QB = G


def attn_consts(P, C):
    r = C["r"]
    C["kmx"] = P.alloc([128, 16], F32, persist=True)
    C["negK"] = P.alloc([128, 16], F32, persist=True)
    C["triu_b"] = P.alloc([128, 128], BF16, persist=True)
    C["r_kmx"] = Reg("kmx")
    P.op("pool", lambda e: e.memset(C["kmx"], 0.0), w=[C["r_kmx"]])
    P.op("pool", lambda e: e.tensor_copy(out=C["triu_b"], in_=C["triu"]), r=[r], w=[r])


def load_weight_bf16(P, w_dram, nrow_chunks, ncols, gain, r_small, stage, r_stage, dst, r_dst, SW):
    si = 0
    ns = len(stage)
    for kc in range(nrow_chunks):
        for c0 in range(0, ncols, SW):
            w_ = min(SW, ncols - c0)
            s = si % ns
            q = "sp"
            eng = ("act", "dve")[si % 2]
            si += 1
            P.dma(q, stage[s][:, 0:w_], w_dram[kc * 128:(kc + 1) * 128, c0:c0 + w_], w=[r_stage[s]])
            rd = r_dst(kc, c0) if callable(r_dst) else r_dst
            o_ = dst[:, kc, c0:c0 + w_]
            i_ = stage[s][:, 0:w_]
            if eng == "act":
                if gain is not None:
                    P.op("act", lambda e, o_=o_, i_=i_, kc=kc: e.activation(out=o_, in_=i_, func=AF.Copy, scale=gain[:, kc:kc + 1]),
                         r=[r_stage[s], r_small], w=[rd])
                else:
                    P.op("act", lambda e, o_=o_, i_=i_: e.activation(out=o_, in_=i_, func=AF.Copy), r=[r_stage[s]], w=[rd])
            else:
                if gain is not None:
                    P.op("dve", lambda e, o_=o_, i_=i_, kc=kc: e.tensor_scalar(out=o_, in0=i_, scalar1=gain[:, kc:kc + 1], scalar2=None, op0=ALU.mult),
                         r=[r_stage[s], r_small], w=[rd])
                else:
                    P.op("dve", lambda e, o_=o_, i_=i_: e.tensor_copy(out=o_, in_=i_), r=[r_stage[s]], w=[rd])


def attn_proj_phase(P, C, src, normT, w_dram, is_kv, XA, VA, LP):
    P.reset_arena()
    ng = LP // G
    N = G
    ncols = 2048 if is_kv else 1024
    wb = P.alloc([128, 8, ncols], BF16)
    r_wb = Reg()
    gain = P.alloc([128, 8], F32)
    r_small = Reg()
    SW = 1024
    stage = [P.alloc([128, SW], F32) for _ in range(2)]
    r_stage = [Reg(), Reg()]
    hbuf = [P.alloc([128, 8, N], F32) for _ in range(2)]
    r_hbuf = [Reg(), Reg()]
    hn = P.alloc([128, 8, N], BF16)
    r_hn = Reg()
    sqc = [P.alloc([128, N], F32) for _ in range(2)]
    ssum = P.alloc([128, N], F32)
    rs = P.alloc([128, N], F32)
    rstd = P.alloc([128, N], F32)
    r_tmp = [Reg() for _ in range(5)]
    xa = [P.alloc([128, G], BF16) for _ in range(3)]
    r_xa = [Reg() for _ in range(3)]
    sqk = [P.alloc([128, G], F32) for _ in range(2)]
    r_sqk = [Reg(), Reg()]
    r3 = [P.alloc([128, G], F32) for _ in range(2)]
    r_r3 = [Reg(), Reg()]
    mx = P.alloc([128, 2], F32)
    r_mx = Reg()
    cr = C["r"]
    P.dma("sp", gain, normT, w=[r_small])
    load_weight_bf16(P, w_dram, 8, ncols, gain, r_small, stage, r_stage, wb, r_wb, SW)
    if is_kv:
        vaug = [P.alloc([128, 8, 129], BF16) for _ in range(2)]
        r_vaug = [Reg(), Reg()]
        for b in range(2):
            P.op("pool", lambda e, b=b: e.memset(vaug[b][:, :, 128:129], 1.0), w=[r_vaug[b]])
        for b in range(3):
            P.op("pool", lambda e, b=b: e.memset(xa[b][64:96, :], 0.0), w=[r_xa[b]])
            P.op("pool", lambda e, b=b: e.memset(xa[b][64:67, :], 1.0), w=[r_xa[b]])
        B3 = P.alloc([128, 3, 128], F32)
        r_B3 = Reg()
        P.op("pool", lambda e: e.iota(B3, pattern=[[1, 3], [0, 128]], base=0, channel_multiplier=0, allow_small_or_imprecise_dtypes=True), w=[r_B3])
        B3f = B3.rearrange("p a b -> p (a b)")
    else:
        A = P.alloc([128, 3, 128], F32)
        Bv = P.alloc([128, 3, 128], F32)
        tpl = P.alloc([128, 8, G], F32)
        r_tpl = Reg()
        P.op("pool", lambda e: e.iota(A, pattern=[[0, 3], [1, 128]], base=0, channel_multiplier=0, allow_small_or_imprecise_dtypes=True), w=[r_tpl])
        P.op("pool", lambda e: e.iota(Bv, pattern=[[128, 3], [0, 128]], base=0, channel_multiplier=0, allow_small_or_imprecise_dtypes=True), w=[r_tpl])
        Af = A.rearrange("p a b -> p (a b)")
        Bf = Bv.rearrange("p a b -> p (a b)")
        P.op("dve", lambda e: e.tensor_scalar(out=Af, in0=Af, scalar1=C["ident"][:, 65:66], scalar2=None, op0=ALU.mult), r=[cr, r_tpl], w=[r_tpl])
        P.op("dve", lambda e: e.tensor_scalar(out=Bf, in0=Bf, scalar1=C["ident"][:, 66:67], scalar2=None, op0=ALU.mult), r=[cr, r_tpl], w=[r_tpl])
        P.op("dve", lambda e: e.tensor_tensor(out=Af, in0=Af, in1=Bf, op=ALU.add), r=[r_tpl], w=[r_tpl])
        for h in range(8):
            P.op("dve", lambda e, h=h: e.tensor_scalar(out=tpl[:, h, :], in0=Af, scalar1=-(2.0 ** -(h + 1)), scalar2=None, op0=ALU.mult), r=[r_tpl], w=[r_tpl])
        for b in range(3):
            P.op("pool", lambda e, b=b: e.memset(xa[b][64:96, :], 0.0), w=[r_xa[b]])
            P.op("pool", lambda e, b=b: e.memset(xa[b][96:97, :], 1.0), w=[r_xa[b]])
        GI = P.alloc([128, 16], F32)
        gq = P.alloc([128, 8, 16], F32)
        P.op("pool", lambda e: e.iota(GI, pattern=[[1, 16]], base=0, channel_multiplier=0, allow_small_or_imprecise_dtypes=True), w=[r_tpl])
        P.op("dve", lambda e: e.tensor_scalar(out=GI, in0=GI, scalar1=C["ident"][:, 66:67], scalar2=None, op0=ALU.mult), r=[cr, r_tpl], w=[r_tpl])
        for h in range(8):
            P.op("dve", lambda e, h=h: e.tensor_scalar(out=gq[:, h, :], in0=GI, scalar1=-(2.0 ** -(h + 1)) * 384.0, scalar2=None, op0=ALU.mult), r=[r_tpl], w=[r_tpl])
        P.op("act", lambda e: e.activation(out=C["negK"][64:65, :], in_=C["kmx"][64:65, :], func=AF.Ln, bias=C["eps"][64:65, :], scale=1.0), r=[C["r_kmx"], cr], w=[C["r_kmx"]])
        P.op("act", lambda e: e.activation(out=C["negK"][64:65, :], in_=C["negK"][64:65, :], func=AF.Exp, scale=0.5), r=[C["r_kmx"]], w=[C["r_kmx"]])
        P.op("dve", lambda e: e.tensor_scalar(out=C["negK"][64:65, :], in0=C["negK"][64:65, :], scalar1=-1.0, scalar2=None, op0=ALU.mult), r=[C["r_kmx"]], w=[C["r_kmx"]])

    srcv = src.rearrange("(c p) t -> p c t", p=128)

    def load(g):
        b = g % 2
        P.dma("sp", hbuf[b], srcv[:, :, g * G:(g + 1) * G], w=[r_hbuf[b]])

    load(0)
    xi = 0
    for g in range(ng):
        b = g % 2
        n0 = g * G
        if g + 1 < ng:
            load(g + 1)
        rms_group(P, C, hbuf[b], r_hbuf[b], N, sqc, ssum, rs, rstd, hn, r_hn, r_tmp, bank=7)
        for hi in range(16):
            bk = hi % 3
            col = hi * 64
            ps = P.bank(bk, G, parts=64)
            P.mm_group(
                [lambda e, kc=kc, ps=ps, col=col: e.matmul(ps, lhsT=wb[:, kc, col:col + 64], rhs=hn[:, kc, :], start=(kc == 0), stop=(kc == 7)) for kc in range(8)],
                [[r_wb, r_hn] for kc in range(8)], [P.bank_regs[bk]])
            x_ = xa[xi % 3]
            rx = r_xa[xi % 3]
            xi += 1
            sq_ = sqk[hi % 2]
            rsq = r_sqk[hi % 2]
            sc = 1.0 if is_kv else 0.125
            P.op("act", lambda e, ps=ps, x_=x_, sc=sc: e.activation(out=x_[0:64, :], in_=ps, func=AF.Copy, scale=sc), r=[P.bank_regs[bk]], w=[rx])
            P.op("act", lambda e, ps=ps, sq_=sq_, sc=sc: e.activation(out=sq_[0:64, :], in_=ps, func=AF.Square, scale=sc), r=[P.bank_regs[bk]], w=[rsq])
            bk2 = 3 + hi % 2
            ps2 = P.bank(bk2, G, parts=65)
            P.mm_group([lambda e, ps2=ps2, sq_=sq_: e.matmul(ps2, lhsT=C["ones_f"][0:64, 0:65], rhs=sq_[0:64, :], start=True, stop=True)],
                       [[cr, rsq]], [P.bank_regs[bk2]])
            if is_kv:
                sl_ = 2.0 ** -((hi // 2) + 1)
                P.op("pool", lambda e, x_=x_, sl_=sl_, g=g: e.tensor_scalar(out=x_[96:97, :], in0=B3f[96:97, :], scalar1=sl_ * 128.0, scalar2=sl_ * 384.0 * g,
                                                                              op0=ALU.mult, op1=ALU.add), r=[r_B3], w=[rx])
                P.op("dve", lambda e, ps2=ps2, hi=hi: e.reduce_max(out=mx[64:65, 0:1], in_=ps2[64:65, :], axis=AX.X), r=[P.bank_regs[bk2]], w=[r_mx])
                P.op("dve", lambda e, hi=hi: e.tensor_tensor(out=C["kmx"][64:65, hi:hi + 1], in0=C["kmx"][64:65, hi:hi + 1], in1=mx[64:65, 0:1], op=ALU.max),
                     r=[r_mx], w=[C["r_kmx"]])
            else:
                rr_ = r3[hi % 2]
                rr3 = r_r3[hi % 2]
                h = hi // 2
                P.op("dve", lambda e, rr_=rr_, h=h, g=g: e.tensor_scalar(out=rr_[64:67, :], in0=tpl[64:67, h, :], scalar1=gq[64:67, h, g:g + 1], scalar2=None, op0=ALU.add),
                     r=[r_tpl], w=[rr3])
                P.op("act", lambda e, rr_=rr_, ps2=ps2: e.activation(out=rr_[64:65, :], in_=ps2[64:65, :], func=AF.Ln, bias=C["eps"][64:65, :], scale=1.0), r=[P.bank_regs[bk2], cr], w=[rr3])
                P.op("act", lambda e, rr_=rr_: e.activation(out=rr_[64:65, :], in_=rr_[64:65, :], func=AF.Exp, scale=0.5), r=[rr3], w=[rr3])
                P.op("dve", lambda e, rr_=rr_, hi=hi: e.tensor_scalar(out=rr_[64:65, :], in0=rr_[64:65, :], scalar1=C["negK"][64:65, hi:hi + 1], scalar2=None, op0=ALU.mult),
                     r=[C["r_kmx"], rr3], w=[rr3])
                P.op("dve", lambda e, rr_=rr_, x_=x_: e.tensor_copy(out=x_[64:67, :], in_=rr_[64:67, :]), r=[rr3], w=[rx])
            P.dma("sp", XA[hi, :, n0:n0 + G], x_[0:97, :], r=[rx])
        if is_kv:
            for i in range(G // 128):
                va = vaug[i % 2]
                rva = r_vaug[i % 2]
                for half in range(2):
                    bk = 5 + half
                    P.mm_group(
                        [lambda e, kc=kc, bk=bk, i=i, half=half: e.matmul(P.bank(bk, 512), lhsT=hn[:, kc, i * 128:(i + 1) * 128],
                                                                           rhs=wb[:, kc, 1024 + half * 512:1024 + (half + 1) * 512],
                                                                           start=(kc == 0), stop=(kc == 7)) for kc in range(8)],
                        [[r_wb, r_hn] for kc in range(8)], [P.bank_regs[bk]])
                    P.op("act", lambda e, bk=bk, va=va, half=half: e.activation(
                        out=va[:, half * 4:(half + 1) * 4, 0:128], in_=P.bank(bk, 512).rearrange("p (h e) -> p h e", h=4), func=AF.Copy),
                        r=[P.bank_regs[bk]], w=[rva])
                r0 = n0 + i * 128
                P.dma("sp", VA[r0:r0 + 128, :, :], va, r=[rva])
    P.barrier()


def attn_core_phase(P, C, QA, KA, VA, OG, lams, lam_init, sublnB, LP):
    P.reset_arena()
    nt = LP // 128
    nqb = LP // QB
    cr = C["r"]
    NR = 97
    ka = [[P.alloc([128, LP], BF16) for _ in range(2)] for _ in range(2)]
    qa = [[P.alloc([128, LP], BF16) for _ in range(2)] for _ in range(2)]
    va = [P.alloc([128, nt, 129], BF16) for _ in range(2)]
    r_in = [Reg(), Reg()]
    BI = P.alloc([128, 1], F32)
    biasK = P.alloc([128, 8], F32)
    r_bias = Reg()
    lam = P.alloc([128, 1], F32)
    subln = P.alloc([128, 128], F32)
    r_small = Reg()
    LA = 1
    NSLOT = 2
    NPT = 4
    pt = [P.alloc([128, 2, QB], BF16) for _ in range(NPT)]
    r_pt = [Reg() for _ in range(NPT)]
    accs = [P.alloc([128, 2, 3, 129], F32) for _ in range(2)]
    r_accs = [[[Reg() for _ in range(3)] for _ in range(2)] for _ in range(2)]
    NO = 6
    o_ = [P.alloc([128, 128], F32) for _ in range(NO)]
    r_o = [Reg() for _ in range(NO)]
    sm = [P.alloc([128, 8], F32) for _ in range(NO)]
    r_sm = [Reg() for _ in range(NO)]
    junk = P.alloc([128, 128], F32)
    NOG = 18
    ogT = [P.alloc([128, 128], BF16) for _ in range(NOG)]
    r_ogT = [Reg() for _ in range(NOG)]

    lv = [P.alloc([128, 64], F32) for _ in range(4)]
    l2 = P.alloc([128, 2], F32)
    for k_ in range(4):
        P.dma("sp", lv[k_], lams[k_], w=[r_small])
    for k_ in range(2):
        P.op("dve", lambda e, k_=k_: e.tensor_tensor(out=lv[2 * k_], in0=lv[2 * k_], in1=lv[2 * k_ + 1], op=ALU.mult), r=[r_small], w=[r_small])
        P.op("dve", lambda e, k_=k_: e.reduce_sum(out=l2[:, k_:k_ + 1], in_=lv[2 * k_], axis=AX.X), r=[r_small], w=[r_small])
    P.op("act", lambda e: e.activation(out=l2, in_=l2, func=AF.Exp), r=[r_small], w=[r_small])
    P.op("dve", lambda e: e.tensor_tensor(out=lam, in0=l2[:, 0:1], in1=l2[:, 1:2], op=ALU.subtract), r=[r_small], w=[r_small])
    P.op("dve", lambda e: e.tensor_scalar(out=lam, in0=lam, scalar1=float(lam_init), scalar2=None, op0=ALU.add), r=[r_small], w=[r_small])
    P.dma("sp", subln, sublnB, w=[r_small])
    P.op("dve", lambda e: e.tensor_scalar(out=subln, in0=subln, scalar1=float(1.0 - lam_init), scalar2=None, op0=ALU.mult), r=[r_small], w=[r_small])
    P.op("pool", lambda e: e.iota(BI, pattern=[[0, 1]], base=0, channel_multiplier=1, allow_small_or_imprecise_dtypes=True), w=[r_bias])
    for h in range(8):
        P.op("dve", lambda e, h=h: e.tensor_scalar(out=biasK[:, h:h + 1], in0=BI, scalar1=2.0 ** -(h + 1), scalar2=None, op0=ALU.mult), r=[r_bias], w=[r_bias])

    def load_parts(h):
        b = h % 2
        parts = []
        for i in range(2):
            parts.append(lambda b=b, i=i, h=h: P.dma("sp", ka[b][i][0:NR, :], KA[2 * h + i, :, :], w=[r_in[b]]))
            parts.append(lambda b=b, i=i, h=h: P.dma("sp", qa[b][i][0:NR, :], QA[2 * h + i, :, :], w=[r_in[b]]))
        vav = VA[:, h, :].rearrange("(t p) e -> p t e", p=128)
        for t0_ in range(0, nt, 11):
            t1_ = min(nt, t0_ + 11)
            parts.append(lambda b=b, t0_=t0_, t1_=t1_, vav=vav: P.dma("sp", va[b][:, t0_:t1_, :], vav[:, t0_:t1_, :], w=[r_in[b]]))
        return parts

    pending_loads = []
    groups = []
    for h in range(8):
        for qb in range(nqb):
            for i in range(2):
                nfull = 3 * qb
                kts = [[k, k + 1] for k in range(0, nfull - 1, 2)]
                if nfull % 2:
                    kts.append([nfull - 1])
                kts += [[nfull], [nfull + 1], [nfull + 2]]
                for gi, kl in enumerate(kts):
                    groups.append((h, qb, i, kl, gi == len(kts) - 1))
    cnt = dict(s=0, p=0, o=0)
    S_info = {}

    def emit_S(idx):
        h, qb, i, kl, _ = groups[idx]
        b = h % 2
        q0 = qb * QB
        slot = cnt["s"] % NSLOT
        cnt["s"] += 1
        j = max(0, kl[0] - 3 * qb)
        off = j * 128
        w_ = QB - off
        fns = []
        regs = []
        for t_, kt in enumerate(kl):
            bk = 3 + 2 * slot + t_
            ps = P.bank(bk, w_)
            fns.append(lambda e, ps=ps, kt=kt, off=off, b=b, i=i, q0=q0: e.matmul(
                ps, lhsT=ka[b][i][0:NR, kt * 128:(kt + 1) * 128], rhs=qa[b][i][0:NR, q0 + off:q0 + QB], start=True, stop=True))
            regs.append(P.bank_regs[bk])
        P.mm_group(fns, [[r_in[b]]] * len(fns), regs)
        S_info[idx] = (slot, regs, j, off, w_)

    deferred = []

    def tick():
        for d in deferred:
            d[0] -= 1
        while deferred and deferred[0][0] <= 0:
            deferred.pop(0)[1]()

    for f_ in load_parts(0):
        f_()
    s_emitted = 0
    ngr = len(groups)
    for idx in range(ngr):
        h, qb, i, kl, is_last = groups[idx]
        b = h % 2
        rI = r_in[b]
        q0 = qb * QB
        if qb == 0 and i == 0 and kl[0] == 0 and h + 1 < 8:
            pending_loads.extend(load_parts(h + 1))
        if pending_loads and (idx % 8 == 0):
            pending_loads.pop(0)()
        while s_emitted < min(idx + LA + 1, ngr):
            if groups[s_emitted][0] != h:
                while pending_loads:
                    pending_loads.pop(0)()
            emit_S(s_emitted)
            s_emitted += 1
        slot, sregs, j, off, w_ = S_info.pop(idx)
        n_ = len(kl)
        p_ = pt[cnt["p"] % NPT]
        rp = r_pt[cnt["p"] % NPT]
        cnt["p"] += 1
        b0 = 3 + 2 * slot
        if n_ == 2:
            src_ap = P.psum[:, b0 * 512:b0 * 512 + 1024].rearrange("p (t c) -> p t c", t=2)[:, :, 0:QB]
            dst_ap = p_[:, :, :]
        else:
            src_ap = P.bank(b0, w_)
            dst_ap = p_[:, 0, 0:w_]
        P.op("act", lambda e, src_ap=src_ap, dst_ap=dst_ap, h=h: e.activation(out=dst_ap, in_=src_ap, func=AF.Exp, bias=biasK[:, h:h + 1], scale=1.0),
             r=sregs + [r_bias], w=[rp])
        if kl[0] >= 3 * qb:
            P.op("pool", lambda e, p_=p_: e.tensor_tensor(out=p_[:, 0, 0:128], in0=p_[:, 0, 0:128], in1=C["triu_b"], op=ALU.mult), r=[rp, cr], w=[rp])
        ab_ = qb % 2
        for t_, kt in enumerate(kl):
            for jj in range(j, 3):
                acc = P.bank(jj, 129)
                c_ = jj * 128 - off
                first = (kt == 0)
                last = (kt == 3 * qb + jj)
                P.mm_group([lambda e, acc=acc, p_=p_, t_=t_, c_=c_, kt=kt, first=first, last=last, b=b: e.matmul(
                    acc, lhsT=p_[:, t_, c_:c_ + 128], rhs=va[b][:, kt, :], start=first, stop=last)],
                    [[rp, rI]], [P.bank_regs[jj]])
                if last:
                    P.op("act", lambda e, acc=acc, ab_=ab_, i=i, jj=jj: e.activation(out=accs[ab_][:, i, jj, :], in_=acc, func=AF.Copy),
                         r=[P.bank_regs[jj]], w=[r_accs[ab_][i][jj]])
        tick()
        if i == 1 and is_last:
            for jj in range(3):
                a0 = accs[ab_][:, 0, jj, :]
                a1 = accs[ab_][:, 1, jj, :]
                ra0, ra1 = r_accs[ab_][0][jj], r_accs[ab_][1][jj]
                oi = cnt["o"]
                cnt["o"] += 1
                s_ = sm[oi % NO]
                rs_ = r_sm[oi % NO]
                o = o_[oi % NO]
                ro = r_o[oi % NO]
                og = ogT[oi % NOG]
                rog = r_ogT[oi % NOG]
                P.op("dve", lambda e, s_=s_, a0=a0: e.reciprocal(out=s_[:, 0:1], in_=a0[:, 128:129]), r=[ra0], w=[rs_])
                P.op("dve", lambda e, s_=s_, a1=a1: e.reciprocal(out=s_[:, 1:2], in_=a1[:, 128:129]), r=[ra1], w=[rs_])
                P.op("dve", lambda e, s_=s_: e.scalar_tensor_tensor(out=s_[:, 2:3], in0=s_[:, 1:2], scalar=-1.0, in1=lam, op0=ALU.mult, op1=ALU.mult), r=[rs_, r_small], w=[rs_])
                P.op("dve", lambda e, s_=s_, a0=a0, o=o: e.tensor_scalar(out=o, in0=a0[:, 0:128], scalar1=s_[:, 0:1], scalar2=None, op0=ALU.mult), r=[ra0, rs_], w=[ro])
                P.op("dve", lambda e, s_=s_, a1=a1, o=o: e.scalar_tensor_tensor(out=o, in0=a1[:, 0:128], scalar=s_[:, 2:3], in1=o, op0=ALU.mult, op1=ALU.add),
                     r=[ra1, rs_, ro], w=[ro])
                P.op("dve", lambda e, s_=s_: e.memset(s_[:, 3:4], 0.0), w=[rs_])
                P.op("act", lambda e, s_=s_, o=o: e.activation(out=junk, in_=o, func=AF.Square, accum_out=s_[:, 3:4]), r=[ro], w=[rs_])
                P.op("act", lambda e, s_=s_: e.activation(out=s_[:, 4:5], in_=s_[:, 3:4], func=AF.Ln, bias=C["eps"], scale=1.0 / 128), r=[rs_, cr], w=[rs_])
                P.op("act", lambda e, s_=s_: e.activation(out=s_[:, 5:6], in_=s_[:, 4:5], func=AF.Exp, scale=-0.5), r=[rs_], w=[rs_])
                P.op("dve", lambda e, s_=s_, o=o: e.scalar_tensor_tensor(out=o, in0=o, scalar=s_[:, 5:6], in1=subln, op0=ALU.mult, op1=ALU.mult),
                     r=[ro, rs_, r_small], w=[ro])
                c0 = q0 + jj * 128

                def fin(o=o, ro=ro, og=og, rog=rog, h=h, c0=c0):
                    tb = 7
                    P.mm_group([lambda e, tb=tb, o=o: e.transpose(P.bank(tb, 128), o, C["ident"])], [[ro, cr]], [P.bank_regs[tb]])
                    P.op("dve", lambda e, tb=tb, og=og: e.tensor_copy(out=og, in_=P.bank(tb, 128)), r=[P.bank_regs[tb]], w=[rog])
                    P.dma("sp", OG[h * 128:(h + 1) * 128, c0:c0 + 128], og, r=[rog])
                deferred.append([2 + jj, fin])
    while deferred:
        deferred.pop(0)[1]()
    P.barrier()


def outproj_phase(P, C, w_o, OG, src, dst, LP):
    P.reset_arena()
    ng = LP // G
    wb = P.alloc([128, 8, 1024], BF16)
    r_wb = Reg()
    stage = [P.alloc([128, 1024], F32) for _ in range(2)]
    r_stage = [Reg(), Reg()]
    load_weight_bf16(P, w_o, 8, 1024, None, None, stage, r_stage, wb, r_wb, 1024)
    hbuf = [P.alloc([128, 8, G], F32) for _ in range(2)]
    ogb = [P.alloc([128, 8, G], BF16) for _ in range(2)]
    r_in = [Reg(), Reg()]
    srcv = src.rearrange("(c p) t -> p c t", p=128)
    dstv = dst.rearrange("(c p) t -> p c t", p=128)
    ogv = OG.rearrange("(c p) t -> p c t", p=128)

    def load(g):
        b = g % 2
        P.dma("sp", hbuf[b], srcv[:, :, g * G:(g + 1) * G], w=[r_in[b]])
        P.dma("sp", ogb[b], ogv[:, :, g * G:(g + 1) * G], w=[r_in[b]])

    load(0)
    for g in range(ng):
        b = g % 2
        if g + 1 < ng:
            load(g + 1)
        for oc in range(8):
            bk = oc % 4
            P.mm_group([lambda e, kc=kc, bk=bk, oc=oc, b=b: e.matmul(P.bank(bk, G), lhsT=wb[:, kc, oc * 128:(oc + 1) * 128], rhs=ogb[b][:, kc, :],
                                                                      start=(kc == 0), stop=(kc == 7)) for kc in range(8)],
                       [[r_wb, r_in[b]] for kc in range(8)], [P.bank_regs[bk]])
            P.op("dve", lambda e, bk=bk, oc=oc, b=b: e.tensor_tensor(out=hbuf[b][:, oc, :], in0=P.bank(bk, G), in1=hbuf[b][:, oc, :], op=ALU.add),
                 r=[P.bank_regs[bk]], w=[r_in[b]])
        P.dma("sp", dstv[:, :, g * G:(g + 1) * G], hbuf[b], r=[r_in[b]])
    P.barrier()


def final_phase(P, C, src, normT, out, LP, t0, t1):
    P.reset_arena()
    ng = LP // G
    N = G
    gain = P.alloc([128, 8], F32)
    r_small = Reg()
    hbuf = [P.alloc([128, 8, N], F32) for _ in range(2)]
    r_hbuf = [Reg(), Reg()]
    hn = P.alloc([128, 8, N], BF16)
    r_hn = Reg()
    sqc = [P.alloc([128, N], F32) for _ in range(2)]
    ssum = P.alloc([128, N], F32)
    rs = P.alloc([128, N], F32)
    rstd = P.alloc([128, N], F32)
    r_tmp = [Reg() for _ in range(5)]
    P.dma("sp", gain, normT, w=[r_small])
    srcv = src.rearrange("(c p) t -> p c t", p=128)
    outv = out.rearrange("(c p) t -> p c t", p=128)

    def load(g):
        b = g % 2
        P.dma("sp", hbuf[b], srcv[:, :, g * G:(g + 1) * G], w=[r_hbuf[b]])

    load(0)
    for g in range(ng):
        b = g % 2
        n0 = g * G
        if g + 1 < ng:
            load(g + 1)
        rms_group(P, C, hbuf[b], r_hbuf[b], N, sqc, ssum, rs, rstd, hn, r_hn, r_tmp, bank=7)
        for c in range(8):
            P.op("dve", lambda e, c=c, b=b: e.scalar_tensor_tensor(out=hbuf[b][:, c, :], in0=hbuf[b][:, c, :], scalar=gain[:, c:c + 1], in1=rstd,
                                                                     op0=ALU.mult, op1=ALU.mult), r=[r_small, r_tmp[4]], w=[r_hbuf[b]])
        a = max(n0, t0)
        bb = min(n0 + G, t1)
        if bb > a:
            P.dma("sp", outv[:, :, a - t0:bb - t0], hbuf[b][:, :, a - n0:bb - n0], r=[r_hbuf[b]])
    P.barrier()

LP_FULL = 4224
L_REAL = 4112
N_META = 16


def build_program(LP=LP_FULL, stop=None, debug=False):
    nc = bass.Bass("TRN2", target_bir_lowering=False)

    def din(name, shape):
        return nc.dram_tensor(name, list(shape), F32, kind="ExternalInput").ap()

    def scr(name, shape, dt=F32, out=False):
        return nc.dram_tensor(name, list(shape), dt, kind=("ExternalOutput" if out else "Internal")).ap()

    ht0 = din("ht0", [1024, LP])
    a_w_in = din("a_w_in", [2, 1024, 4112]); a_w_o = din("a_w_o", [2, 1024, 1024])
    a_normT = din("a_normT", [2, 128, 8]); a_convT = din("a_convT", [2, 128, 24, 4])
    a_logB = din("a_logB", [2, 128, 8]); a_dtbB = din("a_dtbB", [2, 128, 8]); a_onormB = din("a_onormB", [2, 128, 1])
    kv_normT = din("kv_normT", [128, 8]); w_kv = din("w_kv", [1024, 2048])
    lk1B = din("lk1B", [128, 64]); lk2B = din("lk2B", [128, 64])
    b_normT = din("b_normT", [2, 128, 8]); b_w_q = din("b_w_q", [2, 1024, 1024]); b_w_o = din("b_w_o", [2, 1024, 1024])
    lq1B = din("lq1B", [2, 128, 64]); lq2B = din("lq2B", [2, 128, 64]); b_sublnB = din("b_sublnB", [2, 128, 128])
    ffn_normT = din("ffn_normT", [4, 128, 8]); ffn_w_up = din("ffn_w_up", [4, 1024, 5632]); ffn_convT = din("ffn_convT", [4, 128, 44, 3])
    ffn_w_down = din("ffn_w_down", [4, 2816, 1024]); final_normT = din("final_normT", [128, 8])
    yT = nc.dram_tensor("yT", [1024, 4096], F32, kind="ExternalOutput").ap()

    HA = scr("HA", [1024, LP], out=debug); HB = scr("HB", [1024, LP], out=debug)
    QT = scr("QT", [1024, LP]); KT = scr("KT", [1024, LP]); VT = scr("VT", [1024, LP]); ZT = scr("ZT", [1024, LP])
    BT = scr("BT", [LP, 8]); GT = scr("GT", [LP, 8])
    KA = scr("KA", [16, 97, LP], BF16); QA = scr("QA", [16, 97, LP], BF16)
    VA = scr("VA", [LP, 8, 129], BF16); OG = scr("OG", [1024, LP], BF16)

    with ExitStack() as es:
        P = Prog(nc, es)
        C = consts(P)
        gdn_consts(P, C)
        attn_consts(P, C)

        def run():
            src = ht0
            for l in range(2):
                gdn_proj_phase(P, C, a_w_in[l], a_normT[l], a_convT[l], a_logB[l], a_dtbB[l], src, QT, KT, VT, ZT, BT, GT, LP)
                gdn_scan_phase(P, C, a_w_o[l], a_onormB[l], QT, KT, VT, ZT, BT, GT, src, HA, LP)
                if stop == f"mix{l}":
                    return
                ffn_phase(P, C, l, ffn_w_up[l], ffn_w_down[l], ffn_normT[l], ffn_convT[l], HA, HB, LP)
                if stop == f"ffn{l}":
                    return
                src = HB
            attn_proj_phase(P, C, HB, kv_normT, w_kv, True, KA, VA, LP)
            for l in (2, 3):
                j = l - 2
                lam_init = 0.8 - 0.6 * math.exp(-0.3 * l)
                attn_proj_phase(P, C, HB, b_normT[j], b_w_q[j], False, QA, None, LP)
                attn_core_phase(P, C, QA, KA, VA, OG, (lq1B[j], lk1B, lq2B[j], lk2B), lam_init, b_sublnB[j], LP)
                outproj_phase(P, C, b_w_o[j], OG, HB, HA, LP)
                if stop == f"mix{l}":
                    return
                ffn_phase(P, C, l, ffn_w_up[l], ffn_w_down[l], ffn_normT[l], ffn_convT[l], HA, HB, LP)
                if stop == f"ffn{l}":
                    return
            final_phase(P, C, HB, final_normT, yT, LP, N_META, N_META + 4096)

        run()
        info = dict(ninst=P.ninst, nsem=P.nsem)
        P.finish()
    return nc, info


def prep_shared(inputs):
    f = lambda a: np.ascontiguousarray(np.asarray(a, dtype=np.float32))

    def pc(v):
        v = np.asarray(v, dtype=np.float32)
        return np.ascontiguousarray(v.reshape(-1, 128).T)

    def bcast(v, n=128):
        v = np.asarray(v, dtype=np.float32)
        return np.ascontiguousarray(np.broadcast_to(v[None, :], (n, v.shape[0])))

    sh = {}
    sh["a_w_in"] = f(inputs["a_w_in"]); sh["a_w_o"] = f(inputs["a_w_o"])
    sh["a_normT"] = np.stack([pc(inputs["a_norm"][i]) for i in range(2)])
    sh["a_convT"] = np.stack([np.ascontiguousarray(np.asarray(inputs["a_conv"][i], np.float32).T.reshape(24, 128, 4).transpose(1, 0, 2)) for i in range(2)])
    sh["a_logB"] = np.stack([bcast(inputs["a_log"][i]) for i in range(2)])
    sh["a_dtbB"] = np.stack([bcast(inputs["a_dt_bias"][i]) for i in range(2)])
    sh["a_onormB"] = np.stack([f(inputs["a_onorm"][i]).reshape(128, 1) for i in range(2)])
    sh["kv_normT"] = pc(inputs["kv_norm"]); sh["w_kv"] = f(inputs["w_kv"])
    sh["lk1B"] = bcast(inputs["lambda_k1"]); sh["lk2B"] = bcast(inputs["lambda_k2"])
    sh["b_normT"] = np.stack([pc(inputs["b_norm"][i]) for i in range(2)])
    sh["b_w_q"] = f(inputs["b_w_q"]); sh["b_w_o"] = f(inputs["b_w_o"])
    sh["lq1B"] = np.stack([bcast(inputs["b_lambda_q1"][i]) for i in range(2)])
    sh["lq2B"] = np.stack([bcast(inputs["b_lambda_q2"][i]) for i in range(2)])
    sh["b_sublnB"] = np.stack([bcast(inputs["b_subln"][i]) for i in range(2)])
    sh["ffn_normT"] = np.stack([pc(inputs["ffn_norm"][i]) for i in range(4)])
    sh["ffn_w_up"] = f(inputs["ffn_w_up"])
    sh["ffn_convT"] = np.stack([np.ascontiguousarray(np.asarray(inputs["ffn_conv"][i], np.float32).T.reshape(44, 128, 3).transpose(1, 0, 2)) for i in range(4)])
    sh["ffn_w_down"] = f(inputs["ffn_w_down"]); sh["final_normT"] = pc(inputs["final_norm"])
    return sh


def make_ht0(x_b, meta, LP=LP_FULL):
    ht = np.zeros((1024, LP), np.float32)
    ht[:, 0:N_META] = np.asarray(meta, np.float32).T
    n = min(LP - N_META, x_b.shape[0])
    ht[:, N_META:N_META + n] = np.asarray(x_b[:n], np.float32).T
    return ht


_CACHE = {}


def kernel(**inputs):
    x = np.asarray(inputs["x"], dtype=np.float32)
    B = x.shape[0]
    if "nc" not in _CACHE:
        _CACHE["nc"] = build_program()[0]
    nc = _CACHE["nc"]
    sh = prep_shared(inputs)
    in_maps = []
    for b in range(B):
        m = dict(sh)
        m["ht0"] = make_ht0(x[b], inputs["meta_tokens"])
        in_maps.append(m)
    res = run_bass_kernel_spmd(nc, in_maps, core_ids=list(range(B)))
    out = np.empty((B, 4096, 1024), np.float32)
    for b in range(B):
        out[b] = res.results[b]["yT"].T
    return out
```

```python
import math
import numpy as np
from contextlib import ExitStack
import concourse.bass as bass
import concourse.mybir as mybir
from concourse.bass_utils import run_bass_kernel_spmd

F32 = mybir.dt.float32
BF16 = mybir.dt.bfloat16
U8 = mybir.dt.uint8
AF = mybir.ActivationFunctionType
ALU = mybir.AluOpType
AX = mybir.AxisListType

EPOCH = 16000
NDMASEM = 8
DT_SIZE = {F32: 4, BF16: 2, U8: 1}


class Reg:
    __slots__ = ("name", "w", "r")

    def __init__(self, name=""):
        self.name = name
        self.w = None
        self.r = {}


class Prog:
    CE = ("pe", "act", "dve", "pool")
    ALLE = ("pe", "act", "dve", "pool", "sp")

    def __init__(self, nc, es, same_engine_sync=True):
        self.nc = nc
        self.es = es
        self.streams = {k: [] for k in self.ALLE}
        self.cnt = {k: 0 for k in self.CE}
        self.sems = {k: [] for k in self.CE}
        self.waited = {k: {} for k in self.ALLE}
        self.dma_cnt = {k: 0 for k in self.ALLE}
        self.dma_sems = {}
        self.same_engine_sync = same_engine_sync
        self.nsem = 0
        self.ninst = 0
        self.sb_total = 212800
        self.sbuf = es.enter_context(nc.sbuf_tensor("arena", [128, self.sb_total], U8))
        self.psum = es.enter_context(nc.psum_tensor("psum", [128, 4096], F32))
        self.sb_persist = 0
        self.sb_off = 0
        self.bank_regs = [Reg(f"bank{i}") for i in range(8)]

    def alloc(self, shape, dtype, persist=False):
        n = int(np.prod(shape[1:])) * DT_SIZE[dtype]
        n = (n + 31) // 32 * 32
        off = self.sb_off
        assert off + n <= self.sb_total, f"SBUF arena overflow {off}+{n}"
        self.sb_off = off + n
        if persist:
            assert self.sb_persist == off, "persistent allocs must come first"
            self.sb_persist = self.sb_off
        ap = self.sbuf[0:shape[0], off:off + n].bitcast(dtype)
        fs = int(np.prod(shape[1:]))
        ap = ap[:, 0:fs]
        if len(shape) == 3:
            ap = ap.rearrange("p (a b) -> p a b", a=shape[1])
        elif len(shape) == 4:
            ap = ap.rearrange("p (a b c) -> p a b c", a=shape[1], b=shape[2])
        return ap

    def reset_arena(self):
        self.sb_off = self.sb_persist

    def bank(self, i, n=512, parts=128):
        return self.psum[0:parts, i * 512:i * 512 + n]

    def _new_sem(self, name):
        self.nsem += 1
        return self.es.enter_context(self.nc.semaphore(name))

    def _sem_for(self, eng, n):
        e = (n - 1) // EPOCH
        while len(self.sems[eng]) <= e:
            self.sems[eng].append(self._new_sem(f"s_{eng}_{len(self.sems[eng])}"))
        return self.sems[eng][e], (n - 1) % EPOCH + 1

    def _dsem(self, q, slot):
        if q not in self.dma_sems:
            self.dma_sems[q] = [self._new_sem(f"d_{q}_{i}") for i in range(NDMASEM)]
        return self.dma_sems[q][slot]

    def _wait(self, eng, ev):
        if ev[0] == 'c':
            _, src, n = ev
            if src == eng and (eng == 'pe' or not self.same_engine_sync):
                return
            if src == eng and eng in ('act', 'dve') and self.cnt[eng] - n >= 6:
                return

            key = ('c', src)
            if self.waited[eng].get(key, 0) >= n:
                return
            self.waited[eng][key] = n
            sem, val = self._sem_for(src, n)
        else:
            _, q, j = ev
            slot = j % NDMASEM
            need = j // NDMASEM + 1
            key = ('d', q, slot)
            if self.waited[eng].get(key, 0) >= need:
                return
            self.waited[eng][key] = need
            sem = self._dsem(q, slot)
            val = 16 * need
        self.streams[eng].append(lambda e, sem=sem, val=val: e.wait_ge(sem, val))
        self.ninst += 1

    def _deps(self, eng, r, w, is_dma=False):
        for reg in r:
            for ev in (reg.w or ()):
                self._wait(eng, ev)
        for reg in w:
            if is_dma and reg.w and not reg.r and all(ev[0] == 'd' for ev in reg.w):
                continue
            for ev in (reg.w or ()):
                self._wait(eng, ev)
            for ev in reg.r.values():
                self._wait(eng, ev)

    @staticmethod
    def _evkey(ev):
        return (ev[0], ev[1]) if ev[0] == 'c' else (ev[0], ev[1], ev[2] % NDMASEM)

    def _record(self, ev, r, w):
        k = self._evkey(ev)
        for reg in r:
            reg.r[k] = ev
        for reg in w:
            if ev[0] == 'd' and reg.w and not reg.r and all(e2[0] == 'd' for e2 in reg.w):
                reg.w = [e2 for e2 in reg.w if self._evkey(e2) != k] + [ev]
            else:
                reg.w = [ev]
            reg.r = {}

    def op(self, eng, fn, r=(), w=()):
        self._deps(eng, r, w)
        n = self.cnt[eng] + 1
        self.cnt[eng] = n
        sem, _ = self._sem_for(eng, n)
        self.streams[eng].append(lambda e, fn=fn, sem=sem: fn(e).then_inc(sem, 1))
        self.ninst += 1
        self._record(('c', eng, n), r, w)

    def mm_group(self, mms, r_list, w):
        eng = "pe"
        self._deps(eng, (), w)
        n = self.cnt[eng] + 1
        self.cnt[eng] = n
        sem, _ = self._sem_for(eng, n)
        ev = ('c', eng, n)
        last = len(mms) - 1
        for i, fn in enumerate(mms):
            self._deps(eng, r_list[i], ())
            if i == last:
                self.streams[eng].append(lambda e, fn=fn, sem=sem: fn(e).then_inc(sem, 1))
            else:
                self.streams[eng].append(lambda e, fn=fn: fn(e))
            self.ninst += 1
            self._record(ev, r_list[i], ())
        self._record(ev, (), w)

    def dma(self, q, out, in_, r=(), w=()):
        self._deps(q, r, w, is_dma=True)
        j = self.dma_cnt[q]
        self.dma_cnt[q] = j + 1
        sem = self._dsem(q, j % NDMASEM)
        self.streams[q].append(lambda e, out=out, in_=in_, sem=sem: e.dma_start(out=out, in_=in_).then_inc(sem, 16))
        self.ninst += 1
        self._record(('d', q, j), r, w)

    def barrier(self):
        for eng in self.ALLE:
            for src in self.CE:
                if self.cnt[src] > 0 and src != eng:
                    self._wait(eng, ('c', src, self.cnt[src]))
            for q, c in self.dma_cnt.items():
                for j in range(max(0, c - NDMASEM), c):
                    self._wait(eng, ('d', q, j))

    def finish(self):
        for q, c in self.dma_cnt.items():
            for j in range(max(0, c - NDMASEM), c):
                self._wait("sp", ('d', q, j))
        for src in self.CE:
            if self.cnt[src] > 0:
                self._wait("sp", ('c', src, self.cnt[src]))
        nc = self.nc
        block = self.es.enter_context(nc.Block())
        st = self.streams

        @block.sync
        def _(e):
            for f in st["sp"]:
                f(e)

        @block.tensor
        def _(e):
            for f in st["pe"]:
                f(e)

        @block.scalar
        def _(e):
            for f in st["act"]:
                f(e)

        @block.vector
        def _(e):
            for f in st["dve"]:
                f(e)

        @block.gpsimd
        def _(e):
            for f in st["pool"]:
                f(e)


G = 384
D = 1024
DFF = 2816
NFC = 22
EPS = 1e-6


def consts(P):
    C = {}
    C["ones_f"] = P.alloc([128, 128], F32, persist=True)
    C["eps"] = P.alloc([128, 1], F32, persist=True)
    C["r"] = Reg("consts")
    P.op("pool", lambda e: e.memset(C["ones_f"], 1.0), w=[C["r"]])
    P.op("pool", lambda e: e.memset(C["eps"], EPS), w=[C["r"]])
    return C


def rms_group(P, C, hbuf, r_h, N, sqc, ssum, rs, rstd, hn, r_hn, r_tmp, bank, nchunk=8, dim=1024):
    for c in range(nchunk):
        P.op("pool", lambda e, c=c: e.tensor_tensor(out=sqc[c % 2], in0=hbuf[:, c, :], in1=hbuf[:, c, :], op=ALU.mult),
             r=[r_h], w=[r_tmp[c % 2]])
        if c == 0:
            P.op("pool", lambda e: e.tensor_copy(out=ssum, in_=sqc[0]), r=[r_tmp[0]], w=[r_tmp[2]])
        else:
            P.op("pool", lambda e, c=c: e.tensor_tensor(out=ssum, in0=ssum, in1=sqc[c % 2], op=ALU.add),
                 r=[r_tmp[c % 2]], w=[r_tmp[2]])
    br = P.bank_regs[bank]
    P.mm_group([lambda e: e.matmul(P.bank(bank, N), lhsT=C["ones_f"], rhs=ssum, start=True, stop=True)],
               [[C["r"], r_tmp[2]]], [br])
    P.op("act", lambda e: e.activation(out=rs, in_=P.bank(bank, N), func=AF.Ln, bias=C["eps"], scale=1.0 / dim),
         r=[br, C["r"]], w=[r_tmp[3]])
    P.op("act", lambda e: e.activation(out=rstd, in_=rs, func=AF.Exp, scale=-0.5), r=[r_tmp[3]], w=[r_tmp[4]])
    if hn is not None:
        P.op("dve", lambda e: e.tensor_tensor(out=hn, in0=hbuf, in1=rstd.unsqueeze(1).broadcast_to([128, nchunk, N]), op=ALU.mult),
             r=[r_h, r_tmp[4]], w=[r_hn])


def ffn_phase(P, C, layer, w_up, w_down, normT, convT, src, dst, LP):
    P.reset_arena()
    ng = LP // G
    N = G + 2
    wup = P.alloc([128, 8, 2 * DFF], BF16)
    wdn = P.alloc([128, NFC, D], BF16)
    r_wup = [Reg() for _ in range(16)]
    r_wdn = [Reg() for _ in range(11)]
    gain = P.alloc([128, 8], F32)
    cw = P.alloc([128, 44, 3], F32)
    r_small = Reg()
    SW = 1408
    stage = [P.alloc([128, SW], F32) for _ in range(2)]
    r_stage = [Reg(), Reg()]
    hbuf = [P.alloc([128, 8, N], F32) for _ in range(2)]
    r_hbuf = [Reg(), Reg()]
    hn = P.alloc([128, 8, N], BF16)
    r_hn = Reg()
    act = P.alloc([128, NFC, G], BF16)
    r_act = [Reg() for _ in range(NFC)]
    tg = [P.alloc([128, G], F32) for _ in range(2)]
    tu = [P.alloc([128, G], F32) for _ in range(2)]
    r_tg = [Reg(), Reg()]
    r_tu = [Reg(), Reg()]
    sqc = [P.alloc([128, N], F32) for _ in range(2)]
    ssum = P.alloc([128, N], F32)
    rs = P.alloc([128, N], F32)
    rstd = P.alloc([128, N], F32)
    r_tmp = [Reg() for _ in range(5)]

    P.dma("sp", gain, normT, w=[r_small])
    P.dma("sp", cw, convT, w=[r_small])
    stage4 = stage + [hb.rearrange("p c t -> p (c t)")[:, 0:SW] for hb in hbuf]
    r_stage4 = r_stage + r_hbuf
    load_weight_bf16(P, w_up, 8, 2 * DFF, gain, r_small, stage4, r_stage4, wup, lambda kc, c0: r_wup[kc * 2 + c0 // DFF], SW)
    load_weight_bf16(P, w_down, NFC, D, None, None, [s_[:, 0:D] for s_ in stage4], r_stage4, wdn, lambda kc, c0: r_wdn[kc // 2], D)

    tg = tg + [stage[0][:, 0:G]]
    tu = tu + [stage[1][:, 0:G]]
    r_tg = r_tg + [r_stage[0]]
    r_tu = r_tu + [r_stage[1]]

    srcv = src.rearrange("(c p) t -> p c t", p=128)
    dstv = dst.rearrange("(c p) t -> p c t", p=128)

    def load(g):
        b = g % 2
        n0 = g * G
        if g == 0:
            P.op("pool", lambda e: e.memset(hbuf[b][:, :, 0:2], 0.0), w=[r_hbuf[b]])
            P.dma("sp", hbuf[b][:, :, 2:N], srcv[:, :, 0:G], w=[r_hbuf[b]])
        else:
            P.dma("sp", hbuf[b], srcv[:, :, n0 - 2:n0 + G], w=[r_hbuf[b]])

    load(0)
    rms_group(P, C, hbuf[0], r_hbuf[0], N, sqc, ssum, rs, rstd, hn, r_hn, r_tmp, bank=7)
    for g in range(ng):
        b = g % 2
        n0 = g * G
        if g + 1 < ng:
            load(g + 1)
        for j in range(NFC):
            pb = (j % 3) * 2
            for half, (tt, r_tt) in enumerate(((tg, r_tg), (tu, r_tu))):
                bk = pb + half
                col = half * DFF + j * 128
                ch = half * NFC + j
                P.mm_group(
                    [lambda e, kc=kc, bk=bk, col=col: e.matmul(P.bank(bk, N), lhsT=wup[:, kc, col:col + 128], rhs=hn[:, kc, :],
                                                                start=(kc == 0), stop=(kc == 7)) for kc in range(8)],
                    [[r_wup[kc * 2 + (col // (2 * SW))], r_hn] for kc in range(8)], [P.bank_regs[bk]])
                t = tt[j % 3]
                rt = r_tt[j % 3]
                P.op("act", lambda e, bk=bk, t=t, ch=ch: e.activation(out=t, in_=P.bank(bk, N)[:, 0:G], func=AF.Copy, scale=cw[:, ch, 0:1]),
                     r=[P.bank_regs[bk], r_small], w=[rt])
                P.op("dve", lambda e, bk=bk, t=t, ch=ch: e.scalar_tensor_tensor(out=t, in0=P.bank(bk, N)[:, 1:G + 1], scalar=cw[:, ch, 1:2], in1=t,
                                                                                   op0=ALU.mult, op1=ALU.add),
                     r=[P.bank_regs[bk], r_small, rt], w=[rt])
                P.op("dve", lambda e, bk=bk, t=t, ch=ch: e.scalar_tensor_tensor(out=t, in0=P.bank(bk, N)[:, 2:G + 2], scalar=cw[:, ch, 2:3], in1=t,
                                                                                   op0=ALU.mult, op1=ALU.add),
                     r=[P.bank_regs[bk], r_small, rt], w=[rt])
            tgj, tuj = tg[j % 3], tu[j % 3]
            P.op("act", lambda e, tgj=tgj: e.activation(out=tgj, in_=tgj, func=AF.Silu), r=[r_tg[j % 3]], w=[r_tg[j % 3]])
            P.op("dve", lambda e, tgj=tgj, tuj=tuj, j=j: e.tensor_tensor(out=act[:, j, :], in0=tgj, in1=tuj, op=ALU.mult),
                 r=[r_tg[j % 3], r_tu[j % 3]], w=[r_act[j]])
        if g + 1 < ng:
            rms_group(P, C, hbuf[1 - b], r_hbuf[1 - b], N, sqc, ssum, rs, rstd, hn, r_hn, r_tmp, bank=7)
        for oc in range(8):
            bk = 6 + oc % 2
            P.mm_group(
                [lambda e, j=j, bk=bk, oc=oc: e.matmul(P.bank(bk, G), lhsT=wdn[:, j, oc * 128:(oc + 1) * 128], rhs=act[:, j, :],
                                                        start=(j == 0), stop=(j == NFC - 1)) for j in range(NFC)],
                [[r_wdn[j // 2], r_act[j]] for j in range(NFC)], [P.bank_regs[bk]])
            P.op("dve", lambda e, bk=bk, oc=oc, b=b: e.tensor_tensor(out=hbuf[b][:, oc, 2:N], in0=P.bank(bk, G), in1=hbuf[b][:, oc, 2:N], op=ALU.add),
                 r=[P.bank_regs[bk]], w=[r_hbuf[b]])
        P.dma("sp", dstv[:, :, n0:n0 + G], hbuf[b][:, :, 2:N], r=[r_hbuf[b]])
    P.barrier()


NWIN = 4112


def gdn_consts(P, C):
    r = C["r"]
    for name in ("triu", "mus", "negu", "ident"):
        C[name] = P.alloc([128, 128], F32, persist=True)
    P.op("pool", lambda e: e.memset(C["triu"], 1.0), w=[r])
    P.op("pool", lambda e: e.affine_select(out=C["triu"], in_=C["triu"], pattern=[[1, 128]], compare_op=ALU.is_ge, fill=0.0, base=0, channel_multiplier=-1), w=[r])
    P.op("pool", lambda e: e.memset(C["mus"], 1.0), w=[r])
    P.op("pool", lambda e: e.affine_select(out=C["mus"], in_=C["mus"], pattern=[[1, 128]], compare_op=ALU.is_gt, fill=0.0, base=0, channel_multiplier=-1), w=[r])
    P.op("pool", lambda e: e.memset(C["negu"], 0.0), w=[r])
    P.op("pool", lambda e: e.affine_select(out=C["negu"], in_=C["negu"], pattern=[[1, 128]], compare_op=ALU.is_ge, fill=-30000.0, base=0, channel_multiplier=-1), w=[r])
    P.op("pool", lambda e: e.memset(C["ident"], 1.0), w=[r])
    P.op("pool", lambda e: e.affine_select(out=C["ident"], in_=C["ident"], pattern=[[-1, 128]], compare_op=ALU.is_equal, fill=0.0, base=0, channel_multiplier=1), w=[r])


def gdn_proj_phase(P, C, w_in, normT, convT, alogB, dtbB, src, QT, KT, VT, ZT, BT, GT, LP):
    P.reset_arena()
    ng = LP // G
    N = G + 3
    wb = P.alloc([128, 8, NWIN], BF16)
    r_wb = [Reg() for _ in range(8)]
    gain = P.alloc([128, 8], F32)
    cw = P.alloc([128, 24, 4], F32)
    nega = P.alloc([128, 8], F32)
    dtb = P.alloc([128, 8], F32)
    r_small = Reg()
    SW = 2056
    stage = [P.alloc([128, SW], F32) for _ in range(3)]
    r_stage = [Reg() for _ in range(3)]
    hbuf = [P.alloc([128, 8, N], F32) for _ in range(2)]
    r_hbuf = [Reg(), Reg()]
    hn = P.alloc([128, 8, N], BF16)
    r_hn = Reg()
    sqc = [P.alloc([128, N], F32) for _ in range(2)]
    ssum = P.alloc([128, N], F32)
    rs = P.alloc([128, N], F32)
    rstd = P.alloc([128, N], F32)
    r_tmp = [Reg() for _ in range(5)]
    qkb = P.alloc([128, 16, G], F32)
    r_qk = [Reg() for _ in range(16)]
    NT = 3
    tb = [P.alloc([128, G], F32) for _ in range(NT)]
    r_tb = [Reg() for _ in range(NT)]
    sq2 = [P.alloc([128, G], F32) for _ in range(2)]
    r_sq2 = [Reg(), Reg()]
    rr = [P.alloc([128, G], F32) for _ in range(2)]
    r_rr = [Reg(), Reg()]
    sm = [P.alloc([128, 8, 8], F32) for _ in range(2)]
    r_sm = [Reg(), Reg()]
    cr = C["r"]

    P.dma("sp", gain, normT, w=[r_small])
    P.dma("sp", cw, convT, w=[r_small])
    P.dma("sp", nega, alogB, w=[r_small])
    P.dma("sp", dtb, dtbB, w=[r_small])
    P.op("act", lambda e: e.activation(out=nega, in_=nega, func=AF.Exp), r=[r_small], w=[r_small])
    P.op("dve", lambda e: e.tensor_scalar(out=nega, in0=nega, scalar1=-1.0, scalar2=None, op0=ALU.mult), r=[r_small], w=[r_small])
    load_weight_bf16(P, w_in, 8, NWIN, gain, r_small, stage, r_stage, wb, lambda kc, c0: r_wb[kc], SW)

    srcv = src.rearrange("(c p) t -> p c t", p=128)

    def load(g):
        b = g % 2
        n0 = g * G
        if g == 0:
            P.op("pool", lambda e: e.memset(hbuf[b][:, :, 0:3], 0.0), w=[r_hbuf[b]])
            P.dma("sp", hbuf[b][:, :, 3:N], srcv[:, :, 0:G], w=[r_hbuf[b]])
        else:
            P.dma("sp", hbuf[b], srcv[:, :, n0 - 3:n0 + G], w=[r_hbuf[b]])

    def rms(g):
        rms_group(P, C, hbuf[g % 2], r_hbuf[g % 2], N, sqc, ssum, rs, rstd, hn, r_hn, r_tmp, bank=7)

    load(0)
    rms(0)
    ti = 0
    for g in range(ng):
        n0 = g * G
        if g + 1 < ng:
            load(g + 1)
        for ch in range(24):
            bk = ch % 3
            col = ch * 128
            P.mm_group(
                [lambda e, kc=kc, bk=bk, col=col: e.matmul(P.bank(bk, N), lhsT=wb[:, kc, col:col + 128], rhs=hn[:, kc, :],
                                                            start=(kc == 0), stop=(kc == 7)) for kc in range(8)],
                [[r_wb[kc], r_hn] for kc in range(8)], [P.bank_regs[bk]])
            if ch < 16:
                t = qkb[:, ch, :]
                rt = r_qk[ch]
            else:
                t = tb[ti % NT]
                rt = r_tb[ti % NT]
                ti += 1
            P.op("act", lambda e, bk=bk, t=t, ch=ch: e.activation(out=t, in_=P.bank(bk, N)[:, 0:G], func=AF.Copy, scale=cw[:, ch, 0:1]),
                 r=[P.bank_regs[bk], r_small], w=[rt])
            for k in (1, 2, 3):
                P.op("dve", lambda e, bk=bk, t=t, ch=ch, k=k: e.scalar_tensor_tensor(
                    out=t, in0=P.bank(bk, N)[:, k:G + k], scalar=cw[:, ch, k:k + 1], in1=t, op0=ALU.mult, op1=ALU.add),
                    r=[P.bank_regs[bk], r_small, rt], w=[rt])
            P.op("act", lambda e, t=t: e.activation(out=t, in_=t, func=AF.Silu), r=[rt], w=[rt])
            if ch >= 16:
                hh = ch - 16
                P.dma("sp", VT[hh * 128:(hh + 1) * 128, n0:n0 + G], t, r=[rt])
        for hh in range(8):
            bk = hh % 3
            col = 3072 + hh * 128
            P.mm_group(
                [lambda e, kc=kc, bk=bk, col=col: e.matmul(P.bank(bk, G), lhsT=wb[:, kc, col:col + 128], rhs=hn[:, kc, 3:N],
                                                            start=(kc == 0), stop=(kc == 7)) for kc in range(8)],
                [[r_wb[kc], r_hn] for kc in range(8)], [P.bank_regs[bk]])
            t = tb[ti % NT]
            rt = r_tb[ti % NT]
            ti += 1
            P.op("act", lambda e, bk=bk, t=t: e.activation(out=t, in_=P.bank(bk, G), func=AF.Silu), r=[P.bank_regs[bk]], w=[rt])
            P.dma("sp", ZT[hh * 128:(hh + 1) * 128, n0:n0 + G], t, r=[rt])
        for i in range(G // 128):
            bk = 5 + i % 2
            c0 = 3 + i * 128
            P.mm_group(
                [lambda e, kc=kc, bk=bk, c0=c0: e.matmul(P.bank(bk, 16), lhsT=hn[:, kc, c0:c0 + 128], rhs=wb[:, kc, 4096:4112],
                                                          start=(kc == 0), stop=(kc == 7)) for kc in range(8)],
                [[r_wb[kc], r_hn] for kc in range(8)], [P.bank_regs[bk]])
            s_ = sm[i % 2]
            rs_ = r_sm[i % 2]
            ps = P.bank(bk, 16)
            P.op("act", lambda e, ps=ps, s_=s_: e.activation(out=s_[:, 0, :], in_=ps[:, 0:8], func=AF.Exp, scale=-1.0), r=[P.bank_regs[bk]], w=[rs_])
            P.op("dve", lambda e, s_=s_: e.tensor_scalar(out=s_[:, 0, :], in0=s_[:, 0, :], scalar1=1.0, scalar2=None, op0=ALU.add), r=[rs_], w=[rs_])
            P.op("dve", lambda e, s_=s_: e.reciprocal(out=s_[:, 0, :], in_=s_[:, 0, :]), r=[rs_], w=[rs_])
            P.op("dve", lambda e, ps=ps, s_=s_: e.tensor_tensor(out=s_[:, 1, :], in0=ps[:, 8:16], in1=dtb, op=ALU.add), r=[P.bank_regs[bk], r_small], w=[rs_])
            P.op("dve", lambda e, s_=s_: e.tensor_scalar(out=s_[:, 2, :], in0=s_[:, 1, :], scalar1=-1.0, scalar2=None, op0=ALU.mult), r=[rs_], w=[rs_])
            P.op("dve", lambda e, s_=s_: e.tensor_tensor(out=s_[:, 2, :], in0=s_[:, 2, :], in1=s_[:, 1, :], op=ALU.max), r=[rs_], w=[rs_])
            P.op("act", lambda e, s_=s_: e.activation(out=s_[:, 3, :], in_=s_[:, 2, :], func=AF.Exp, scale=-1.0), r=[rs_], w=[rs_])
            P.op("act", lambda e, s_=s_: e.activation(out=s_[:, 3, :], in_=s_[:, 3, :], func=AF.Ln, bias=C["ones_f"][:, 0:1], scale=1.0), r=[rs_, cr], w=[rs_])
            P.op("dve", lambda e, s_=s_: e.tensor_scalar(out=s_[:, 4, :], in0=s_[:, 1, :], scalar1=0.0, scalar2=None, op0=ALU.max), r=[rs_], w=[rs_])
            P.op("dve", lambda e, s_=s_: e.tensor_tensor(out=s_[:, 4, :], in0=s_[:, 4, :], in1=s_[:, 3, :], op=ALU.add), r=[rs_], w=[rs_])
            P.op("dve", lambda e, s_=s_: e.tensor_tensor(out=s_[:, 5, :], in0=s_[:, 4, :], in1=nega, op=ALU.mult), r=[rs_, r_small], w=[rs_])
            r0 = n0 + i * 128
            P.dma("sp", BT[r0:r0 + 128, :], s_[:, 0, :], r=[rs_])
            P.dma("sp", GT[r0:r0 + 128, :], s_[:, 5, :], r=[rs_])
        if g + 1 < ng:
            rms(g + 1)
        for ch in range(16):
            t = qkb[:, ch, :]
            rt = r_qk[ch]
            s2 = sq2[ch % 2]
            P.op("pool", lambda e, t=t, s2=s2: e.tensor_tensor(out=s2, in0=t, in1=t, op=ALU.mult), r=[rt], w=[r_sq2[ch % 2]])
            bk2 = 3 + ch % 2
            P.mm_group([lambda e, bk2=bk2, s2=s2: e.matmul(P.bank(bk2, G), lhsT=C["ones_f"], rhs=s2, start=True, stop=True)],
                       [[cr, r_sq2[ch % 2]]], [P.bank_regs[bk2]])
            r2 = rr[ch % 2]
            P.op("act", lambda e, bk2=bk2, r2=r2: e.activation(out=r2, in_=P.bank(bk2, G), func=AF.Ln, bias=C["eps"], scale=1.0),
                 r=[P.bank_regs[bk2], cr], w=[r_rr[ch % 2]])
            P.op("act", lambda e, r2=r2: e.activation(out=r2, in_=r2, func=AF.Exp, scale=-0.5), r=[r_rr[ch % 2]], w=[r_rr[ch % 2]])
            sc = (128.0 ** -0.5) if ch < 8 else 1.0
            P.op("dve", lambda e, t=t, r2=r2, sc=sc: e.scalar_tensor_tensor(out=t, in0=t, scalar=sc, in1=r2, op0=ALU.mult, op1=ALU.mult),
                 r=[rt, r_rr[ch % 2]], w=[rt])
            dstT = (QT, KT)[ch // 8]
            hh = ch % 8
            P.dma("sp", dstT[hh * 128:(hh + 1) * 128, n0:n0 + G], t, r=[rt])
    P.barrier()


def gdn_scan_phase(P, C, w_o, onormB, QT, KT, VT, ZT, BT, GT, src, dst, LP):
    P.reset_arena()
    nt = LP // 128
    cr = C["r"]
    wo = P.alloc([128, 8, 1024], BF16)
    r_wo = Reg()
    ogain = P.alloc([128, 1], F32)
    identb = P.alloc([128, 128], BF16)
    r_small = Reg()
    stage = [P.alloc([128, 1024], F32) for _ in range(2)]
    r_stage = [Reg(), Reg()]
    P.dma("sp", ogain, onormB, w=[r_small])
    P.op("pool", lambda e: e.tensor_copy(out=identb, in_=C["ident"]), r=[cr], w=[r_small])
    load_weight_bf16(P, w_o, 8, 1024, None, None, stage, r_stage, wo, r_wo, 1024)

    def big(dt=F32):
        return P.alloc([128, 8, 128], dt)

    IN = [dict(k=big(), q=big(), v=big(), z=big(), h=big(), b=P.alloc([128, 8], F32), g=P.alloc([128, 8], F32), r=Reg()) for _ in range(2)]
    S = big(); r_S = Reg()
    f32names = ["Dg", "DT", "DTs", "egc", "Y", "u", "vtmp", "sq", "og", "P0", "Pt0", "P1", "Pt1"]
    bfnames = ["Ybf", "Sbf", "kbf", "qbf", "kg", "kdec", "vtm", "wT", "qg", "IT", "vnew"]
    T = {n: big() for n in f32names}
    T.update({n: big(BF16) for n in bfnames})
    R = {n: Reg(n) for n in f32names + bfnames}
    ogb = P.alloc([128, 8, 128], BF16); r_ogb = Reg()
    sm = {n: P.alloc([128, 8], F32) for n in ("gc", "egc", "gl", "egl", "fd")}
    r_sm = {n: Reg() for n in sm}
    rs = P.alloc([128, 8, 128], F32); r_rs = Reg()

    P.op("pool", lambda e: e.memset(S, 0.0), w=[r_S])
    P.op("pool", lambda e: e.memset(T["Sbf"], 0.0), w=[R["Sbf"]])

    def v3(dram):
        return dram.rearrange("(h d) t -> d h t", d=128)

    def load(t):
        I = IN[t % 2]
        c0 = t * 128
        for key, dr in (("k", KT), ("q", QT), ("v", VT), ("z", ZT), ("h", src)):
            P.dma("sp", I[key], v3(dr)[:, :, c0:c0 + 128], w=[I["r"]])
        P.dma("sp", I["b"], BT[c0:c0 + 128, :], w=[I["r"]])
        P.dma("sp", I["g"], GT[c0:c0 + 128, :], w=[I["r"]])

    slot_i = [0]

    def slot():
        s = slot_i[0] % 3
        slot_i[0] += 1
        ap = P.psum[:, s * 1024:(s + 1) * 1024].rearrange("p (h c) -> p h c", h=8)
        return ap, [P.bank_regs[2 * s], P.bank_regs[2 * s + 1]], s

    def bc_row(m):
        return m.unsqueeze(1).broadcast_to([128, 8, 128])

    def bc_col(x):
        return x.unsqueeze(2).broadcast_to([128, 8, 128])

    def mm8(ps, regs, lhs, rhs, rl, rr_, transpose=None):
        fns = []
        rls = []
        for h in range(8):
            if transpose is not None:
                fn = (lambda e, h=h: e.transpose(ps[:, h, :], lhs[:, h, :], transpose))
            else:
                fn = (lambda e, h=h: e.matmul(ps[:, h, :], lhsT=lhs[:, h, :], rhs=rhs[:, h, :], start=True, stop=True))
            fns.append(fn)
            rls.append(list(rl) + list(rr_))
        P.mm_group(fns, rls, regs)

    load(0)
    for t in range(nt):
        I = IN[t % 2]
        rI = I["r"]
        c0 = t * 128
        if t + 1 < nt:
            load(t + 1)
        P.op("pool", lambda e, I=I: e.tensor_copy(out=T["kbf"], in_=I["k"]), r=[rI], w=[R["kbf"]])
        P.op("pool", lambda e, I=I: e.tensor_copy(out=T["qbf"], in_=I["q"]), r=[rI], w=[R["qbf"]])
        P.op("dve", lambda e, I=I: e.tensor_tensor(out=T["Dg"], in0=bc_row(C["triu"]), in1=bc_col(I["g"]), op=ALU.mult), r=[cr, rI], w=[R["Dg"]])
        gcB, rg_gcB, sg = slot()
        gcBf = P.psum[:, sg * 1024:(sg + 1) * 1024]
        Dgf = T["Dg"].rearrange("p h c -> p (h c)")
        P.mm_group([lambda e: e.matmul(gcBf[:, 0:512], lhsT=C["ones_f"], rhs=Dgf[:, 0:512], start=True, stop=True),
                    lambda e: e.matmul(gcBf[:, 512:1024], lhsT=C["ones_f"], rhs=Dgf[:, 512:1024], start=True, stop=True)],
                   [[cr, R["Dg"]], [cr, R["Dg"]]], rg_gcB)
        P.mm_group([lambda e, I=I: e.matmul(P.bank(6, 8), lhsT=C["triu"], rhs=I["g"], start=True, stop=True)], [[cr, rI]], [P.bank_regs[6]])
        P.op("dve", lambda e: e.tensor_copy(out=sm["gc"], in_=P.bank(6, 8)), r=[P.bank_regs[6]], w=[r_sm["gc"]])
        P.op("dve", lambda e: e.tensor_copy(out=sm["gl"], in_=gcB[:, :, 127]), r=rg_gcB, w=[r_sm["gl"]])
        P.op("act", lambda e: e.activation(out=sm["egc"], in_=sm["gc"], func=AF.Exp), r=[r_sm["gc"]], w=[r_sm["egc"]])
        P.op("act", lambda e: e.activation(out=sm["egl"], in_=sm["gl"], func=AF.Exp), r=[r_sm["gl"]], w=[r_sm["egl"]])
        P.op("dve", lambda e: e.tensor_tensor(out=sm["fd"], in0=sm["gl"], in1=sm["gc"], op=ALU.subtract), r=[r_sm["gl"], r_sm["gc"]], w=[r_sm["fd"]])
        P.op("act", lambda e: e.activation(out=sm["fd"], in_=sm["fd"], func=AF.Exp), r=[r_sm["fd"]], w=[r_sm["fd"]])
        P.op("dve", lambda e: e.tensor_tensor(out=T["DT"], in0=gcB, in1=bc_col(sm["gc"]), op=ALU.subtract), r=rg_gcB + [r_sm["gc"]], w=[R["DT"]])
        P.op("dve", lambda e: e.tensor_tensor(out=T["DT"], in0=T["DT"], in1=bc_row(C["negu"]), op=ALU.add), r=[cr, R["DT"]], w=[R["DT"]])
        P.op("act", lambda e: e.activation(out=T["DT"], in_=T["DT"], func=AF.Exp), r=[R["DT"]], w=[R["DT"]])
        P.op("act", lambda e: e.activation(out=T["egc"], in_=gcB, func=AF.Exp), r=rg_gcB, w=[R["egc"]])
        P.op("dve", lambda e: e.tensor_tensor(out=T["DTs"], in0=T["DT"], in1=bc_row(C["mus"]), op=ALU.mult), r=[cr, R["DT"]], w=[R["DTs"]])
        KK, rg_KK, _ = slot()
        mm8(KK, rg_KK, T["kbf"], T["kbf"], [R["kbf"]], [])
        ITp, rg_IT, _ = slot()
        mm8(ITp, rg_IT, T["kbf"], T["qbf"], [R["kbf"]], [R["qbf"]])
        for h in range(8):
            P.op("dve", lambda e, h=h, I=I: e.scalar_tensor_tensor(out=T["P0"][:, h, :], in0=KK[:, h, :], scalar=I["b"][:, h:h + 1], in1=T["DTs"][:, h, :],
                                                                      op0=ALU.mult, op1=ALU.mult), r=rg_KK + [rI, R["DTs"]], w=[R["P0"]])
        P.op("dve", lambda e: e.tensor_tensor(out=T["IT"], in0=ITp, in1=T["DT"], op=ALU.mult), r=rg_IT + [R["DT"]], w=[R["IT"]])
        P.op("pool", lambda e, I=I: e.tensor_tensor(out=T["qg"], in0=I["q"], in1=T["egc"], op=ALU.mult), r=[rI, R["egc"]], w=[R["qg"]])
        psb, rg, st_ = slot()
        mm8(psb, rg, T["P0"], None, [R["P0"], cr], [], transpose=C["ident"])
        P.op("act", lambda e, psb=psb: e.activation(out=T["Pt0"], in_=psb, func=AF.Copy), r=rg, w=[R["Pt0"]])
        P.op("dve", lambda e: e.tensor_tensor(out=T["Y"], in0=bc_row(C["ident"]), in1=T["P0"], op=ALU.subtract), r=[cr, R["P0"]], w=[R["Y"]])
        cur = 0
        for k in range(1, 7):
            Pc, Ptc = T[f"P{cur}"], T[f"Pt{cur}"]
            rPc, rPtc = R[f"P{cur}"], R[f"Pt{cur}"]
            nx = 1 - cur
            Pn, Ptn = T[f"P{nx}"], T[f"Pt{nx}"]
            rPn, rPtn = R[f"P{nx}"], R[f"Pt{nx}"]
            ps2, rg2, _ = slot()
            mm8(ps2, rg2, Pc, Ptc, [rPc], [rPtc])
            if k < 6:
                ps1, rg1, _ = slot()
                mm8(ps1, rg1, Ptc, Pc, [rPtc], [rPc])
            P.op("dve", lambda e, ps2=ps2, Ptn=Ptn: e.tensor_copy(out=Ptn, in_=ps2), r=rg2, w=[rPtn])
            if k < 6:
                P.op("act", lambda e, ps1=ps1, Pn=Pn: e.activation(out=Pn, in_=ps1, func=AF.Copy), r=rg1, w=[rPn])
            ps3, rg3, _ = slot()
            mm8(ps3, rg3, Ptn, T["Y"], [rPtn], [R["Y"]])
            P.op("dve", lambda e, ps3=ps3: e.tensor_tensor(out=T["Y"], in0=ps3, in1=T["Y"], op=ALU.add), r=rg3, w=[R["Y"]])
            cur = nx
        P.op("act", lambda e: e.activation(out=T["Ybf"], in_=T["Y"], func=AF.Copy), r=[R["Y"]], w=[R["Ybf"]])
        ps, rg, _ = slot()
        mm8(ps, rg, I["k"], None, [rI, cr], [], transpose=C["ident"])
        P.op("dve", lambda e, ps=ps: e.tensor_tensor(out=T["kg"], in0=ps, in1=bc_col(sm["egc"]), op=ALU.mult), r=rg + [r_sm["egc"]], w=[R["kg"]])
        P.op("dve", lambda e, ps=ps: e.tensor_tensor(out=T["kdec"], in0=ps, in1=bc_col(sm["fd"]), op=ALU.mult), r=rg + [r_sm["fd"]], w=[R["kdec"]])
        ps, rg, _ = slot()
        mm8(ps, rg, I["v"], None, [rI, cr], [], transpose=C["ident"])
        P.op("act", lambda e, ps=ps: e.activation(out=T["vtm"], in_=ps, func=AF.Copy), r=rg, w=[R["vtm"]])
        ps, rg, _ = slot()
        mm8(ps, rg, T["Ybf"], T["vtm"], [R["Ybf"]], [R["vtm"]])
        P.op("act", lambda e, ps=ps: e.activation(out=T["u"], in_=ps, func=AF.Copy), r=rg, w=[R["u"]])
        ps, rg, _ = slot()
        mm8(ps, rg, T["kg"], T["Ybf"], [R["kg"]], [R["Ybf"]])
        P.op("act", lambda e, ps=ps: e.activation(out=T["wT"], in_=ps, func=AF.Copy), r=rg, w=[R["wT"]])
        ps, rg, _ = slot()
        mm8(ps, rg, T["wT"], T["Sbf"], [R["wT"]], [R["Sbf"]])
        P.op("dve", lambda e, ps=ps: e.tensor_tensor(out=T["vtmp"], in0=T["u"], in1=ps, op=ALU.subtract), r=rg + [R["u"]], w=[R["vtmp"]])
        P.op("dve", lambda e, I=I: e.tensor_tensor(out=T["vnew"], in0=T["vtmp"], in1=bc_col(I["b"]), op=ALU.mult), r=[rI, R["vtmp"]], w=[R["vnew"]])
        pso, rgo, _ = slot()
        fns = []
        rls = []
        for h in range(8):
            fns.append(lambda e, h=h: e.matmul(pso[:, h, :], lhsT=T["Sbf"][:, h, :], rhs=T["qg"][:, h, :], start=True, stop=False))
            rls.append([R["Sbf"], R["qg"]])
            fns.append(lambda e, h=h: e.matmul(pso[:, h, :], lhsT=T["vnew"][:, h, :], rhs=T["IT"][:, h, :], start=False, stop=True))
            rls.append([R["vnew"], R["IT"]])
        P.mm_group(fns, rls, rgo)
        ps, rg, _ = slot()
        mm8(ps, rg, T["kdec"], T["vnew"], [R["kdec"]], [R["vnew"]])
        P.op("dve", lambda e: e.tensor_tensor(out=S, in0=S, in1=bc_col(sm["egl"]), op=ALU.mult), r=[r_sm["egl"]], w=[r_S])
        P.op("dve", lambda e, ps=ps: e.tensor_tensor(out=S, in0=S, in1=ps, op=ALU.add), r=rg, w=[r_S])
        P.op("act", lambda e: e.activation(out=T["Sbf"], in_=S, func=AF.Copy), r=[r_S], w=[R["Sbf"]])
        P.op("act", lambda e: e.activation(out=T["sq"], in_=pso, func=AF.Square), r=rgo, w=[R["sq"]])
        pst, rgt, st_ = slot()
        pstf = P.psum[:, st_ * 1024:(st_ + 1) * 1024]
        sqf = T["sq"].rearrange("p h c -> p (h c)")
        P.mm_group([lambda e: e.matmul(pstf[:, 0:512], lhsT=C["ones_f"], rhs=sqf[:, 0:512], start=True, stop=True),
                    lambda e: e.matmul(pstf[:, 512:1024], lhsT=C["ones_f"], rhs=sqf[:, 512:1024], start=True, stop=True)],
                   [[cr, R["sq"]], [cr, R["sq"]]], rgt)
        P.op("act", lambda e: e.activation(out=rs, in_=pst, func=AF.Ln, bias=C["eps"], scale=1.0 / 128), r=rgt + [cr], w=[r_rs])
        P.op("act", lambda e: e.activation(out=rs, in_=rs, func=AF.Exp, scale=-0.5), r=[r_rs], w=[r_rs])
        P.op("dve", lambda e: e.tensor_tensor(out=T["og"], in0=pso, in1=rs, op=ALU.mult), r=rgo + [r_rs], w=[R["og"]])
        P.op("dve", lambda e, I=I: e.scalar_tensor_tensor(out=ogb, in0=T["og"], scalar=ogain[:, 0:1], in1=I["z"], op0=ALU.mult, op1=ALU.mult),
             r=[R["og"], r_small, rI], w=[r_ogb])
        psp, rgp, _ = slot()
        fns = []
        rls = []
        for oc in range(8):
            for h in range(8):
                fns.append(lambda e, oc=oc, h=h: e.matmul(psp[:, oc, :], lhsT=wo[:, h, oc * 128:(oc + 1) * 128], rhs=ogb[:, h, :], start=(h == 0), stop=(h == 7)))
                rls.append([r_wo, r_ogb])
        P.mm_group(fns, rls, rgp)
        P.op("dve", lambda e, I=I: e.tensor_tensor(out=I["h"], in0=psp, in1=I["h"], op=ALU.add), r=rgp, w=[rI])
        P.dma("sp", v3(dst)[:, :, c0:c0 + 128], I["h"], r=[rI])
    P.barrier()


QB = G


def attn_consts(P, C):
    r = C["r"]
    C["kmx"] = P.alloc([128, 16], F32, persist=True)
    C["negK"] = P.alloc([128, 16], F32, persist=True)
    C["triu_b"] = P.alloc([128, 128], BF16, persist=True)
    C["r_kmx"] = Reg("kmx")
    P.op("pool", lambda e: e.memset(C["kmx"], 0.0), w=[C["r_kmx"]])
    P.op("pool", lambda e: e.tensor_copy(out=C["triu_b"], in_=C["triu"]), r=[r], w=[r])


def load_weight_bf16(P, w_dram, nrow_chunks, ncols, gain, r_small, stage, r_stage, dst, r_dst, SW):
    si = 0
    ns = len(stage)
    for kc in range(nrow_chunks):
        for c0 in range(0, ncols, SW):
            w_ = min(SW, ncols - c0)
            s = si % ns
            q = "sp"
            eng = ("act", "dve")[si % 2]
            si += 1
            P.dma(q, stage[s][:, 0:w_], w_dram[kc * 128:(kc + 1) * 128, c0:c0 + w_], w=[r_stage[s]])
            rd = r_dst(kc, c0) if callable(r_dst) else r_dst
            o_ = dst[:, kc, c0:c0 + w_]
            i_ = stage[s][:, 0:w_]
            if eng == "act":
                if gain is not None:
                    P.op("act", lambda e, o_=o_, i_=i_, kc=kc: e.activation(out=o_, in_=i_, func=AF.Copy, scale=gain[:, kc:kc + 1]),
                         r=[r_stage[s], r_small], w=[rd])
                else:
                    P.op("act", lambda e, o_=o_, i_=i_: e.activation(out=o_, in_=i_, func=AF.Copy), r=[r_stage[s]], w=[rd])
            else:
                if gain is not None:
                    P.op("dve", lambda e, o_=o_, i_=i_, kc=kc: e.tensor_scalar(out=o_, in0=i_, scalar1=gain[:, kc:kc + 1], scalar2=None, op0=ALU.mult),
                         r=[r_stage[s], r_small], w=[rd])
                else:
                    P.op("dve", lambda e, o_=o_, i_=i_: e.tensor_copy(out=o_, in_=i_), r=[r_stage[s]], w=[rd])


def attn_proj_phase(P, C, src, normT, w_dram, is_kv, XA, VA, LP):
    P.reset_arena()
    ng = LP // G
    N = G
    ncols = 2048 if is_kv else 1024
    wb = P.alloc([128, 8, ncols], BF16)
    r_wb = Reg()
    gain = P.alloc([128, 8], F32)
    r_small = Reg()
    SW = 1024
    stage = [P.alloc([128, SW], F32) for _ in range(2)]
    r_stage = [Reg(), Reg()]
    hbuf = [P.alloc([128, 8, N], F32) for _ in range(2)]
    r_hbuf = [Reg(), Reg()]
    hn = P.alloc([128, 8, N], BF16)
    r_hn = Reg()
    sqc = [P.alloc([128, N], F32) for _ in range(2)]
    ssum = P.alloc([128, N], F32)
    rs = P.alloc([128, N], F32)
    rstd = P.alloc([128, N], F32)
    r_tmp = [Reg() for _ in range(5)]
    xa = [P.alloc([128, G], BF16) for _ in range(3)]
    r_xa = [Reg() for _ in range(3)]
    sqk = [P.alloc([128, G], F32) for _ in range(2)]
    r_sqk = [Reg(), Reg()]
    r3 = [P.alloc([128, G], F32) for _ in range(2)]
    r_r3 = [Reg(), Reg()]
    mx = P.alloc([128, 2], F32)
    r_mx = Reg()
    cr = C["r"]
    P.dma("sp", gain, normT, w=[r_small])
    load_weight_bf16(P, w_dram, 8, ncols, gain, r_small, stage, r_stage, wb, r_wb, SW)
    if is_kv:
        vaug = [P.alloc([128, 8, 129], BF16) for _ in range(2)]
        r_vaug = [Reg(), Reg()]
        for b in range(2):
            P.op("pool", lambda e, b=b: e.memset(vaug[b][:, :, 128:129], 1.0), w=[r_vaug[b]])
        for b in range(3):
            P.op("pool", lambda e, b=b: e.memset(xa[b][64:67, :], 1.0), w=[r_xa[b]])
    else:
        A = P.alloc([128, 3, 128], F32)
        Bv = P.alloc([128, 3, 128], F32)
        tpl = P.alloc([128, 8, G], F32)
        r_tpl = Reg()
        P.op("pool", lambda e: e.iota(A, pattern=[[0, 3], [1, 128]], base=0, channel_multiplier=0, allow_small_or_imprecise_dtypes=True), w=[r_tpl])
        P.op("pool", lambda e: e.iota(Bv, pattern=[[128, 3], [0, 128]], base=0, channel_multiplier=0, allow_small_or_imprecise_dtypes=True), w=[r_tpl])
        Af = A.rearrange("p a b -> p (a b)")
        Bf = Bv.rearrange("p a b -> p (a b)")
        P.op("dve", lambda e: e.tensor_scalar(out=Af, in0=Af, scalar1=C["ident"][:, 65:66], scalar2=None, op0=ALU.mult), r=[cr, r_tpl], w=[r_tpl])
        P.op("dve", lambda e: e.tensor_scalar(out=Bf, in0=Bf, scalar1=C["ident"][:, 66:67], scalar2=None, op0=ALU.mult), r=[cr, r_tpl], w=[r_tpl])
        P.op("dve", lambda e: e.tensor_tensor(out=Af, in0=Af, in1=Bf, op=ALU.add), r=[r_tpl], w=[r_tpl])
        for h in range(8):
            P.op("dve", lambda e, h=h: e.tensor_scalar(out=tpl[:, h, :], in0=Af, scalar1=-(2.0 ** -(h + 1)), scalar2=None, op0=ALU.mult), r=[r_tpl], w=[r_tpl])
        P.op("act", lambda e: e.activation(out=C["negK"][64:65, :], in_=C["kmx"][64:65, :], func=AF.Ln, bias=C["eps"][64:65, :], scale=1.0), r=[C["r_kmx"], cr], w=[C["r_kmx"]])
        P.op("act", lambda e: e.activation(out=C["negK"][64:65, :], in_=C["negK"][64:65, :], func=AF.Exp, scale=0.5), r=[C["r_kmx"]], w=[C["r_kmx"]])
        P.op("dve", lambda e: e.tensor_scalar(out=C["negK"][64:65, :], in0=C["negK"][64:65, :], scalar1=-1.0, scalar2=None, op0=ALU.mult), r=[C["r_kmx"]], w=[C["r_kmx"]])

    srcv = src.rearrange("(c p) t -> p c t", p=128)

    def load(g):
        b = g % 2
        P.dma("sp", hbuf[b], srcv[:, :, g * G:(g + 1) * G], w=[r_hbuf[b]])

    load(0)
    xi = 0
    for g in range(ng):
        b = g % 2
        n0 = g * G
        if g + 1 < ng:
            load(g + 1)
        rms_group(P, C, hbuf[b], r_hbuf[b], N, sqc, ssum, rs, rstd, hn, r_hn, r_tmp, bank=7)
        for hi in range(16):
            bk = hi % 3
            col = hi * 64
            ps = P.bank(bk, G, parts=64)
            P.mm_group(
                [lambda e, kc=kc, ps=ps, col=col: e.matmul(ps, lhsT=wb[:, kc, col:col + 64], rhs=hn[:, kc, :], start=(kc == 0), stop=(kc == 7)) for kc in range(8)],
                [[r_wb, r_hn] for kc in range(8)], [P.bank_regs[bk]])
            x_ = xa[xi % 3]
            rx = r_xa[xi % 3]
            xi += 1
            sq_ = sqk[hi % 2]
            rsq = r_sqk[hi % 2]
            sc = 1.0 if is_kv else 0.125
            P.op("act", lambda e, ps=ps, x_=x_, sc=sc: e.activation(out=x_[0:64, :], in_=ps, func=AF.Copy, scale=sc), r=[P.bank_regs[bk]], w=[rx])
            P.op("act", lambda e, ps=ps, sq_=sq_, sc=sc: e.activation(out=sq_[0:64, :], in_=ps, func=AF.Square, scale=sc), r=[P.bank_regs[bk]], w=[rsq])
            bk2 = 3 + hi % 2
            ps2 = P.bank(bk2, G, parts=65)
            P.mm_group([lambda e, ps2=ps2, sq_=sq_: e.matmul(ps2, lhsT=C["ones_f"][0:64, 0:65], rhs=sq_[0:64, :], start=True, stop=True)],
                       [[cr, rsq]], [P.bank_regs[bk2]])
            if is_kv:
                P.op("dve", lambda e, ps2=ps2, hi=hi: e.reduce_max(out=mx[64:65, 0:1], in_=ps2[64:65, :], axis=AX.X), r=[P.bank_regs[bk2]], w=[r_mx])
                P.op("dve", lambda e, hi=hi: e.tensor_tensor(out=C["kmx"][64:65, hi:hi + 1], in0=C["kmx"][64:65, hi:hi + 1], in1=mx[64:65, 0:1], op=ALU.max),
                     r=[r_mx], w=[C["r_kmx"]])
            else:
                rr_ = r3[hi % 2]
                rr3 = r_r3[hi % 2]
                h = hi // 2
                P.op("dve", lambda e, rr_=rr_, h=h: e.tensor_copy(out=rr_[64:67, :], in_=tpl[64:67, h, :]), r=[r_tpl], w=[rr3])
                P.op("act", lambda e, rr_=rr_, ps2=ps2: e.activation(out=rr_[64:65, :], in_=ps2[64:65, :], func=AF.Ln, bias=C["eps"][64:65, :], scale=1.0), r=[P.bank_regs[bk2], cr], w=[rr3])
                P.op("act", lambda e, rr_=rr_: e.activation(out=rr_[64:65, :], in_=rr_[64:65, :], func=AF.Exp, scale=0.5), r=[rr3], w=[rr3])
                P.op("dve", lambda e, rr_=rr_, hi=hi: e.tensor_scalar(out=rr_[64:65, :], in0=rr_[64:65, :], scalar1=C["negK"][64:65, hi:hi + 1], scalar2=None, op0=ALU.mult),
                     r=[C["r_kmx"], rr3], w=[rr3])
                P.op("dve", lambda e, rr_=rr_, x_=x_: e.tensor_copy(out=x_[64:67, :], in_=rr_[64:67, :]), r=[rr3], w=[rx])
            P.dma("sp", XA[hi, :, n0:n0 + G], x_[0:67, :], r=[rx])
        if is_kv:
            for i in range(G // 128):
                va = vaug[i % 2]
                rva = r_vaug[i % 2]
                for half in range(2):
                    bk = 5 + half
                    P.mm_group(
                        [lambda e, kc=kc, bk=bk, i=i, half=half: e.matmul(P.bank(bk, 512), lhsT=hn[:, kc, i * 128:(i + 1) * 128],
                                                                           rhs=wb[:, kc, 1024 + half * 512:1024 + (half + 1) * 512],
                                                                           start=(kc == 0), stop=(kc == 7)) for kc in range(8)],
                        [[r_wb, r_hn] for kc in range(8)], [P.bank_regs[bk]])
                    P.op("act", lambda e, bk=bk, va=va, half=half: e.activation(
                        out=va[:, half * 4:(half + 1) * 4, 0:128], in_=P.bank(bk, 512).rearrange("p (h e) -> p h e", h=4), func=AF.Copy),
                        r=[P.bank_regs[bk]], w=[rva])
                r0 = n0 + i * 128
                P.dma("sp", VA[r0:r0 + 128, :, :], va, r=[rva])
    P.barrier()


def attn_core_phase(P, C, QA, KA, VA, OG, lams, lam_init, sublnB, LP):
    P.reset_arena()
    nt = LP // 128
    nqb = LP // QB
    cr = C["r"]
    ka = [[P.alloc([128, LP], BF16) for _ in range(2)] for _ in range(2)]
    qa = [[P.alloc([128, LP], BF16) for _ in range(2)] for _ in range(2)]
    va = [P.alloc([128, nt, 129], BF16) for _ in range(2)]
    r_in = [Reg(), Reg()]
    BI = P.alloc([128, 33], F32)
    biasT = P.alloc([128, 8, 33], F32)
    r_bias = Reg()
    lam = P.alloc([128, 1], F32)
    subln = P.alloc([128, 128], F32)
    r_small = Reg()
    LA = 3
    NSB = 4
    SB0 = 3
    NPT = LA + 3
    pt = [P.alloc([128, QB], BF16) for _ in range(NPT)]
    r_pt = [Reg() for _ in range(NPT)]
    accs = [P.alloc([128, 2, 3, 129], F32) for _ in range(2)]
    r_accs = [[[Reg() for _ in range(3)] for _ in range(2)] for _ in range(2)]
    NO = 6
    o_ = [P.alloc([128, 128], F32) for _ in range(NO)]
    r_o = [Reg() for _ in range(NO)]
    sm = [P.alloc([128, 8], F32) for _ in range(NO)]
    r_sm = [Reg() for _ in range(NO)]
    junk = P.alloc([128, 128], F32)
    NOG = 18
    ogT = [P.alloc([128, 128], BF16) for _ in range(NOG)]
    r_ogT = [Reg() for _ in range(NOG)]

    lv = [P.alloc([128, 64], F32) for _ in range(4)]
    l2 = P.alloc([128, 2], F32)
    for k_ in range(4):
        P.dma("sp", lv[k_], lams[k_], w=[r_small])
    for k_ in range(2):
        P.op("dve", lambda e, k_=k_: e.tensor_tensor(out=lv[2 * k_], in0=lv[2 * k_], in1=lv[2 * k_ + 1], op=ALU.mult), r=[r_small], w=[r_small])
        P.op("dve", lambda e, k_=k_: e.reduce_sum(out=l2[:, k_:k_ + 1], in_=lv[2 * k_], axis=AX.X), r=[r_small], w=[r_small])
    P.op("act", lambda e: e.activation(out=l2, in_=l2, func=AF.Exp), r=[r_small], w=[r_small])
    P.op("dve", lambda e: e.tensor_tensor(out=lam, in0=l2[:, 0:1], in1=l2[:, 1:2], op=ALU.subtract), r=[r_small], w=[r_small])
    P.op("dve", lambda e: e.tensor_scalar(out=lam, in0=lam, scalar1=float(lam_init), scalar2=None, op0=ALU.add), r=[r_small], w=[r_small])
    P.dma("sp", subln, sublnB, w=[r_small])
    P.op("dve", lambda e: e.tensor_scalar(out=subln, in0=subln, scalar1=float(1.0 - lam_init), scalar2=None, op0=ALU.mult), r=[r_small], w=[r_small])
    P.op("pool", lambda e: e.iota(BI, pattern=[[128, 33]], base=-30 * 128, channel_multiplier=1, allow_small_or_imprecise_dtypes=True), w=[r_bias])
    for h in range(8):
        P.op("dve", lambda e, h=h: e.tensor_scalar(out=biasT[:, h, :], in0=BI, scalar1=2.0 ** -(h + 1), scalar2=None, op0=ALU.mult), r=[r_bias], w=[r_bias])

    def load_parts(h):
        b = h % 2
        parts = []
        for i in range(2):
            parts.append(lambda b=b, i=i, h=h: P.dma("sp", ka[b][i][0:67, :], KA[2 * h + i, :, :], w=[r_in[b]]))
            parts.append(lambda b=b, i=i, h=h: P.dma("sp", qa[b][i][0:67, :], QA[2 * h + i, :, :], w=[r_in[b]]))
        vav = VA[:, h, :].rearrange("(t p) e -> p t e", p=128)
        for t0_ in range(0, nt, 11):
            t1_ = min(nt, t0_ + 11)
            parts.append(lambda b=b, t0_=t0_, t1_=t1_, vav=vav: P.dma("sp", va[b][:, t0_:t1_, :], vav[:, t0_:t1_, :], w=[r_in[b]]))
        return parts

    def load(h):
        for f in load_parts(h):
            f()

    pending_loads = []

    steps = []
    for h in range(8):
        for qb in range(nqb):
            for i in range(2):
                nkt = 3 * qb + 3
                for kt in range(nkt):
                    steps.append((h, qb, i, kt, nkt))
    cnt = dict(s=0, p=0, o=0)
    S_info = {}

    def emit_S(idx):
        h, qb, i, kt, nkt = steps[idx]
        b = h % 2
        q0 = qb * QB
        j = max(0, kt - 3 * qb)
        off = j * 128
        w_ = QB - off
        sb = SB0 + (cnt["s"] % NSB)
        cnt["s"] += 1
        ps = P.bank(sb, w_)
        P.mm_group([lambda e, ps=ps, kt=kt, off=off, b=b, i=i, q0=q0: e.matmul(
            ps, lhsT=ka[b][i][0:67, kt * 128:(kt + 1) * 128], rhs=qa[b][i][0:67, q0 + off:q0 + QB], start=True, stop=True)],
            [[r_in[b]]], [P.bank_regs[sb]])
        S_info[idx] = (ps, sb, j, off, w_)

    deferred = []

    def tick():
        for d in deferred:
            d[0] -= 1
        while deferred and deferred[0][0] <= 0:
            deferred.pop(0)[1]()

    load(0)
    s_emitted = 0
    nsteps = len(steps)
    for idx in range(nsteps):
        h, qb, i, kt, nkt = steps[idx]
        b = h % 2
        rI = r_in[b]
        q0 = qb * QB
        if qb == 0 and i == 0 and kt == 0 and h + 1 < 8:
            pending_loads.extend(load_parts(h + 1))
        if pending_loads and (idx % 16 == 0):
            pending_loads.pop(0)()
        while s_emitted < min(idx + LA + 1, nsteps):
            if steps[s_emitted][0] != h:
                while pending_loads:
                    pending_loads.pop(0)()
            emit_S(s_emitted)
            s_emitted += 1
        ps, sb, j, off, w_ = S_info.pop(idx)
        p_ = pt[cnt["p"] % NPT]
        rp = r_pt[cnt["p"] % NPT]
        cnt["p"] += 1
        dlt = kt - 3 * qb + 30
        P.op("act", lambda e, ps=ps, p_=p_, w_=w_, h=h, dlt=dlt: e.activation(out=p_[:, 0:w_], in_=ps, func=AF.Exp, bias=biasT[:, h, dlt:dlt + 1], scale=1.0),
             r=[P.bank_regs[sb], r_bias], w=[rp])
        if kt >= 3 * qb:
            P.op("pool", lambda e, p_=p_: e.tensor_tensor(out=p_[:, 0:128], in0=p_[:, 0:128], in1=C["triu_b"], op=ALU.mult), r=[rp, cr], w=[rp])
        ab_ = qb % 2
        for jj in range(j, 3):
            abk = jj
            acc = P.bank(abk, 129)
            c_ = jj * 128 - off
            first = (kt == 0)
            last = (kt == 3 * qb + jj)
            P.mm_group([lambda e, acc=acc, p_=p_, c_=c_, kt=kt, first=first, last=last, b=b: e.matmul(
                acc, lhsT=p_[:, c_:c_ + 128], rhs=va[b][:, kt, :], start=first, stop=last)],
                [[rp, rI]], [P.bank_regs[abk]])
            if last:
                P.op("act", lambda e, acc=acc, ab_=ab_, i=i, jj=jj: e.activation(out=accs[ab_][:, i, jj, :], in_=acc, func=AF.Copy),
                     r=[P.bank_regs[abk]], w=[r_accs[ab_][i][jj]])
        tick()
        if i == 1 and kt == nkt - 1:
            for jj in range(3):
                a0 = accs[ab_][:, 0, jj, :]
                a1 = accs[ab_][:, 1, jj, :]
                ra0, ra1 = r_accs[ab_][0][jj], r_accs[ab_][1][jj]
                oi = cnt["o"]
                cnt["o"] += 1
                s_ = sm[oi % NO]
                rs_ = r_sm[oi % NO]
                o = o_[oi % NO]
                ro = r_o[oi % NO]
                og = ogT[oi % NOG]
                rog = r_ogT[oi % NOG]
                P.op("dve", lambda e, s_=s_, a0=a0: e.reciprocal(out=s_[:, 0:1], in_=a0[:, 128:129]), r=[ra0], w=[rs_])
                P.op("dve", lambda e, s_=s_, a1=a1: e.reciprocal(out=s_[:, 1:2], in_=a1[:, 128:129]), r=[ra1], w=[rs_])
                P.op("dve", lambda e, s_=s_: e.scalar_tensor_tensor(out=s_[:, 2:3], in0=s_[:, 1:2], scalar=-1.0, in1=lam, op0=ALU.mult, op1=ALU.mult), r=[rs_, r_small], w=[rs_])
                P.op("dve", lambda e, s_=s_, a0=a0, o=o: e.tensor_scalar(out=o, in0=a0[:, 0:128], scalar1=s_[:, 0:1], scalar2=None, op0=ALU.mult), r=[ra0, rs_], w=[ro])
                P.op("dve", lambda e, s_=s_, a1=a1, o=o: e.scalar_tensor_tensor(out=o, in0=a1[:, 0:128], scalar=s_[:, 2:3], in1=o, op0=ALU.mult, op1=ALU.add),
                     r=[ra1, rs_, ro], w=[ro])
                P.op("dve", lambda e, s_=s_: e.memset(s_[:, 3:4], 0.0), w=[rs_])
                P.op("act", lambda e, s_=s_, o=o: e.activation(out=junk, in_=o, func=AF.Square, accum_out=s_[:, 3:4]), r=[ro], w=[rs_])
                P.op("act", lambda e, s_=s_: e.activation(out=s_[:, 4:5], in_=s_[:, 3:4], func=AF.Ln, bias=C["eps"], scale=1.0 / 128), r=[rs_, cr], w=[rs_])
                P.op("act", lambda e, s_=s_: e.activation(out=s_[:, 5:6], in_=s_[:, 4:5], func=AF.Exp, scale=-0.5), r=[rs_], w=[rs_])
                P.op("dve", lambda e, s_=s_, o=o: e.scalar_tensor_tensor(out=o, in0=o, scalar=s_[:, 5:6], in1=subln, op0=ALU.mult, op1=ALU.mult),
                     r=[ro, rs_, r_small], w=[ro])
                c0 = q0 + jj * 128

                def fin(o=o, ro=ro, og=og, rog=rog, h=h, c0=c0):
                    tb = 7
                    P.mm_group([lambda e, tb=tb, o=o: e.transpose(P.bank(tb, 128), o, C["ident"])], [[ro, cr]], [P.bank_regs[tb]])
                    P.op("dve", lambda e, tb=tb, og=og: e.tensor_copy(out=og, in_=P.bank(tb, 128)), r=[P.bank_regs[tb]], w=[rog])
                    P.dma("sp", OG[h * 128:(h + 1) * 128, c0:c0 + 128], og, r=[rog])
                deferred.append([3 + jj, fin])
    while deferred:
        deferred.pop(0)[1]()
    P.barrier()


def outproj_phase(P, C, w_o, OG, src, dst, LP):
    P.reset_arena()
    ng = LP // G
    wb = P.alloc([128, 8, 1024], BF16)
    r_wb = Reg()
    stage = [P.alloc([128, 1024], F32) for _ in range(2)]
    r_stage = [Reg(), Reg()]
    load_weight_bf16(P, w_o, 8, 1024, None, None, stage, r_stage, wb, r_wb, 1024)
    hbuf = [P.alloc([128, 8, G], F32) for _ in range(2)]
    ogb = [P.alloc([128, 8, G], BF16) for _ in range(2)]
    r_in = [Reg(), Reg()]
    srcv = src.rearrange("(c p) t -> p c t", p=128)
    dstv = dst.rearrange("(c p) t -> p c t", p=128)
    ogv = OG.rearrange("(c p) t -> p c t", p=128)

    def load(g):
        b = g % 2
        P.dma("sp", hbuf[b], srcv[:, :, g * G:(g + 1) * G], w=[r_in[b]])
        P.dma("sp", ogb[b], ogv[:, :, g * G:(g + 1) * G], w=[r_in[b]])

    load(0)
    for g in range(ng):
        b = g % 2
        if g + 1 < ng:
            load(g + 1)
        for oc in range(8):
            bk = oc % 4
            P.mm_group([lambda e, kc=kc, bk=bk, oc=oc, b=b: e.matmul(P.bank(bk, G), lhsT=wb[:, kc, oc * 128:(oc + 1) * 128], rhs=ogb[b][:, kc, :],
                                                                      start=(kc == 0), stop=(kc == 7)) for kc in range(8)],
                       [[r_wb, r_in[b]] for kc in range(8)], [P.bank_regs[bk]])
            P.op("dve", lambda e, bk=bk, oc=oc, b=b: e.tensor_tensor(out=hbuf[b][:, oc, :], in0=P.bank(bk, G), in1=hbuf[b][:, oc, :], op=ALU.add),
                 r=[P.bank_regs[bk]], w=[r_in[b]])
        P.dma("sp", dstv[:, :, g * G:(g + 1) * G], hbuf[b], r=[r_in[b]])
    P.barrier()


def final_phase(P, C, src, normT, out, LP, t0, t1):
    P.reset_arena()
    ng = LP // G
    N = G
    gain = P.alloc([128, 8], F32)
    r_small = Reg()
    hbuf = [P.alloc([128, 8, N], F32) for _ in range(2)]
    r_hbuf = [Reg(), Reg()]
    hn = P.alloc([128, 8, N], BF16)
    r_hn = Reg()
    sqc = [P.alloc([128, N], F32) for _ in range(2)]
    ssum = P.alloc([128, N], F32)
    rs = P.alloc([128, N], F32)
    rstd = P.alloc([128, N], F32)
    r_tmp = [Reg() for _ in range(5)]
    P.dma("sp", gain, normT, w=[r_small])
    srcv = src.rearrange("(c p) t -> p c t", p=128)
    outv = out.rearrange("(c p) t -> p c t", p=128)

    def load(g):
        b = g % 2
        P.dma("sp", hbuf[b], srcv[:, :, g * G:(g + 1) * G], w=[r_hbuf[b]])

    load(0)
    for g in range(ng):
        b = g % 2
        n0 = g * G
        if g + 1 < ng:
            load(g + 1)
        rms_group(P, C, hbuf[b], r_hbuf[b], N, sqc, ssum, rs, rstd, hn, r_hn, r_tmp, bank=7)
        for c in range(8):
            P.op("dve", lambda e, c=c, b=b: e.scalar_tensor_tensor(out=hbuf[b][:, c, :], in0=hbuf[b][:, c, :], scalar=gain[:, c:c + 1], in1=rstd,
                                                                     op0=ALU.mult, op1=ALU.mult), r=[r_small, r_tmp[4]], w=[r_hbuf[b]])
        a = max(n0, t0)
        bb = min(n0 + G, t1)
        if bb > a:
            P.dma("sp", outv[:, :, a - t0:bb - t0], hbuf[b][:, :, a - n0:bb - n0], r=[r_hbuf[b]])
    P.barrier()

LP_FULL = 4224
L_REAL = 4112
N_META = 16


def build_program(LP=LP_FULL, stop=None, debug=False):
    nc = bass.Bass("TRN2", target_bir_lowering=False)

    def din(name, shape):
        return nc.dram_tensor(name, list(shape), F32, kind="ExternalInput").ap()

    def scr(name, shape, dt=F32, out=False):
        return nc.dram_tensor(name, list(shape), dt, kind=("ExternalOutput" if out else "Internal")).ap()

    ht0 = din("ht0", [1024, LP])
    a_w_in = din("a_w_in", [2, 1024, 4112]); a_w_o = din("a_w_o", [2, 1024, 1024])
    a_normT = din("a_normT", [2, 128, 8]); a_convT = din("a_convT", [2, 128, 24, 4])
    a_logB = din("a_logB", [2, 128, 8]); a_dtbB = din("a_dtbB", [2, 128, 8]); a_onormB = din("a_onormB", [2, 128, 1])
    kv_normT = din("kv_normT", [128, 8]); w_kv = din("w_kv", [1024, 2048])
    lk1B = din("lk1B", [128, 64]); lk2B = din("lk2B", [128, 64])
    b_normT = din("b_normT", [2, 128, 8]); b_w_q = din("b_w_q", [2, 1024, 1024]); b_w_o = din("b_w_o", [2, 1024, 1024])
    lq1B = din("lq1B", [2, 128, 64]); lq2B = din("lq2B", [2, 128, 64]); b_sublnB = din("b_sublnB", [2, 128, 128])
    ffn_normT = din("ffn_normT", [4, 128, 8]); ffn_w_up = din("ffn_w_up", [4, 1024, 5632]); ffn_convT = din("ffn_convT", [4, 128, 44, 3])
    ffn_w_down = din("ffn_w_down", [4, 2816, 1024]); final_normT = din("final_normT", [128, 8])
    yT = nc.dram_tensor("yT", [1024, 4096], F32, kind="ExternalOutput").ap()

    HA = scr("HA", [1024, LP], out=debug); HB = scr("HB", [1024, LP], out=debug)
    QT = scr("QT", [1024, LP]); KT = scr("KT", [1024, LP]); VT = scr("VT", [1024, LP]); ZT = scr("ZT", [1024, LP])
    BT = scr("BT", [LP, 8]); GT = scr("GT", [LP, 8])
    KA = scr("KA", [16, 67, LP], BF16); QA = scr("QA", [16, 67, LP], BF16)
    VA = scr("VA", [LP, 8, 129], BF16); OG = scr("OG", [1024, LP], BF16)

    with ExitStack() as es:
        P = Prog(nc, es)
        C = consts(P)
        gdn_consts(P, C)
        attn_consts(P, C)

        def run():
            src = ht0
            for l in range(2):
                gdn_proj_phase(P, C, a_w_in[l], a_normT[l], a_convT[l], a_logB[l], a_dtbB[l], src, QT, KT, VT, ZT, BT, GT, LP)
                gdn_scan_phase(P, C, a_w_o[l], a_onormB[l], QT, KT, VT, ZT, BT, GT, src, HA, LP)
                if stop == f"mix{l}":
                    return
                ffn_phase(P, C, l, ffn_w_up[l], ffn_w_down[l], ffn_normT[l], ffn_convT[l], HA, HB, LP)
                if stop == f"ffn{l}":
                    return
                src = HB
            attn_proj_phase(P, C, HB, kv_normT, w_kv, True, KA, VA, LP)
            for l in (2, 3):
                j = l - 2
                lam_init = 0.8 - 0.6 * math.exp(-0.3 * l)
                attn_proj_phase(P, C, HB, b_normT[j], b_w_q[j], False, QA, None, LP)
                attn_core_phase(P, C, QA, KA, VA, OG, (lq1B[j], lk1B, lq2B[j], lk2B), lam_init, b_sublnB[j], LP)
                outproj_phase(P, C, b_w_o[j], OG, HB, HA, LP)
                if stop == f"mix{l}":
                    return
                ffn_phase(P, C, l, ffn_w_up[l], ffn_w_down[l], ffn_normT[l], ffn_convT[l], HA, HB, LP)
                if stop == f"ffn{l}":
                    return
            final_phase(P, C, HB, final_normT, yT, LP, N_META, N_META + 4096)

        run()
        info = dict(ninst=P.ninst, nsem=P.nsem)
        P.finish()
    return nc, info


def prep_shared(inputs):
    f = lambda a: np.ascontiguousarray(np.asarray(a, dtype=np.float32))

    def pc(v):
        v = np.asarray(v, dtype=np.float32)
        return np.ascontiguousarray(v.reshape(-1, 128).T)

    def bcast(v, n=128):
        v = np.asarray(v, dtype=np.float32)
        return np.ascontiguousarray(np.broadcast_to(v[None, :], (n, v.shape[0])))

    sh = {}
    sh["a_w_in"] = f(inputs["a_w_in"]); sh["a_w_o"] = f(inputs["a_w_o"])
    sh["a_normT"] = np.stack([pc(inputs["a_norm"][i]) for i in range(2)])
    sh["a_convT"] = np.stack([np.ascontiguousarray(np.asarray(inputs["a_conv"][i], np.float32).T.reshape(24, 128, 4).transpose(1, 0, 2)) for i in range(2)])
    sh["a_logB"] = np.stack([bcast(inputs["a_log"][i]) for i in range(2)])
    sh["a_dtbB"] = np.stack([bcast(inputs["a_dt_bias"][i]) for i in range(2)])
    sh["a_onormB"] = np.stack([f(inputs["a_onorm"][i]).reshape(128, 1) for i in range(2)])
    sh["kv_normT"] = pc(inputs["kv_norm"]); sh["w_kv"] = f(inputs["w_kv"])
    sh["lk1B"] = bcast(inputs["lambda_k1"]); sh["lk2B"] = bcast(inputs["lambda_k2"])
    sh["b_normT"] = np.stack([pc(inputs["b_norm"][i]) for i in range(2)])
    sh["b_w_q"] = f(inputs["b_w_q"]); sh["b_w_o"] = f(inputs["b_w_o"])
    sh["lq1B"] = np.stack([bcast(inputs["b_lambda_q1"][i]) for i in range(2)])
    sh["lq2B"] = np.stack([bcast(inputs["b_lambda_q2"][i]) for i in range(2)])
    sh["b_sublnB"] = np.stack([bcast(inputs["b_subln"][i]) for i in range(2)])
    sh["ffn_normT"] = np.stack([pc(inputs["ffn_norm"][i]) for i in range(4)])
    sh["ffn_w_up"] = f(inputs["ffn_w_up"])
    sh["ffn_convT"] = np.stack([np.ascontiguousarray(np.asarray(inputs["ffn_conv"][i], np.float32).T.reshape(44, 128, 3).transpose(1, 0, 2)) for i in range(4)])
    sh["ffn_w_down"] = f(inputs["ffn_w_down"]); sh["final_normT"] = pc(inputs["final_norm"])
    return sh


def make_ht0(x_b, meta, LP=LP_FULL):
    ht = np.zeros((1024, LP), np.float32)
    ht[:, 0:N_META] = np.asarray(meta, np.float32).T
    n = min(LP - N_META, x_b.shape[0])
    ht[:, N_META:N_META + n] = np.asarray(x_b[:n], np.float32).T
    return ht


_CACHE = {}


def kernel(**inputs):
    x = np.asarray(inputs["x"], dtype=np.float32)
    B = x.shape[0]
    if "nc" not in _CACHE:
        _CACHE["nc"] = build_program()[0]
    nc = _CACHE["nc"]
    sh = prep_shared(inputs)
    in_maps = []
    for b in range(B):
        m = dict(sh)
        m["ht0"] = make_ht0(x[b], inputs["meta_tokens"])
        in_maps.append(m)
    res = run_bass_kernel_spmd(nc, in_maps, core_ids=list(range(B)))
    out = np.empty((B, 4096, 1024), np.float32)
    for b in range(B):
        out[b] = res.results[b]["yT"].T
    return out
```

```python
import math
import numpy as np
from contextlib import ExitStack
import concourse.bass as bass
import concourse.mybir as mybir
from concourse.bass_utils import run_bass_kernel_spmd

F32 = mybir.dt.float32
BF16 = mybir.dt.bfloat16
U8 = mybir.dt.uint8
AF = mybir.ActivationFunctionType
ALU = mybir.AluOpType
AX = mybir.AxisListType

EPOCH = 16000
NDMASEM = 8
DT_SIZE = {F32: 4, BF16: 2, U8: 1}


class Reg:
    __slots__ = ("name", "w", "r")

    def __init__(self, name=""):
        self.name = name
        self.w = None
        self.r = {}


class Prog:
    CE = ("pe", "act", "dve", "pool")
    ALLE = ("pe", "act", "dve", "pool", "sp")

    def __init__(self, nc, es, same_engine_sync=True):
        self.nc = nc
        self.es = es
        self.streams = {k: [] for k in self.ALLE}
        self.cnt = {k: 0 for k in self.CE}
        self.sems = {k: [] for k in self.CE}
        self.waited = {k: {} for k in self.ALLE}
        self.dma_cnt = {k: 0 for k in self.ALLE}
        self.dma_sems = {}
        self.same_engine_sync = same_engine_sync
        self.nsem = 0
        self.ninst = 0
        self.sb_total = 212800
        self.sbuf = es.enter_context(nc.sbuf_tensor("arena", [128, self.sb_total], U8))
        self.psum = es.enter_context(nc.psum_tensor("psum", [128, 4096], F32))
        self.sb_persist = 0
        self.sb_off = 0
        self.bank_regs = [Reg(f"bank{i}") for i in range(8)]

    def alloc(self, shape, dtype, persist=False):
        n = int(np.prod(shape[1:])) * DT_SIZE[dtype]
        n = (n + 31) // 32 * 32
        off = self.sb_off
        assert off + n <= self.sb_total, f"SBUF arena overflow {off}+{n}"
        self.sb_off = off + n
        if persist:
            assert self.sb_persist == off, "persistent allocs must come first"
            self.sb_persist = self.sb_off
        ap = self.sbuf[0:shape[0], off:off + n].bitcast(dtype)
        fs = int(np.prod(shape[1:]))
        ap = ap[:, 0:fs]
        if len(shape) == 3:
            ap = ap.rearrange("p (a b) -> p a b", a=shape[1])
        elif len(shape) == 4:
            ap = ap.rearrange("p (a b c) -> p a b c", a=shape[1], b=shape[2])
        return ap

    def reset_arena(self):
        self.sb_off = self.sb_persist

    def bank(self, i, n=512, parts=128):
        return self.psum[0:parts, i * 512:i * 512 + n]

    def _new_sem(self, name):
        self.nsem += 1
        return self.es.enter_context(self.nc.semaphore(name))

    def _sem_for(self, eng, n):
        e = (n - 1) // EPOCH
        while len(self.sems[eng]) <= e:
            self.sems[eng].append(self._new_sem(f"s_{eng}_{len(self.sems[eng])}"))
        return self.sems[eng][e], (n - 1) % EPOCH + 1

    def _dsem(self, q, slot):
        if q not in self.dma_sems:
            self.dma_sems[q] = [self._new_sem(f"d_{q}_{i}") for i in range(NDMASEM)]
        return self.dma_sems[q][slot]

    def _wait(self, eng, ev):
        if ev[0] == 'c':
            _, src, n = ev
            if src == eng and (eng == 'pe' or not self.same_engine_sync):
                return
            if src == eng and eng in ('act', 'dve') and self.cnt[eng] - n >= 6:
                return

            key = ('c', src)
            if self.waited[eng].get(key, 0) >= n:
                return
            self.waited[eng][key] = n
            sem, val = self._sem_for(src, n)
        else:
            _, q, j = ev
            slot = j % NDMASEM
            need = j // NDMASEM + 1
            key = ('d', q, slot)
            if self.waited[eng].get(key, 0) >= need:
                return
            self.waited[eng][key] = need
            sem = self._dsem(q, slot)
            val = 16 * need
        self.streams[eng].append(lambda e, sem=sem, val=val: e.wait_ge(sem, val))
        self.ninst += 1

    def _deps(self, eng, r, w, is_dma=False):
        for reg in r:
            for ev in (reg.w or ()):
                self._wait(eng, ev)
        for reg in w:
            if is_dma and reg.w and not reg.r and all(ev[0] == 'd' for ev in reg.w):
                continue
            for ev in (reg.w or ()):
                self._wait(eng, ev)
            for ev in reg.r.values():
                self._wait(eng, ev)

    @staticmethod
    def _evkey(ev):
        return (ev[0], ev[1]) if ev[0] == 'c' else (ev[0], ev[1], ev[2] % NDMASEM)

    def _record(self, ev, r, w):
        k = self._evkey(ev)
        for reg in r:
            reg.r[k] = ev
        for reg in w:
            if ev[0] == 'd' and reg.w and not reg.r and all(e2[0] == 'd' for e2 in reg.w):
                reg.w = [e2 for e2 in reg.w if self._evkey(e2) != k] + [ev]
            else:
                reg.w = [ev]
            reg.r = {}

    def op(self, eng, fn, r=(), w=()):
        self._deps(eng, r, w)
        n = self.cnt[eng] + 1
        self.cnt[eng] = n
        sem, _ = self._sem_for(eng, n)
        self.streams[eng].append(lambda e, fn=fn, sem=sem: fn(e).then_inc(sem, 1))
        self.ninst += 1
        self._record(('c', eng, n), r, w)

    def mm_group(self, mms, r_list, w):
        eng = "pe"
        self._deps(eng, (), w)
        n = self.cnt[eng] + 1
        self.cnt[eng] = n
        sem, _ = self._sem_for(eng, n)
        ev = ('c', eng, n)
        last = len(mms) - 1
        for i, fn in enumerate(mms):
            self._deps(eng, r_list[i], ())
            if i == last:
                self.streams[eng].append(lambda e, fn=fn, sem=sem: fn(e).then_inc(sem, 1))
            else:
                self.streams[eng].append(lambda e, fn=fn: fn(e))
            self.ninst += 1
            self._record(ev, r_list[i], ())
        self._record(ev, (), w)

    def dma(self, q, out, in_, r=(), w=()):
        self._deps(q, r, w, is_dma=True)
        j = self.dma_cnt[q]
        self.dma_cnt[q] = j + 1
        sem = self._dsem(q, j % NDMASEM)
        self.streams[q].append(lambda e, out=out, in_=in_, sem=sem: e.dma_start(out=out, in_=in_).then_inc(sem, 16))
        self.ninst += 1
        self._record(('d', q, j), r, w)

    def barrier(self):
        for eng in self.ALLE:
            for src in self.CE:
                if self.cnt[src] > 0 and src != eng:
                    self._wait(eng, ('c', src, self.cnt[src]))
            for q, c in self.dma_cnt.items():
                for j in range(max(0, c - NDMASEM), c):
                    self._wait(eng, ('d', q, j))

    def finish(self):
        for q, c in self.dma_cnt.items():
            for j in range(max(0, c - NDMASEM), c):
                self._wait("sp", ('d', q, j))
        for src in self.CE:
            if self.cnt[src] > 0:
                self._wait("sp", ('c', src, self.cnt[src]))
        nc = self.nc
        block = self.es.enter_context(nc.Block())
        st = self.streams

        @block.sync
        def _(e):
            for f in st["sp"]:
                f(e)

        @block.tensor
        def _(e):
            for f in st["pe"]:
                f(e)

        @block.scalar
        def _(e):
            for f in st["act"]:
                f(e)

        @block.vector
        def _(e):
            for f in st["dve"]:
                f(e)

        @block.gpsimd
        def _(e):
            for f in st["pool"]:
                f(e)


G = 384
D = 1024
DFF = 2816
NFC = 22
EPS = 1e-6


def consts(P):
    C = {}
    C["ones_f"] = P.alloc([128, 128], F32, persist=True)
    C["eps"] = P.alloc([128, 1], F32, persist=True)
    C["r"] = Reg("consts")
    P.op("pool", lambda e: e.memset(C["ones_f"], 1.0), w=[C["r"]])
    P.op("pool", lambda e: e.memset(C["eps"], EPS), w=[C["r"]])
    return C


def rms_group(P, C, hbuf, r_h, N, sqc, ssum, rs, rstd, hn, r_hn, r_tmp, bank, nchunk=8, dim=1024):
    for c in range(nchunk):
        P.op("pool", lambda e, c=c: e.tensor_tensor(out=sqc[c % 2], in0=hbuf[:, c, :], in1=hbuf[:, c, :], op=ALU.mult),
             r=[r_h], w=[r_tmp[c % 2]])
        if c == 0:
            P.op("pool", lambda e: e.tensor_copy(out=ssum, in_=sqc[0]), r=[r_tmp[0]], w=[r_tmp[2]])
        else:
            P.op("pool", lambda e, c=c: e.tensor_tensor(out=ssum, in0=ssum, in1=sqc[c % 2], op=ALU.add),
                 r=[r_tmp[c % 2]], w=[r_tmp[2]])
    br = P.bank_regs[bank]
    P.mm_group([lambda e: e.matmul(P.bank(bank, N), lhsT=C["ones_f"], rhs=ssum, start=True, stop=True)],
               [[C["r"], r_tmp[2]]], [br])
    P.op("act", lambda e: e.activation(out=rs, in_=P.bank(bank, N), func=AF.Ln, bias=C["eps"], scale=1.0 / dim),
         r=[br, C["r"]], w=[r_tmp[3]])
    P.op("act", lambda e: e.activation(out=rstd, in_=rs, func=AF.Exp, scale=-0.5), r=[r_tmp[3]], w=[r_tmp[4]])
    if hn is not None:
        P.op("dve", lambda e: e.tensor_tensor(out=hn, in0=hbuf, in1=rstd.unsqueeze(1).broadcast_to([128, nchunk, N]), op=ALU.mult),
             r=[r_h, r_tmp[4]], w=[r_hn])


def ffn_phase(P, C, layer, w_up, w_down, normT, convT, src, dst, LP):
    P.reset_arena()
    ng = LP // G
    N = G + 2
    wup = P.alloc([128, 8, 2 * DFF], BF16)
    wdn = P.alloc([128, NFC, D], BF16)
    r_wup = [Reg() for _ in range(16)]
    r_wdn = [Reg() for _ in range(11)]
    gain = P.alloc([128, 8], F32)
    cw = P.alloc([128, 44, 3], F32)
    r_small = Reg()
    SW = 1408
    stage = [P.alloc([128, SW], F32) for _ in range(2)]
    r_stage = [Reg(), Reg()]
    hbuf = [P.alloc([128, 8, N], F32) for _ in range(2)]
    r_hbuf = [Reg(), Reg()]
    hn = P.alloc([128, 8, N], BF16)
    r_hn = Reg()
    act = P.alloc([128, NFC, G], BF16)
    r_act = [Reg() for _ in range(NFC)]
    tg = [P.alloc([128, G], F32) for _ in range(2)]
    tu = [P.alloc([128, G], F32) for _ in range(2)]
    r_tg = [Reg(), Reg()]
    r_tu = [Reg(), Reg()]
    sqc = [P.alloc([128, N], F32) for _ in range(2)]
    ssum = P.alloc([128, N], F32)
    rs = P.alloc([128, N], F32)
    rstd = P.alloc([128, N], F32)
    r_tmp = [Reg() for _ in range(5)]

    P.dma("sp", gain, normT, w=[r_small])
    P.dma("sp", cw, convT, w=[r_small])
    stage4 = stage + [hb.rearrange("p c t -> p (c t)")[:, 0:SW] for hb in hbuf]
    r_stage4 = r_stage + r_hbuf
    load_weight_bf16(P, w_up, 8, 2 * DFF, gain, r_small, stage4, r_stage4, wup, lambda kc, c0: r_wup[kc * 2 + c0 // DFF], SW)
    load_weight_bf16(P, w_down, NFC, D, None, None, [s_[:, 0:D] for s_ in stage4], r_stage4, wdn, lambda kc, c0: r_wdn[kc // 2], D)

    tg = tg + [stage[0][:, 0:G]]
    tu = tu + [stage[1][:, 0:G]]
    r_tg = r_tg + [r_stage[0]]
    r_tu = r_tu + [r_stage[1]]

    srcv = src.rearrange("(c p) t -> p c t", p=128)
    dstv = dst.rearrange("(c p) t -> p c t", p=128)

    def load(g):
        b = g % 2
        n0 = g * G
        if g == 0:
            P.op("pool", lambda e: e.memset(hbuf[b][:, :, 0:2], 0.0), w=[r_hbuf[b]])
            P.dma("sp", hbuf[b][:, :, 2:N], srcv[:, :, 0:G], w=[r_hbuf[b]])
        else:
            P.dma("sp", hbuf[b], srcv[:, :, n0 - 2:n0 + G], w=[r_hbuf[b]])

    load(0)
    rms_group(P, C, hbuf[0], r_hbuf[0], N, sqc, ssum, rs, rstd, hn, r_hn, r_tmp, bank=7)
    for g in range(ng):
        b = g % 2
        n0 = g * G
        if g + 1 < ng:
            load(g + 1)
        for j in range(NFC):
            pb = (j % 3) * 2
            for half, (tt, r_tt) in enumerate(((tg, r_tg), (tu, r_tu))):
                bk = pb + half
                col = half * DFF + j * 128
                ch = half * NFC + j
                P.mm_group(
                    [lambda e, kc=kc, bk=bk, col=col: e.matmul(P.bank(bk, N), lhsT=wup[:, kc, col:col + 128], rhs=hn[:, kc, :],
                                                                start=(kc == 0), stop=(kc == 7)) for kc in range(8)],
                    [[r_wup[kc * 2 + (col // (2 * SW))], r_hn] for kc in range(8)], [P.bank_regs[bk]])
                t = tt[j % 3]
                rt = r_tt[j % 3]
                P.op("act", lambda e, bk=bk, t=t, ch=ch: e.activation(out=t, in_=P.bank(bk, N)[:, 0:G], func=AF.Copy, scale=cw[:, ch, 0:1]),
                     r=[P.bank_regs[bk], r_small], w=[rt])
                P.op("dve", lambda e, bk=bk, t=t, ch=ch: e.scalar_tensor_tensor(out=t, in0=P.bank(bk, N)[:, 1:G + 1], scalar=cw[:, ch, 1:2], in1=t,
                                                                                   op0=ALU.mult, op1=ALU.add),
                     r=[P.bank_regs[bk], r_small, rt], w=[rt])
                P.op("dve", lambda e, bk=bk, t=t, ch=ch: e.scalar_tensor_tensor(out=t, in0=P.bank(bk, N)[:, 2:G + 2], scalar=cw[:, ch, 2:3], in1=t,
                                                                                   op0=ALU.mult, op1=ALU.add),
                     r=[P.bank_regs[bk], r_small, rt], w=[rt])
            tgj, tuj = tg[j % 3], tu[j % 3]
            P.op("act", lambda e, tgj=tgj: e.activation(out=tgj, in_=tgj, func=AF.Silu), r=[r_tg[j % 3]], w=[r_tg[j % 3]])
            P.op("dve", lambda e, tgj=tgj, tuj=tuj, j=j: e.tensor_tensor(out=act[:, j, :], in0=tgj, in1=tuj, op=ALU.mult),
                 r=[r_tg[j % 3], r_tu[j % 3]], w=[r_act[j]])
        if g + 1 < ng:
            rms_group(P, C, hbuf[1 - b], r_hbuf[1 - b], N, sqc, ssum, rs, rstd, hn, r_hn, r_tmp, bank=7)
        for oc in range(8):
            bk = 6 + oc % 2
            P.mm_group(
                [lambda e, j=j, bk=bk, oc=oc: e.matmul(P.bank(bk, G), lhsT=wdn[:, j, oc * 128:(oc + 1) * 128], rhs=act[:, j, :],
                                                        start=(j == 0), stop=(j == NFC - 1)) for j in range(NFC)],
                [[r_wdn[j // 2], r_act[j]] for j in range(NFC)], [P.bank_regs[bk]])
            P.op("dve", lambda e, bk=bk, oc=oc, b=b: e.tensor_tensor(out=hbuf[b][:, oc, 2:N], in0=P.bank(bk, G), in1=hbuf[b][:, oc, 2:N], op=ALU.add),
                 r=[P.bank_regs[bk]], w=[r_hbuf[b]])
        P.dma("sp", dstv[:, :, n0:n0 + G], hbuf[b][:, :, 2:N], r=[r_hbuf[b]])
    P.barrier()


NWIN = 4112


def gdn_consts(P, C):
    r = C["r"]
    for name in ("triu", "mus", "negu", "ident"):
        C[name] = P.alloc([128, 128], F32, persist=True)
    P.op("pool", lambda e: e.memset(C["triu"], 1.0), w=[r])
    P.op("pool", lambda e: e.affine_select(out=C["triu"], in_=C["triu"], pattern=[[1, 128]], compare_op=ALU.is_ge, fill=0.0, base=0, channel_multiplier=-1), w=[r])
    P.op("pool", lambda e: e.memset(C["mus"], 1.0), w=[r])
    P.op("pool", lambda e: e.affine_select(out=C["mus"], in_=C["mus"], pattern=[[1, 128]], compare_op=ALU.is_gt, fill=0.0, base=0, channel_multiplier=-1), w=[r])
    P.op("pool", lambda e: e.memset(C["negu"], 0.0), w=[r])
    P.op("pool", lambda e: e.affine_select(out=C["negu"], in_=C["negu"], pattern=[[1, 128]], compare_op=ALU.is_ge, fill=-30000.0, base=0, channel_multiplier=-1), w=[r])
    P.op("pool", lambda e: e.memset(C["ident"], 1.0), w=[r])
    P.op("pool", lambda e: e.affine_select(out=C["ident"], in_=C["ident"], pattern=[[-1, 128]], compare_op=ALU.is_equal, fill=0.0, base=0, channel_multiplier=1), w=[r])


def gdn_proj_phase(P, C, w_in, normT, convT, alogB, dtbB, src, QT, KT, VT, ZT, BT, GT, LP):
    P.reset_arena()
    ng = LP // G
    N = G + 3
    wb = P.alloc([128, 8, NWIN], BF16)
    r_wb = [Reg() for _ in range(8)]
    gain = P.alloc([128, 8], F32)
    cw = P.alloc([128, 24, 4], F32)
    nega = P.alloc([128, 8], F32)
    dtb = P.alloc([128, 8], F32)
    r_small = Reg()
    SW = 2056
    stage = [P.alloc([128, SW], F32) for _ in range(3)]
    r_stage = [Reg() for _ in range(3)]
    hbuf = [P.alloc([128, 8, N], F32) for _ in range(2)]
    r_hbuf = [Reg(), Reg()]
    hn = P.alloc([128, 8, N], BF16)
    r_hn = Reg()
    sqc = [P.alloc([128, N], F32) for _ in range(2)]
    ssum = P.alloc([128, N], F32)
    rs = P.alloc([128, N], F32)
    rstd = P.alloc([128, N], F32)
    r_tmp = [Reg() for _ in range(5)]
    qkb = P.alloc([128, 16, G], F32)
    r_qk = [Reg() for _ in range(16)]
    NT = 3
    tb = [P.alloc([128, G], F32) for _ in range(NT)]
    r_tb = [Reg() for _ in range(NT)]
    sq2 = [P.alloc([128, G], F32) for _ in range(2)]
    r_sq2 = [Reg(), Reg()]
    rr = [P.alloc([128, G], F32) for _ in range(2)]
    r_rr = [Reg(), Reg()]
    sm = [P.alloc([128, 8, 8], F32) for _ in range(2)]
    r_sm = [Reg(), Reg()]
    cr = C["r"]

    P.dma("sp", gain, normT, w=[r_small])
    P.dma("sp", cw, convT, w=[r_small])
    P.dma("sp", nega, alogB, w=[r_small])
    P.dma("sp", dtb, dtbB, w=[r_small])
    P.op("act", lambda e: e.activation(out=nega, in_=nega, func=AF.Exp), r=[r_small], w=[r_small])
    P.op("dve", lambda e: e.tensor_scalar(out=nega, in0=nega, scalar1=-1.0, scalar2=None, op0=ALU.mult), r=[r_small], w=[r_small])
    load_weight_bf16(P, w_in, 8, NWIN, gain, r_small, stage, r_stage, wb, lambda kc, c0: r_wb[kc], SW)

    srcv = src.rearrange("(c p) t -> p c t", p=128)

    def load(g):
        b = g % 2
        n0 = g * G
        if g == 0:
            P.op("pool", lambda e: e.memset(hbuf[b][:, :, 0:3], 0.0), w=[r_hbuf[b]])
            P.dma("sp", hbuf[b][:, :, 3:N], srcv[:, :, 0:G], w=[r_hbuf[b]])
        else:
            P.dma("sp", hbuf[b], srcv[:, :, n0 - 3:n0 + G], w=[r_hbuf[b]])

    def rms(g):
        rms_group(P, C, hbuf[g % 2], r_hbuf[g % 2], N, sqc, ssum, rs, rstd, hn, r_hn, r_tmp, bank=7)

    load(0)
    rms(0)
    ti = 0
    for g in range(ng):
        n0 = g * G
        if g + 1 < ng:
            load(g + 1)
        for ch in range(24):
            bk = ch % 3
            col = ch * 128
            P.mm_group(
                [lambda e, kc=kc, bk=bk, col=col: e.matmul(P.bank(bk, N), lhsT=wb[:, kc, col:col + 128], rhs=hn[:, kc, :],
                                                            start=(kc == 0), stop=(kc == 7)) for kc in range(8)],
                [[r_wb[kc], r_hn] for kc in range(8)], [P.bank_regs[bk]])
            if ch < 16:
                t = qkb[:, ch, :]
                rt = r_qk[ch]
            else:
                t = tb[ti % NT]
                rt = r_tb[ti % NT]
                ti += 1
            P.op("act", lambda e, bk=bk, t=t, ch=ch: e.activation(out=t, in_=P.bank(bk, N)[:, 0:G], func=AF.Copy, scale=cw[:, ch, 0:1]),
                 r=[P.bank_regs[bk], r_small], w=[rt])
            for k in (1, 2, 3):
                P.op("dve", lambda e, bk=bk, t=t, ch=ch, k=k: e.scalar_tensor_tensor(
                    out=t, in0=P.bank(bk, N)[:, k:G + k], scalar=cw[:, ch, k:k + 1], in1=t, op0=ALU.mult, op1=ALU.add),
                    r=[P.bank_regs[bk], r_small, rt], w=[rt])
            P.op("act", lambda e, t=t: e.activation(out=t, in_=t, func=AF.Silu), r=[rt], w=[rt])
            if ch >= 16:
                hh = ch - 16
                P.dma("sp", VT[hh * 128:(hh + 1) * 128, n0:n0 + G], t, r=[rt])
        for hh in range(8):
            bk = hh % 3
            col = 3072 + hh * 128
            P.mm_group(
                [lambda e, kc=kc, bk=bk, col=col: e.matmul(P.bank(bk, G), lhsT=wb[:, kc, col:col + 128], rhs=hn[:, kc, 3:N],
                                                            start=(kc == 0), stop=(kc == 7)) for kc in range(8)],
                [[r_wb[kc], r_hn] for kc in range(8)], [P.bank_regs[bk]])
            t = tb[ti % NT]
            rt = r_tb[ti % NT]
            ti += 1
            P.op("act", lambda e, bk=bk, t=t: e.activation(out=t, in_=P.bank(bk, G), func=AF.Silu), r=[P.bank_regs[bk]], w=[rt])
            P.dma("sp", ZT[hh * 128:(hh + 1) * 128, n0:n0 + G], t, r=[rt])
        for i in range(G // 128):
            bk = 5 + i % 2
            c0 = 3 + i * 128
            P.mm_group(
                [lambda e, kc=kc, bk=bk, c0=c0: e.matmul(P.bank(bk, 16), lhsT=hn[:, kc, c0:c0 + 128], rhs=wb[:, kc, 4096:4112],
                                                          start=(kc == 0), stop=(kc == 7)) for kc in range(8)],
                [[r_wb[kc], r_hn] for kc in range(8)], [P.bank_regs[bk]])
            s_ = sm[i % 2]
            rs_ = r_sm[i % 2]
            ps = P.bank(bk, 16)
            P.op("act", lambda e, ps=ps, s_=s_: e.activation(out=s_[:, 0, :], in_=ps[:, 0:8], func=AF.Exp, scale=-1.0), r=[P.bank_regs[bk]], w=[rs_])
            P.op("dve", lambda e, s_=s_: e.tensor_scalar(out=s_[:, 0, :], in0=s_[:, 0, :], scalar1=1.0, scalar2=None, op0=ALU.add), r=[rs_], w=[rs_])
            P.op("dve", lambda e, s_=s_: e.reciprocal(out=s_[:, 0, :], in_=s_[:, 0, :]), r=[rs_], w=[rs_])
            P.op("dve", lambda e, ps=ps, s_=s_: e.tensor_tensor(out=s_[:, 1, :], in0=ps[:, 8:16], in1=dtb, op=ALU.add), r=[P.bank_regs[bk], r_small], w=[rs_])
            P.op("dve", lambda e, s_=s_: e.tensor_scalar(out=s_[:, 2, :], in0=s_[:, 1, :], scalar1=-1.0, scalar2=None, op0=ALU.mult), r=[rs_], w=[rs_])
            P.op("dve", lambda e, s_=s_: e.tensor_tensor(out=s_[:, 2, :], in0=s_[:, 2, :], in1=s_[:, 1, :], op=ALU.max), r=[rs_], w=[rs_])
            P.op("act", lambda e, s_=s_: e.activation(out=s_[:, 3, :], in_=s_[:, 2, :], func=AF.Exp, scale=-1.0), r=[rs_], w=[rs_])
            P.op("act", lambda e, s_=s_: e.activation(out=s_[:, 3, :], in_=s_[:, 3, :], func=AF.Ln, bias=C["ones_f"][:, 0:1], scale=1.0), r=[rs_, cr], w=[rs_])
            P.op("dve", lambda e, s_=s_: e.tensor_scalar(out=s_[:, 4, :], in0=s_[:, 1, :], scalar1=0.0, scalar2=None, op0=ALU.max), r=[rs_], w=[rs_])
            P.op("dve", lambda e, s_=s_: e.tensor_tensor(out=s_[:, 4, :], in0=s_[:, 4, :], in1=s_[:, 3, :], op=ALU.add), r=[rs_], w=[rs_])
            P.op("dve", lambda e, s_=s_: e.tensor_tensor(out=s_[:, 5, :], in0=s_[:, 4, :], in1=nega, op=ALU.mult), r=[rs_, r_small], w=[rs_])
            r0 = n0 + i * 128
            P.dma("sp", BT[r0:r0 + 128, :], s_[:, 0, :], r=[rs_])
            P.dma("sp", GT[r0:r0 + 128, :], s_[:, 5, :], r=[rs_])
        if g + 1 < ng:
            rms(g + 1)
        for ch in range(16):
            t = qkb[:, ch, :]
            rt = r_qk[ch]
            s2 = sq2[ch % 2]
            P.op("pool", lambda e, t=t, s2=s2: e.tensor_tensor(out=s2, in0=t, in1=t, op=ALU.mult), r=[rt], w=[r_sq2[ch % 2]])
            bk2 = 3 + ch % 2
            P.mm_group([lambda e, bk2=bk2, s2=s2: e.matmul(P.bank(bk2, G), lhsT=C["ones_f"], rhs=s2, start=True, stop=True)],
                       [[cr, r_sq2[ch % 2]]], [P.bank_regs[bk2]])
            r2 = rr[ch % 2]
            P.op("act", lambda e, bk2=bk2, r2=r2: e.activation(out=r2, in_=P.bank(bk2, G), func=AF.Ln, bias=C["eps"], scale=1.0),
                 r=[P.bank_regs[bk2], cr], w=[r_rr[ch % 2]])
            P.op("act", lambda e, r2=r2: e.activation(out=r2, in_=r2, func=AF.Exp, scale=-0.5), r=[r_rr[ch % 2]], w=[r_rr[ch % 2]])
            sc = (128.0 ** -0.5) if ch < 8 else 1.0
            P.op("dve", lambda e, t=t, r2=r2, sc=sc: e.scalar_tensor_tensor(out=t, in0=t, scalar=sc, in1=r2, op0=ALU.mult, op1=ALU.mult),
                 r=[rt, r_rr[ch % 2]], w=[rt])
            dstT = (QT, KT)[ch // 8]
            hh = ch % 8
            P.dma("sp", dstT[hh * 128:(hh + 1) * 128, n0:n0 + G], t, r=[rt])
    P.barrier()


def gdn_scan_phase(P, C, w_o, onormB, QT, KT, VT, ZT, BT, GT, src, dst, LP):
    P.reset_arena()
    nt = LP // 128
    cr = C["r"]
    wo = P.alloc([128, 8, 1024], BF16)
    r_wo = Reg()
    ogain = P.alloc([128, 1], F32)
    identb = P.alloc([128, 128], BF16)
    r_small = Reg()
    stage = [P.alloc([128, 1024], F32) for _ in range(2)]
    r_stage = [Reg(), Reg()]
    P.dma("sp", ogain, onormB, w=[r_small])
    P.op("pool", lambda e: e.tensor_copy(out=identb, in_=C["ident"]), r=[cr], w=[r_small])
    load_weight_bf16(P, w_o, 8, 1024, None, None, stage, r_stage, wo, r_wo, 1024)

    def big(dt=F32):
        return P.alloc([128, 8, 128], dt)

    IN = [dict(k=big(), q=big(), v=big(), z=big(), h=big(), b=P.alloc([128, 8], F32), g=P.alloc([128, 8], F32), r=Reg()) for _ in range(2)]
    S = big(); r_S = Reg()
    f32names = ["Dg", "DT", "DTs", "egc", "Y", "u", "vtmp", "sq", "og", "P0", "Pt0", "P1", "Pt1"]
    bfnames = ["Ybf", "Sbf", "kbf", "qbf", "kg", "kdec", "vtm", "wT", "qg", "IT", "vnew"]
    T = {n: big() for n in f32names}
    T.update({n: big(BF16) for n in bfnames})
    R = {n: Reg(n) for n in f32names + bfnames}
    ogb = P.alloc([128, 8, 128], BF16); r_ogb = Reg()
    sm = {n: P.alloc([128, 8], F32) for n in ("gc", "egc", "gl", "egl", "fd")}
    r_sm = {n: Reg() for n in sm}
    rs = P.alloc([128, 8, 128], F32); r_rs = Reg()

    P.op("pool", lambda e: e.memset(S, 0.0), w=[r_S])
    P.op("pool", lambda e: e.memset(T["Sbf"], 0.0), w=[R["Sbf"]])

    def v3(dram):
        return dram.rearrange("(h d) t -> d h t", d=128)

    def load(t):
        I = IN[t % 2]
        c0 = t * 128
        for key, dr in (("k", KT), ("q", QT), ("v", VT), ("z", ZT), ("h", src)):
            P.dma("sp", I[key], v3(dr)[:, :, c0:c0 + 128], w=[I["r"]])
        P.dma("sp", I["b"], BT[c0:c0 + 128, :], w=[I["r"]])
        P.dma("sp", I["g"], GT[c0:c0 + 128, :], w=[I["r"]])

    slot_i = [0]

    def slot():
        s = slot_i[0] % 3
        slot_i[0] += 1
        ap = P.psum[:, s * 1024:(s + 1) * 1024].rearrange("p (h c) -> p h c", h=8)
        return ap, [P.bank_regs[2 * s], P.bank_regs[2 * s + 1]], s

    def bc_row(m):
        return m.unsqueeze(1).broadcast_to([128, 8, 128])

    def bc_col(x):
        return x.unsqueeze(2).broadcast_to([128, 8, 128])

    def mm8(ps, regs, lhs, rhs, rl, rr_, transpose=None):
        fns = []
        rls = []
        for h in range(8):
            if transpose is not None:
                fn = (lambda e, h=h: e.transpose(ps[:, h, :], lhs[:, h, :], transpose))
            else:
                fn = (lambda e, h=h: e.matmul(ps[:, h, :], lhsT=lhs[:, h, :], rhs=rhs[:, h, :], start=True, stop=True))
            fns.append(fn)
            rls.append(list(rl) + list(rr_))
        P.mm_group(fns, rls, regs)

    load(0)
    for t in range(nt):
        I = IN[t % 2]
        rI = I["r"]
        c0 = t * 128
        if t + 1 < nt:
            load(t + 1)
        P.op("pool", lambda e, I=I: e.tensor_copy(out=T["kbf"], in_=I["k"]), r=[rI], w=[R["kbf"]])
        P.op("pool", lambda e, I=I: e.tensor_copy(out=T["qbf"], in_=I["q"]), r=[rI], w=[R["qbf"]])
        P.op("dve", lambda e, I=I: e.tensor_tensor(out=T["Dg"], in0=bc_row(C["triu"]), in1=bc_col(I["g"]), op=ALU.mult), r=[cr, rI], w=[R["Dg"]])
        gcB, rg_gcB, sg = slot()
        gcBf = P.psum[:, sg * 1024:(sg + 1) * 1024]
        Dgf = T["Dg"].rearrange("p h c -> p (h c)")
        P.mm_group([lambda e: e.matmul(gcBf[:, 0:512], lhsT=C["ones_f"], rhs=Dgf[:, 0:512], start=True, stop=True),
                    lambda e: e.matmul(gcBf[:, 512:1024], lhsT=C["ones_f"], rhs=Dgf[:, 512:1024], start=True, stop=True)],
                   [[cr, R["Dg"]], [cr, R["Dg"]]], rg_gcB)
        P.mm_group([lambda e, I=I: e.matmul(P.bank(6, 8), lhsT=C["triu"], rhs=I["g"], start=True, stop=True)], [[cr, rI]], [P.bank_regs[6]])
        P.op("dve", lambda e: e.tensor_copy(out=sm["gc"], in_=P.bank(6, 8)), r=[P.bank_regs[6]], w=[r_sm["gc"]])
        P.op("dve", lambda e: e.tensor_copy(out=sm["gl"], in_=gcB[:, :, 127]), r=rg_gcB, w=[r_sm["gl"]])
        P.op("act", lambda e: e.activation(out=sm["egc"], in_=sm["gc"], func=AF.Exp), r=[r_sm["gc"]], w=[r_sm["egc"]])
        P.op("act", lambda e: e.activation(out=sm["egl"], in_=sm["gl"], func=AF.Exp), r=[r_sm["gl"]], w=[r_sm["egl"]])
        P.op("dve", lambda e: e.tensor_tensor(out=sm["fd"], in0=sm["gl"], in1=sm["gc"], op=ALU.subtract), r=[r_sm["gl"], r_sm["gc"]], w=[r_sm["fd"]])
        P.op("act", lambda e: e.activation(out=sm["fd"], in_=sm["fd"], func=AF.Exp), r=[r_sm["fd"]], w=[r_sm["fd"]])
        P.op("dve", lambda e: e.tensor_tensor(out=T["DT"], in0=gcB, in1=bc_col(sm["gc"]), op=ALU.subtract), r=rg_gcB + [r_sm["gc"]], w=[R["DT"]])
        P.op("dve", lambda e: e.tensor_tensor(out=T["DT"], in0=T["DT"], in1=bc_row(C["negu"]), op=ALU.add), r=[cr, R["DT"]], w=[R["DT"]])
        P.op("act", lambda e: e.activation(out=T["DT"], in_=T["DT"], func=AF.Exp), r=[R["DT"]], w=[R["DT"]])
        P.op("act", lambda e: e.activation(out=T["egc"], in_=gcB, func=AF.Exp), r=rg_gcB, w=[R["egc"]])
        P.op("dve", lambda e: e.tensor_tensor(out=T["DTs"], in0=T["DT"], in1=bc_row(C["mus"]), op=ALU.mult), r=[cr, R["DT"]], w=[R["DTs"]])
        KK, rg_KK, _ = slot()
        mm8(KK, rg_KK, T["kbf"], T["kbf"], [R["kbf"]], [])
        ITp, rg_IT, _ = slot()
        mm8(ITp, rg_IT, T["kbf"], T["qbf"], [R["kbf"]], [R["qbf"]])
        for h in range(8):
            P.op("dve", lambda e, h=h, I=I: e.scalar_tensor_tensor(out=T["P0"][:, h, :], in0=KK[:, h, :], scalar=I["b"][:, h:h + 1], in1=T["DTs"][:, h, :],
                                                                      op0=ALU.mult, op1=ALU.mult), r=rg_KK + [rI, R["DTs"]], w=[R["P0"]])
        P.op("dve", lambda e: e.tensor_tensor(out=T["IT"], in0=ITp, in1=T["DT"], op=ALU.mult), r=rg_IT + [R["DT"]], w=[R["IT"]])
        P.op("pool", lambda e, I=I: e.tensor_tensor(out=T["qg"], in0=I["q"], in1=T["egc"], op=ALU.mult), r=[rI, R["egc"]], w=[R["qg"]])
        psb, rg, st_ = slot()
        mm8(psb, rg, T["P0"], None, [R["P0"], cr], [], transpose=C["ident"])
        P.op("act", lambda e, psb=psb: e.activation(out=T["Pt0"], in_=psb, func=AF.Copy), r=rg, w=[R["Pt0"]])
        P.op("dve", lambda e: e.tensor_tensor(out=T["Y"], in0=bc_row(C["ident"]), in1=T["P0"], op=ALU.subtract), r=[cr, R["P0"]], w=[R["Y"]])
        cur = 0
        for k in range(1, 7):
            Pc, Ptc = T[f"P{cur}"], T[f"Pt{cur}"]
            rPc, rPtc = R[f"P{cur}"], R[f"Pt{cur}"]
            nx = 1 - cur
            Pn, Ptn = T[f"P{nx}"], T[f"Pt{nx}"]
            rPn, rPtn = R[f"P{nx}"], R[f"Pt{nx}"]
            ps2, rg2, _ = slot()
            mm8(ps2, rg2, Pc, Ptc, [rPc], [rPtc])
            if k < 6:
                ps1, rg1, _ = slot()
                mm8(ps1, rg1, Ptc, Pc, [rPtc], [rPc])
            P.op("dve", lambda e, ps2=ps2, Ptn=Ptn: e.tensor_copy(out=Ptn, in_=ps2), r=rg2, w=[rPtn])
            if k < 6:
                P.op("act", lambda e, ps1=ps1, Pn=Pn: e.activation(out=Pn, in_=ps1, func=AF.Copy), r=rg1, w=[rPn])
            ps3, rg3, _ = slot()
            mm8(ps3, rg3, Ptn, T["Y"], [rPtn], [R["Y"]])
            P.op("dve", lambda e, ps3=ps3: e.tensor_tensor(out=T["Y"], in0=ps3, in1=T["Y"], op=ALU.add), r=rg3, w=[R["Y"]])
            cur = nx
        P.op("act", lambda e: e.activation(out=T["Ybf"], in_=T["Y"], func=AF.Copy), r=[R["Y"]], w=[R["Ybf"]])
        ps, rg, _ = slot()
        mm8(ps, rg, I["k"], None, [rI, cr], [], transpose=C["ident"])
        P.op("dve", lambda e, ps=ps: e.tensor_tensor(out=T["kg"], in0=ps, in1=bc_col(sm["egc"]), op=ALU.mult), r=rg + [r_sm["egc"]], w=[R["kg"]])
        P.op("dve", lambda e, ps=ps: e.tensor_tensor(out=T["kdec"], in0=ps, in1=bc_col(sm["fd"]), op=ALU.mult), r=rg + [r_sm["fd"]], w=[R["kdec"]])
        ps, rg, _ = slot()
        mm8(ps, rg, I["v"], None, [rI, cr], [], transpose=C["ident"])
        P.op("act", lambda e, ps=ps: e.activation(out=T["vtm"], in_=ps, func=AF.Copy), r=rg, w=[R["vtm"]])
        ps, rg, _ = slot()
        mm8(ps, rg, T["Ybf"], T["vtm"], [R["Ybf"]], [R["vtm"]])
        P.op("act", lambda e, ps=ps: e.activation(out=T["u"], in_=ps, func=AF.Copy), r=rg, w=[R["u"]])
        ps, rg, _ = slot()
        mm8(ps, rg, T["kg"], T["Ybf"], [R["kg"]], [R["Ybf"]])
        P.op("act", lambda e, ps=ps: e.activation(out=T["wT"], in_=ps, func=AF.Copy), r=rg, w=[R["wT"]])
        ps, rg, _ = slot()
        mm8(ps, rg, T["wT"], T["Sbf"], [R["wT"]], [R["Sbf"]])
        P.op("dve", lambda e, ps=ps: e.tensor_tensor(out=T["vtmp"], in0=T["u"], in1=ps, op=ALU.subtract), r=rg + [R["u"]], w=[R["vtmp"]])
        P.op("dve", lambda e, I=I: e.tensor_tensor(out=T["vnew"], in0=T["vtmp"], in1=bc_col(I["b"]), op=ALU.mult), r=[rI, R["vtmp"]], w=[R["vnew"]])
        pso, rgo, _ = slot()
        fns = []
        rls = []
        for h in range(8):
            fns.append(lambda e, h=h: e.matmul(pso[:, h, :], lhsT=T["Sbf"][:, h, :], rhs=T["qg"][:, h, :], start=True, stop=False))
            rls.append([R["Sbf"], R["qg"]])
            fns.append(lambda e, h=h: e.matmul(pso[:, h, :], lhsT=T["vnew"][:, h, :], rhs=T["IT"][:, h, :], start=False, stop=True))
            rls.append([R["vnew"], R["IT"]])
        P.mm_group(fns, rls, rgo)
        ps, rg, _ = slot()
        mm8(ps, rg, T["kdec"], T["vnew"], [R["kdec"]], [R["vnew"]])
        P.op("dve", lambda e: e.tensor_tensor(out=S, in0=S, in1=bc_col(sm["egl"]), op=ALU.mult), r=[r_sm["egl"]], w=[r_S])
        P.op("dve", lambda e, ps=ps: e.tensor_tensor(out=S, in0=S, in1=ps, op=ALU.add), r=rg, w=[r_S])
        P.op("act", lambda e: e.activation(out=T["Sbf"], in_=S, func=AF.Copy), r=[r_S], w=[R["Sbf"]])
        P.op("act", lambda e: e.activation(out=T["sq"], in_=pso, func=AF.Square), r=rgo, w=[R["sq"]])
        pst, rgt, st_ = slot()
        pstf = P.psum[:, st_ * 1024:(st_ + 1) * 1024]
        sqf = T["sq"].rearrange("p h c -> p (h c)")
        P.mm_group([lambda e: e.matmul(pstf[:, 0:512], lhsT=C["ones_f"], rhs=sqf[:, 0:512], start=True, stop=True),
                    lambda e: e.matmul(pstf[:, 512:1024], lhsT=C["ones_f"], rhs=sqf[:, 512:1024], start=True, stop=True)],
                   [[cr, R["sq"]], [cr, R["sq"]]], rgt)
        P.op("act", lambda e: e.activation(out=rs, in_=pst, func=AF.Ln, bias=C["eps"], scale=1.0 / 128), r=rgt + [cr], w=[r_rs])
        P.op("act", lambda e: e.activation(out=rs, in_=rs, func=AF.Exp, scale=-0.5), r=[r_rs], w=[r_rs])
        P.op("dve", lambda e: e.tensor_tensor(out=T["og"], in0=pso, in1=rs, op=ALU.mult), r=rgo + [r_rs], w=[R["og"]])
        P.op("dve", lambda e, I=I: e.scalar_tensor_tensor(out=ogb, in0=T["og"], scalar=ogain[:, 0:1], in1=I["z"], op0=ALU.mult, op1=ALU.mult),
             r=[R["og"], r_small, rI], w=[r_ogb])
        psp, rgp, _ = slot()
        fns = []
        rls = []
        for oc in range(8):
            for h in range(8):
                fns.append(lambda e, oc=oc, h=h: e.matmul(psp[:, oc, :], lhsT=wo[:, h, oc * 128:(oc + 1) * 128], rhs=ogb[:, h, :], start=(h == 0), stop=(h == 7)))
                rls.append([r_wo, r_ogb])
        P.mm_group(fns, rls, rgp)
        P.op("dve", lambda e, I=I: e.tensor_tensor(out=I["h"], in0=psp, in1=I["h"], op=ALU.add), r=rgp, w=[rI])
        P.dma("sp", v3(dst)[:, :, c0:c0 + 128], I["h"], r=[rI])
    P.barrier()


QB = G


def attn_consts(P, C):
    r = C["r"]
    C["kmx"] = P.alloc([128, 16], F32, persist=True)
    C["negK"] = P.alloc([128, 16], F32, persist=True)
    C["triu_b"] = P.alloc([128, 128], BF16, persist=True)
    C["r_kmx"] = Reg("kmx")
    P.op("pool", lambda e: e.memset(C["kmx"], 0.0), w=[C["r_kmx"]])
    P.op("pool", lambda e: e.tensor_copy(out=C["triu_b"], in_=C["triu"]), r=[r], w=[r])


def load_weight_bf16(P, w_dram, nrow_chunks, ncols, gain, r_small, stage, r_stage, dst, r_dst, SW):
    si = 0
    ns = len(stage)
    for kc in range(nrow_chunks):
        for c0 in range(0, ncols, SW):
            w_ = min(SW, ncols - c0)
            s = si % ns
            q = "sp"
            eng = ("act", "dve")[si % 2]
            si += 1
            P.dma(q, stage[s][:, 0:w_], w_dram[kc * 128:(kc + 1) * 128, c0:c0 + w_], w=[r_stage[s]])
            rd = r_dst(kc, c0) if callable(r_dst) else r_dst
            o_ = dst[:, kc, c0:c0 + w_]
            i_ = stage[s][:, 0:w_]
            if eng == "act":
                if gain is not None:
                    P.op("act", lambda e, o_=o_, i_=i_, kc=kc: e.activation(out=o_, in_=i_, func=AF.Copy, scale=gain[:, kc:kc + 1]),
                         r=[r_stage[s], r_small], w=[rd])
                else:
                    P.op("act", lambda e, o_=o_, i_=i_: e.activation(out=o_, in_=i_, func=AF.Copy), r=[r_stage[s]], w=[rd])
            else:
                if gain is not None:
                    P.op("dve", lambda e, o_=o_, i_=i_, kc=kc: e.tensor_scalar(out=o_, in0=i_, scalar1=gain[:, kc:kc + 1], scalar2=None, op0=ALU.mult),
                         r=[r_stage[s], r_small], w=[rd])
                else:
                    P.op("dve", lambda e, o_=o_, i_=i_: e.tensor_copy(out=o_, in_=i_), r=[r_stage[s]], w=[rd])


def attn_proj_phase(P, C, src, normT, w_dram, is_kv, XA, VA, LP):
    P.reset_arena()
    ng = LP // G
    N = G
    ncols = 2048 if is_kv else 1024
    wb = P.alloc([128, 8, ncols], BF16)
    r_wb = Reg()
    gain = P.alloc([128, 8], F32)
    r_small = Reg()
    SW = 1024
    stage = [P.alloc([128, SW], F32) for _ in range(2)]
    r_stage = [Reg(), Reg()]
    hbuf = [P.alloc([128, 8, N], F32) for _ in range(2)]
    r_hbuf = [Reg(), Reg()]
    hn = P.alloc([128, 8, N], BF16)
    r_hn = Reg()
    sqc = [P.alloc([128, N], F32) for _ in range(2)]
    ssum = P.alloc([128, N], F32)
    rs = P.alloc([128, N], F32)
    rstd = P.alloc([128, N], F32)
    r_tmp = [Reg() for _ in range(5)]
    xa = [P.alloc([128, G], BF16) for _ in range(3)]
    r_xa = [Reg() for _ in range(3)]
    sqk = [P.alloc([128, G], F32) for _ in range(2)]
    r_sqk = [Reg(), Reg()]
    r3 = [P.alloc([128, G], F32) for _ in range(2)]
    r_r3 = [Reg(), Reg()]
    mx = P.alloc([128, 2], F32)
    r_mx = Reg()
    cr = C["r"]
    P.dma("sp", gain, normT, w=[r_small])
    load_weight_bf16(P, w_dram, 8, ncols, gain, r_small, stage, r_stage, wb, r_wb, SW)
    if is_kv:
        vaug = [P.alloc([128, 8, 129], BF16) for _ in range(2)]
        r_vaug = [Reg(), Reg()]
        for b in range(2):
            P.op("pool", lambda e, b=b: e.memset(vaug[b][:, :, 128:129], 1.0), w=[r_vaug[b]])
        for b in range(3):
            P.op("pool", lambda e, b=b: e.memset(xa[b][64:67, :], 1.0), w=[r_xa[b]])
    else:
        A = P.alloc([128, 3, 128], F32)
        Bv = P.alloc([128, 3, 128], F32)
        tpl = P.alloc([128, 8, G], F32)
        r_tpl = Reg()
        P.op("pool", lambda e: e.iota(A, pattern=[[0, 3], [1, 128]], base=0, channel_multiplier=0, allow_small_or_imprecise_dtypes=True), w=[r_tpl])
        P.op("pool", lambda e: e.iota(Bv, pattern=[[128, 3], [0, 128]], base=0, channel_multiplier=0, allow_small_or_imprecise_dtypes=True), w=[r_tpl])
        Af = A.rearrange("p a b -> p (a b)")
        Bf = Bv.rearrange("p a b -> p (a b)")
        P.op("dve", lambda e: e.tensor_scalar(out=Af, in0=Af, scalar1=C["ident"][:, 65:66], scalar2=None, op0=ALU.mult), r=[cr, r_tpl], w=[r_tpl])
        P.op("dve", lambda e: e.tensor_scalar(out=Bf, in0=Bf, scalar1=C["ident"][:, 66:67], scalar2=None, op0=ALU.mult), r=[cr, r_tpl], w=[r_tpl])
        P.op("dve", lambda e: e.tensor_tensor(out=Af, in0=Af, in1=Bf, op=ALU.add), r=[r_tpl], w=[r_tpl])
        for h in range(8):
            P.op("dve", lambda e, h=h: e.tensor_scalar(out=tpl[:, h, :], in0=Af, scalar1=-(2.0 ** -(h + 1)), scalar2=None, op0=ALU.mult), r=[r_tpl], w=[r_tpl])
        P.op("act", lambda e: e.activation(out=C["negK"][64:65, :], in_=C["kmx"][64:65, :], func=AF.Ln, bias=C["eps"][64:65, :], scale=1.0), r=[C["r_kmx"], cr], w=[C["r_kmx"]])
        P.op("act", lambda e: e.activation(out=C["negK"][64:65, :], in_=C["negK"][64:65, :], func=AF.Exp, scale=0.5), r=[C["r_kmx"]], w=[C["r_kmx"]])
        P.op("dve", lambda e: e.tensor_scalar(out=C["negK"][64:65, :], in0=C["negK"][64:65, :], scalar1=-1.0, scalar2=None, op0=ALU.mult), r=[C["r_kmx"]], w=[C["r_kmx"]])

    srcv = src.rearrange("(c p) t -> p c t", p=128)

    def load(g):
        b = g % 2
        P.dma("sp", hbuf[b], srcv[:, :, g * G:(g + 1) * G], w=[r_hbuf[b]])

    load(0)
    xi = 0
    for g in range(ng):
        b = g % 2
        n0 = g * G
        if g + 1 < ng:
            load(g + 1)
        rms_group(P, C, hbuf[b], r_hbuf[b], N, sqc, ssum, rs, rstd, hn, r_hn, r_tmp, bank=7)
        for hi in range(16):
            bk = hi % 3
            col = hi * 64
            ps = P.bank(bk, G, parts=64)
            P.mm_group(
                [lambda e, kc=kc, ps=ps, col=col: e.matmul(ps, lhsT=wb[:, kc, col:col + 64], rhs=hn[:, kc, :], start=(kc == 0), stop=(kc == 7)) for kc in range(8)],
                [[r_wb, r_hn] for kc in range(8)], [P.bank_regs[bk]])
            x_ = xa[xi % 3]
            rx = r_xa[xi % 3]
            xi += 1
            sq_ = sqk[hi % 2]
            rsq = r_sqk[hi % 2]
            sc = 1.0 if is_kv else 0.125
            P.op("act", lambda e, ps=ps, x_=x_, sc=sc: e.activation(out=x_[0:64, :], in_=ps, func=AF.Copy, scale=sc), r=[P.bank_regs[bk]], w=[rx])
            P.op("act", lambda e, ps=ps, sq_=sq_, sc=sc: e.activation(out=sq_[0:64, :], in_=ps, func=AF.Square, scale=sc), r=[P.bank_regs[bk]], w=[rsq])
            bk2 = 3 + hi % 2
            ps2 = P.bank(bk2, G, parts=65)
            P.mm_group([lambda e, ps2=ps2, sq_=sq_: e.matmul(ps2, lhsT=C["ones_f"][0:64, 0:65], rhs=sq_[0:64, :], start=True, stop=True)],
                       [[cr, rsq]], [P.bank_regs[bk2]])
            if is_kv:
                P.op("dve", lambda e, ps2=ps2, hi=hi: e.reduce_max(out=mx[64:65, 0:1], in_=ps2[64:65, :], axis=AX.X), r=[P.bank_regs[bk2]], w=[r_mx])
                P.op("dve", lambda e, hi=hi: e.tensor_tensor(out=C["kmx"][64:65, hi:hi + 1], in0=C["kmx"][64:65, hi:hi + 1], in1=mx[64:65, 0:1], op=ALU.max),
                     r=[r_mx], w=[C["r_kmx"]])
            else:
                rr_ = r3[hi % 2]
                rr3 = r_r3[hi % 2]
                h = hi // 2
                P.op("dve", lambda e, rr_=rr_, h=h: e.tensor_copy(out=rr_[64:67, :], in_=tpl[64:67, h, :]), r=[r_tpl], w=[rr3])
                P.op("act", lambda e, rr_=rr_, ps2=ps2: e.activation(out=rr_[64:65, :], in_=ps2[64:65, :], func=AF.Ln, bias=C["eps"][64:65, :], scale=1.0), r=[P.bank_regs[bk2], cr], w=[rr3])
                P.op("act", lambda e, rr_=rr_: e.activation(out=rr_[64:65, :], in_=rr_[64:65, :], func=AF.Exp, scale=0.5), r=[rr3], w=[rr3])
                P.op("dve", lambda e, rr_=rr_, hi=hi: e.tensor_scalar(out=rr_[64:65, :], in0=rr_[64:65, :], scalar1=C["negK"][64:65, hi:hi + 1], scalar2=None, op0=ALU.mult),
                     r=[C["r_kmx"], rr3], w=[rr3])
                P.op("dve", lambda e, rr_=rr_, x_=x_: e.tensor_copy(out=x_[64:67, :], in_=rr_[64:67, :]), r=[rr3], w=[rx])
            P.dma("sp", XA[hi, :, n0:n0 + G], x_[0:67, :], r=[rx])
        if is_kv:
            for i in range(G // 128):
                va = vaug[i % 2]
                rva = r_vaug[i % 2]
                for half in range(2):
                    bk = 5 + half
                    P.mm_group(
                        [lambda e, kc=kc, bk=bk, i=i, half=half: e.matmul(P.bank(bk, 512), lhsT=hn[:, kc, i * 128:(i + 1) * 128],
                                                                           rhs=wb[:, kc, 1024 + half * 512:1024 + (half + 1) * 512],
                                                                           start=(kc == 0), stop=(kc == 7)) for kc in range(8)],
                        [[r_wb, r_hn] for kc in range(8)], [P.bank_regs[bk]])
                    P.op("act", lambda e, bk=bk, va=va, half=half: e.activation(
                        out=va[:, half * 4:(half + 1) * 4, 0:128], in_=P.bank(bk, 512).rearrange("p (h e) -> p h e", h=4), func=AF.Copy),
                        r=[P.bank_regs[bk]], w=[rva])
                r0 = n0 + i * 128
                P.dma("sp", VA[r0:r0 + 128, :, :], va, r=[rva])
    P.barrier()


def attn_core_phase(P, C, QA, KA, VA, OG, lams, lam_init, sublnB, LP):
    P.reset_arena()
    nt = LP // 128
    nqb = LP // QB
    cr = C["r"]
    ka = [[P.alloc([128, LP], BF16) for _ in range(2)] for _ in range(2)]
    qa = [[P.alloc([128, LP], BF16) for _ in range(2)] for _ in range(2)]
    va = [P.alloc([128, nt, 129], BF16) for _ in range(2)]
    r_in = [Reg(), Reg()]
    BI = P.alloc([128, 33], F32)
    biasT = P.alloc([128, 8, 33], F32)
    r_bias = Reg()
    lam = P.alloc([128, 1], F32)
    subln = P.alloc([128, 128], F32)
    r_small = Reg()
    LA = 3
    NSB = 4
    SB0 = 3
    NPT = LA + 3
    pt = [P.alloc([128, QB], BF16) for _ in range(NPT)]
    r_pt = [Reg() for _ in range(NPT)]
    accs = [P.alloc([128, 2, 3, 129], F32) for _ in range(2)]
    r_accs = [[[Reg() for _ in range(3)] for _ in range(2)] for _ in range(2)]
    NO = 6
    o_ = [P.alloc([128, 128], F32) for _ in range(NO)]
    r_o = [Reg() for _ in range(NO)]
    sm = [P.alloc([128, 8], F32) for _ in range(NO)]
    r_sm = [Reg() for _ in range(NO)]
    junk = P.alloc([128, 128], F32)
    r_junk = Reg()
    NOG = 18
    ogT = [P.alloc([128, 128], BF16) for _ in range(NOG)]
    r_ogT = [Reg() for _ in range(NOG)]

    lv = [P.alloc([128, 64], F32) for _ in range(4)]
    l2 = P.alloc([128, 2], F32)
    for k_ in range(4):
        P.dma("sp", lv[k_], lams[k_], w=[r_small])
    for k_ in range(2):
        P.op("dve", lambda e, k_=k_: e.tensor_tensor(out=lv[2 * k_], in0=lv[2 * k_], in1=lv[2 * k_ + 1], op=ALU.mult), r=[r_small], w=[r_small])
        P.op("dve", lambda e, k_=k_: e.reduce_sum(out=l2[:, k_:k_ + 1], in_=lv[2 * k_], axis=AX.X), r=[r_small], w=[r_small])
    P.op("act", lambda e: e.activation(out=l2, in_=l2, func=AF.Exp), r=[r_small], w=[r_small])
    P.op("dve", lambda e: e.tensor_tensor(out=lam, in0=l2[:, 0:1], in1=l2[:, 1:2], op=ALU.subtract), r=[r_small], w=[r_small])
    P.op("dve", lambda e: e.tensor_scalar(out=lam, in0=lam, scalar1=float(lam_init), scalar2=None, op0=ALU.add), r=[r_small], w=[r_small])
    P.dma("sp", subln, sublnB, w=[r_small])
    P.op("dve", lambda e: e.tensor_scalar(out=subln, in0=subln, scalar1=float(1.0 - lam_init), scalar2=None, op0=ALU.mult), r=[r_small], w=[r_small])
    P.op("pool", lambda e: e.iota(BI, pattern=[[128, 33]], base=-30 * 128, channel_multiplier=1, allow_small_or_imprecise_dtypes=True), w=[r_bias])
    for h in range(8):
        P.op("dve", lambda e, h=h: e.tensor_scalar(out=biasT[:, h, :], in0=BI, scalar1=2.0 ** -(h + 1), scalar2=None, op0=ALU.mult), r=[r_bias], w=[r_bias])

    def load_parts(h):
        b = h % 2
        parts = []
        for i in range(2):
            parts.append(lambda b=b, i=i, h=h: P.dma("sp", ka[b][i][0:67, :], KA[2 * h + i, :, :], w=[r_in[b]]))
            parts.append(lambda b=b, i=i, h=h: P.dma("sp", qa[b][i][0:67, :], QA[2 * h + i, :, :], w=[r_in[b]]))
        vav = VA[:, h, :].rearrange("(t p) e -> p t e", p=128)
        for t0_ in range(0, nt, 11):
            t1_ = min(nt, t0_ + 11)
            parts.append(lambda b=b, t0_=t0_, t1_=t1_, vav=vav: P.dma("sp", va[b][:, t0_:t1_, :], vav[:, t0_:t1_, :], w=[r_in[b]]))
        return parts

    def load(h):
        for f in load_parts(h):
            f()

    pending_loads = []

    steps = []
    for h in range(8):
        for qb in range(nqb):
            for i in range(2):
                nkt = 3 * qb + 3
                for kt in range(nkt):
                    steps.append((h, qb, i, kt, nkt))
    cnt = dict(s=0, p=0, o=0)
    S_info = {}

    def emit_S(idx):
        h, qb, i, kt, nkt = steps[idx]
        b = h % 2
        q0 = qb * QB
        j = max(0, kt - 3 * qb)
        off = j * 128
        w_ = QB - off
        sb = SB0 + (cnt["s"] % NSB)
        cnt["s"] += 1
        ps = P.bank(sb, w_)
        P.mm_group([lambda e, ps=ps, kt=kt, off=off, b=b, i=i, q0=q0: e.matmul(
            ps, lhsT=ka[b][i][0:67, kt * 128:(kt + 1) * 128], rhs=qa[b][i][0:67, q0 + off:q0 + QB], start=True, stop=True)],
            [[r_in[b]]], [P.bank_regs[sb]])
        S_info[idx] = (ps, sb, j, off, w_)

    deferred = []

    def tick():
        for d in deferred:
            d[0] -= 1
        while deferred and deferred[0][0] <= 0:
            deferred.pop(0)[1]()

    load(0)
    s_emitted = 0
    nsteps = len(steps)
    for idx in range(nsteps):
        h, qb, i, kt, nkt = steps[idx]
        b = h % 2
        rI = r_in[b]
        q0 = qb * QB
        if qb == 0 and i == 0 and kt == 0 and h + 1 < 8:
            pending_loads.extend(load_parts(h + 1))
        if pending_loads and (idx % 16 == 0):
            pending_loads.pop(0)()
        while s_emitted < min(idx + LA + 1, nsteps):
            if steps[s_emitted][0] != h:
                while pending_loads:
                    pending_loads.pop(0)()
            emit_S(s_emitted)
            s_emitted += 1
        ps, sb, j, off, w_ = S_info.pop(idx)
        p_ = pt[cnt["p"] % NPT]
        rp = r_pt[cnt["p"] % NPT]
        cnt["p"] += 1
        dlt = kt - 3 * qb + 30
        P.op("act", lambda e, ps=ps, p_=p_, w_=w_, h=h, dlt=dlt: e.activation(out=p_[:, 0:w_], in_=ps, func=AF.Exp, bias=biasT[:, h, dlt:dlt + 1], scale=1.0),
             r=[P.bank_regs[sb], r_bias], w=[rp])
        if kt >= 3 * qb:
            P.op("pool", lambda e, p_=p_: e.tensor_tensor(out=p_[:, 0:128], in0=p_[:, 0:128], in1=C["triu_b"], op=ALU.mult), r=[rp, cr], w=[rp])
        ab_ = qb % 2
        for jj in range(j, 3):
            abk = jj
            acc = P.bank(abk, 129)
            c_ = jj * 128 - off
            first = (kt == 0)
            last = (kt == 3 * qb + jj)
            P.mm_group([lambda e, acc=acc, p_=p_, c_=c_, kt=kt, first=first, last=last, b=b: e.matmul(
                acc, lhsT=p_[:, c_:c_ + 128], rhs=va[b][:, kt, :], start=first, stop=last)],
                [[rp, rI]], [P.bank_regs[abk]])
            if last:
                P.op("dve", lambda e, acc=acc, ab_=ab_, i=i, jj=jj: e.tensor_copy(out=accs[ab_][:, i, jj, :], in_=acc),
                     r=[P.bank_regs[abk]], w=[r_accs[ab_][i][jj]])
        tick()
        if i == 1 and kt == nkt - 1:
            for jj in range(3):
                a0 = accs[ab_][:, 0, jj, :]
                a1 = accs[ab_][:, 1, jj, :]
                ra0, ra1 = r_accs[ab_][0][jj], r_accs[ab_][1][jj]
                oi = cnt["o"]
                cnt["o"] += 1
                s_ = sm[oi % NO]
                rs_ = r_sm[oi % NO]
                o = o_[oi % NO]
                ro = r_o[oi % NO]
                og = ogT[oi % NOG]
                rog = r_ogT[oi % NOG]
                P.op("dve", lambda e, s_=s_, a0=a0: e.reciprocal(out=s_[:, 0:1], in_=a0[:, 128:129]), r=[ra0], w=[rs_])
                P.op("dve", lambda e, s_=s_, a1=a1: e.reciprocal(out=s_[:, 1:2], in_=a1[:, 128:129]), r=[ra1], w=[rs_])
                P.op("dve", lambda e, s_=s_: e.scalar_tensor_tensor(out=s_[:, 2:3], in0=s_[:, 1:2], scalar=-1.0, in1=lam, op0=ALU.mult, op1=ALU.mult), r=[rs_, r_small], w=[rs_])
                P.op("dve", lambda e, s_=s_, a0=a0, o=o: e.tensor_scalar(out=o, in0=a0[:, 0:128], scalar1=s_[:, 0:1], scalar2=None, op0=ALU.mult), r=[ra0, rs_], w=[ro])
                P.op("dve", lambda e, s_=s_, a1=a1, o=o: e.scalar_tensor_tensor(out=o, in0=a1[:, 0:128], scalar=s_[:, 2:3], in1=o, op0=ALU.mult, op1=ALU.add),
                     r=[ra1, rs_, ro], w=[ro])
                P.op("dve", lambda e, o=o: e.tensor_tensor(out=junk, in0=o, in1=o, op=ALU.mult), r=[ro], w=[r_junk])
                P.op("dve", lambda e, s_=s_: e.reduce_sum(out=s_[:, 3:4], in_=junk, axis=AX.X), r=[r_junk], w=[rs_])
                P.op("act", lambda e, s_=s_: e.activation(out=s_[:, 4:5], in_=s_[:, 3:4], func=AF.Ln, bias=C["eps"], scale=1.0 / 128), r=[rs_, cr], w=[rs_])
                P.op("act", lambda e, s_=s_: e.activation(out=s_[:, 5:6], in_=s_[:, 4:5], func=AF.Exp, scale=-0.5), r=[rs_], w=[rs_])
                P.op("dve", lambda e, s_=s_, o=o: e.scalar_tensor_tensor(out=o, in0=o, scalar=s_[:, 5:6], in1=subln, op0=ALU.mult, op1=ALU.mult),
                     r=[ro, rs_, r_small], w=[ro])
                c0 = q0 + jj * 128

                def fin(o=o, ro=ro, og=og, rog=rog, h=h, c0=c0):
                    tb = 7
                    P.mm_group([lambda e, tb=tb, o=o: e.transpose(P.bank(tb, 128), o, C["ident"])], [[ro, cr]], [P.bank_regs[tb]])
                    P.op("dve", lambda e, tb=tb, og=og: e.tensor_copy(out=og, in_=P.bank(tb, 128)), r=[P.bank_regs[tb]], w=[rog])
                    P.dma("sp", OG[h * 128:(h + 1) * 128, c0:c0 + 128], og, r=[rog])
                deferred.append([3 + jj, fin])
    while deferred:
        deferred.pop(0)[1]()
    P.barrier()


def outproj_phase(P, C, w_o, OG, src, dst, LP):
    P.reset_arena()
    ng = LP // G
    wb = P.alloc([128, 8, 1024], BF16)
    r_wb = Reg()
    stage = [P.alloc([128, 1024], F32) for _ in range(2)]
    r_stage = [Reg(), Reg()]
    load_weight_bf16(P, w_o, 8, 1024, None, None, stage, r_stage, wb, r_wb, 1024)
    hbuf = [P.alloc([128, 8, G], F32) for _ in range(2)]
    ogb = [P.alloc([128, 8, G], BF16) for _ in range(2)]
    r_in = [Reg(), Reg()]
    srcv = src.rearrange("(c p) t -> p c t", p=128)
    dstv = dst.rearrange("(c p) t -> p c t", p=128)
    ogv = OG.rearrange("(c p) t -> p c t", p=128)

    def load(g):
        b = g % 2
        P.dma("sp", hbuf[b], srcv[:, :, g * G:(g + 1) * G], w=[r_in[b]])
        P.dma("sp", ogb[b], ogv[:, :, g * G:(g + 1) * G], w=[r_in[b]])

    load(0)
    for g in range(ng):
        b = g % 2
        if g + 1 < ng:
            load(g + 1)
        for oc in range(8):
            bk = oc % 4
            P.mm_group([lambda e, kc=kc, bk=bk, oc=oc, b=b: e.matmul(P.bank(bk, G), lhsT=wb[:, kc, oc * 128:(oc + 1) * 128], rhs=ogb[b][:, kc, :],
                                                                      start=(kc == 0), stop=(kc == 7)) for kc in range(8)],
                       [[r_wb, r_in[b]] for kc in range(8)], [P.bank_regs[bk]])
            P.op("dve", lambda e, bk=bk, oc=oc, b=b: e.tensor_tensor(out=hbuf[b][:, oc, :], in0=P.bank(bk, G), in1=hbuf[b][:, oc, :], op=ALU.add),
                 r=[P.bank_regs[bk]], w=[r_in[b]])
        P.dma("sp", dstv[:, :, g * G:(g + 1) * G], hbuf[b], r=[r_in[b]])
    P.barrier()


def final_phase(P, C, src, normT, out, LP, t0, t1):
    P.reset_arena()
    ng = LP // G
    N = G
    gain = P.alloc([128, 8], F32)
    r_small = Reg()
    hbuf = [P.alloc([128, 8, N], F32) for _ in range(2)]
    r_hbuf = [Reg(), Reg()]
    hn = P.alloc([128, 8, N], BF16)
    r_hn = Reg()
    sqc = [P.alloc([128, N], F32) for _ in range(2)]
    ssum = P.alloc([128, N], F32)
    rs = P.alloc([128, N], F32)
    rstd = P.alloc([128, N], F32)
    r_tmp = [Reg() for _ in range(5)]
    P.dma("sp", gain, normT, w=[r_small])
    srcv = src.rearrange("(c p) t -> p c t", p=128)
    outv = out.rearrange("(c p) t -> p c t", p=128)

    def load(g):
        b = g % 2
        P.dma("sp", hbuf[b], srcv[:, :, g * G:(g + 1) * G], w=[r_hbuf[b]])

    load(0)
    for g in range(ng):
        b = g % 2
        n0 = g * G
        if g + 1 < ng:
            load(g + 1)
        rms_group(P, C, hbuf[b], r_hbuf[b], N, sqc, ssum, rs, rstd, hn, r_hn, r_tmp, bank=7)
        for c in range(8):
            P.op("dve", lambda e, c=c, b=b: e.scalar_tensor_tensor(out=hbuf[b][:, c, :], in0=hbuf[b][:, c, :], scalar=gain[:, c:c + 1], in1=rstd,
                                                                     op0=ALU.mult, op1=ALU.mult), r=[r_small, r_tmp[4]], w=[r_hbuf[b]])
        a = max(n0, t0)
        bb = min(n0 + G, t1)
        if bb > a:
            P.dma("sp", outv[:, :, a - t0:bb - t0], hbuf[b][:, :, a - n0:bb - n0], r=[r_hbuf[b]])
    P.barrier()

LP_FULL = 4224
L_REAL = 4112
N_META = 16


def build_program(LP=LP_FULL, stop=None, debug=False):
    nc = bass.Bass("TRN2", target_bir_lowering=False)

    def din(name, shape):
        return nc.dram_tensor(name, list(shape), F32, kind="ExternalInput").ap()

    def scr(name, shape, dt=F32, out=False):
        return nc.dram_tensor(name, list(shape), dt, kind=("ExternalOutput" if out else "Internal")).ap()

    ht0 = din("ht0", [1024, LP])
    a_w_in = din("a_w_in", [2, 1024, 4112]); a_w_o = din("a_w_o", [2, 1024, 1024])
    a_normT = din("a_normT", [2, 128, 8]); a_convT = din("a_convT", [2, 128, 24, 4])
    a_logB = din("a_logB", [2, 128, 8]); a_dtbB = din("a_dtbB", [2, 128, 8]); a_onormB = din("a_onormB", [2, 128, 1])
    kv_normT = din("kv_normT", [128, 8]); w_kv = din("w_kv", [1024, 2048])
    lk1B = din("lk1B", [128, 64]); lk2B = din("lk2B", [128, 64])
    b_normT = din("b_normT", [2, 128, 8]); b_w_q = din("b_w_q", [2, 1024, 1024]); b_w_o = din("b_w_o", [2, 1024, 1024])
    lq1B = din("lq1B", [2, 128, 64]); lq2B = din("lq2B", [2, 128, 64]); b_sublnB = din("b_sublnB", [2, 128, 128])
    ffn_normT = din("ffn_normT", [4, 128, 8]); ffn_w_up = din("ffn_w_up", [4, 1024, 5632]); ffn_convT = din("ffn_convT", [4, 128, 44, 3])
    ffn_w_down = din("ffn_w_down", [4, 2816, 1024]); final_normT = din("final_normT", [128, 8])
    yT = nc.dram_tensor("yT", [1024, 4096], F32, kind="ExternalOutput").ap()

    HA = scr("HA", [1024, LP], out=debug); HB = scr("HB", [1024, LP], out=debug)
    QT = scr("QT", [1024, LP]); KT = scr("KT", [1024, LP]); VT = scr("VT", [1024, LP]); ZT = scr("ZT", [1024, LP])
    BT = scr("BT", [LP, 8]); GT = scr("GT", [LP, 8])
    KA = scr("KA", [16, 67, LP], BF16); QA = scr("QA", [16, 67, LP], BF16)
    VA = scr("VA", [LP, 8, 129], BF16); OG = scr("OG", [1024, LP], BF16)

    with ExitStack() as es:
        P = Prog(nc, es)
        C = consts(P)
        gdn_consts(P, C)
        attn_consts(P, C)

        def run():
            src = ht0
            for l in range(2):
                gdn_proj_phase(P, C, a_w_in[l], a_normT[l], a_convT[l], a_logB[l], a_dtbB[l], src, QT, KT, VT, ZT, BT, GT, LP)
                gdn_scan_phase(P, C, a_w_o[l], a_onormB[l], QT, KT, VT, ZT, BT, GT, src, HA, LP)
                if stop == f"mix{l}":
                    return
                ffn_phase(P, C, l, ffn_w_up[l], ffn_w_down[l], ffn_normT[l], ffn_convT[l], HA, HB, LP)
                if stop == f"ffn{l}":
                    return
                src = HB
            attn_proj_phase(P, C, HB, kv_normT, w_kv, True, KA, VA, LP)
            for l in (2, 3):
                j = l - 2
                lam_init = 0.8 - 0.6 * math.exp(-0.3 * l)
                attn_proj_phase(P, C, HB, b_normT[j], b_w_q[j], False, QA, None, LP)
                attn_core_phase(P, C, QA, KA, VA, OG, (lq1B[j], lk1B, lq2B[j], lk2B), lam_init, b_sublnB[j], LP)
                outproj_phase(P, C, b_w_o[j], OG, HB, HA, LP)
                if stop == f"mix{l}":
                    return
                ffn_phase(P, C, l, ffn_w_up[l], ffn_w_down[l], ffn_normT[l], ffn_convT[l], HA, HB, LP)
                if stop == f"ffn{l}":
                    return
            final_phase(P, C, HB, final_normT, yT, LP, N_META, N_META + 4096)

        run()
        info = dict(ninst=P.ninst, nsem=P.nsem)
        P.finish()
    return nc, info


def prep_shared(inputs):
    f = lambda a: np.ascontiguousarray(np.asarray(a, dtype=np.float32))

    def pc(v):
        v = np.asarray(v, dtype=np.float32)
        return np.ascontiguousarray(v.reshape(-1, 128).T)

    def bcast(v, n=128):
        v = np.asarray(v, dtype=np.float32)
        return np.ascontiguousarray(np.broadcast_to(v[None, :], (n, v.shape[0])))

    sh = {}
    sh["a_w_in"] = f(inputs["a_w_in"]); sh["a_w_o"] = f(inputs["a_w_o"])
    sh["a_normT"] = np.stack([pc(inputs["a_norm"][i]) for i in range(2)])
    sh["a_convT"] = np.stack([np.ascontiguousarray(np.asarray(inputs["a_conv"][i], np.float32).T.reshape(24, 128, 4).transpose(1, 0, 2)) for i in range(2)])
    sh["a_logB"] = np.stack([bcast(inputs["a_log"][i]) for i in range(2)])
    sh["a_dtbB"] = np.stack([bcast(inputs["a_dt_bias"][i]) for i in range(2)])
    sh["a_onormB"] = np.stack([f(inputs["a_onorm"][i]).reshape(128, 1) for i in range(2)])
    sh["kv_normT"] = pc(inputs["kv_norm"]); sh["w_kv"] = f(inputs["w_kv"])
    sh["lk1B"] = bcast(inputs["lambda_k1"]); sh["lk2B"] = bcast(inputs["lambda_k2"])
    sh["b_normT"] = np.stack([pc(inputs["b_norm"][i]) for i in range(2)])
    sh["b_w_q"] = f(inputs["b_w_q"]); sh["b_w_o"] = f(inputs["b_w_o"])
    sh["lq1B"] = np.stack([bcast(inputs["b_lambda_q1"][i]) for i in range(2)])
    sh["lq2B"] = np.stack([bcast(inputs["b_lambda_q2"][i]) for i in range(2)])
    sh["b_sublnB"] = np.stack([bcast(inputs["b_subln"][i]) for i in range(2)])
    sh["ffn_normT"] = np.stack([pc(inputs["ffn_norm"][i]) for i in range(4)])
    sh["ffn_w_up"] = f(inputs["ffn_w_up"])
    sh["ffn_convT"] = np.stack([np.ascontiguousarray(np.asarray(inputs["ffn_conv"][i], np.float32).T.reshape(44, 128, 3).transpose(1, 0, 2)) for i in range(4)])
    sh["ffn_w_down"] = f(inputs["ffn_w_down"]); sh["final_normT"] = pc(inputs["final_norm"])
    return sh


def make_ht0(x_b, meta, LP=LP_FULL):
    ht = np.zeros((1024, LP), np.float32)
    ht[:, 0:N_META] = np.asarray(meta, np.float32).T
    n = min(LP - N_META, x_b.shape[0])
    ht[:, N_META:N_META + n] = np.asarray(x_b[:n], np.float32).T
    return ht


_CACHE = {}


def kernel(**inputs):
    x = np.asarray(inputs["x"], dtype=np.float32)
    B = x.shape[0]
    if "nc" not in _CACHE:
        _CACHE["nc"] = build_program()[0]
    nc = _CACHE["nc"]
    sh = prep_shared(inputs)
    in_maps = []
    for b in range(B):
        m = dict(sh)
        m["ht0"] = make_ht0(x[b], inputs["meta_tokens"])
        in_maps.append(m)
    res = run_bass_kernel_spmd(nc, in_maps, core_ids=list(range(B)))
    out = np.empty((B, 4096, 1024), np.float32)
    for b in range(B):
        out[b] = res.results[b]["yT"].T
    return out
```
